# Optimizing a Trainium2 kernel written in Bass

```python
import jax
import jax.numpy as jnp
from jax import lax
import numpy as np


D_MODEL = 1024
BATCH = 2
SEQ = 8192
DEPTH = 1

M_HEADS = 4
M_HEAD_DIM = 128
M_WIDTH = M_HEADS * M_HEAD_DIM
M_CONV = 4
M_CHUNK = 64

A_HEADS = 8
A_HEAD_DIM = 64
A_WIDTH = A_HEADS * A_HEAD_DIM
MOBA_BLOCK = 256
MOBA_TOPK = 3
Q_BLOCK = 64
ROPE_THETA = 10000.0

D_FF = ((8 * D_MODEL + 3 * 256 - 1) // (3 * 256)) * 256

N_MOD = 6
IN_SPLITS = (M_WIDTH, M_WIDTH, M_WIDTH, M_WIDTH, M_HEADS, M_HEADS, A_WIDTH, A_WIDTH, A_WIDTH, D_MODEL, D_MODEL)
D_IN = 4 * M_WIDTH + 2 * M_HEADS + 3 * A_WIDTH + 2 * D_MODEL
F_GATE_OFFSET = 4 * M_WIDTH + M_HEADS
RMS_EPS = 1e-6

kernel_name = 'hybrid_mlstm_moba_gated_block'


def rms_norm(x, g):
    xf = x.astype(jnp.float32)
    y = xf * lax.rsqrt(jnp.mean(xf * xf, axis=-1, keepdims=True) + RMS_EPS)
    return (y * g.astype(jnp.float32)).astype(x.dtype)


def modulate(h, shift, scale):
    return h * (1 + scale[:, None, :]) + shift[:, None, :]


def causal_dwconv(x, w, b):
    K = w.shape[0]
    S = x.shape[1]
    xp = jnp.pad(x, ((0, 0), (K - 1, 0), (0, 0)))
    y = b
    for j in range(K):
        y = y + xp[:, j:j + S, :] * w[j]
    return y


def split_heads(t, n_heads, head_dim):
    B, S, _ = t.shape
    return t.reshape(B, S, n_heads, head_dim).transpose(0, 2, 1, 3)


def merge_heads(t):
    B, H, S, Dh = t.shape
    return t.transpose(0, 2, 1, 3).reshape(B, S, H * Dh)


def rope(x, positions):
    half = x.shape[-1] // 2
    inv_freq = ROPE_THETA ** (-jnp.arange(half, dtype=jnp.float32) / half)
    ang = positions.astype(jnp.float32)[:, None, :, None] * inv_freq
    cos, sin = jnp.cos(ang), jnp.sin(ang)
    xf = x.astype(jnp.float32)
    x1, x2 = xf[..., :half], xf[..., half:]
    out = jnp.concatenate([x1 * cos - x2 * sin, x2 * cos + x1 * sin], axis=-1)
    return out.astype(x.dtype)


def mlstm_chunkwise(q, k, v, i_pre, f_pre):
    B, H, S, Dh = q.shape
    L = M_CHUNK
    NC = S // L
    f32 = jnp.float32
    q = q.astype(f32).reshape(B, H, NC, L, Dh)
    k = (k.astype(f32) * (Dh ** -0.5)).reshape(B, H, NC, L, Dh)
    v = v.astype(f32).reshape(B, H, NC, L, Dh)
    log_i = i_pre.astype(f32).reshape(B, H, NC, L)
    log_f = jax.nn.log_sigmoid(f_pre.astype(f32)).reshape(B, H, NC, L)
    b = jnp.cumsum(log_f, axis=-1)
    b_last = b[..., -1]
    a = b_last[..., None] - b + log_i

    def step(carry, inp):
        C, n, m = carry
        a_c, bl_c, k_c, v_c = inp
        m_new = jnp.maximum(bl_c + m, jnp.max(a_c, axis=-1))
        decay = jnp.exp(bl_c + m - m_new)
        w = jnp.exp(a_c - m_new[..., None])
        C_new = decay[..., None, None] * C + jnp.einsum('bhl,bhld,bhle->bhde', w, v_c, k_c)
        n_new = decay[..., None] * n + jnp.einsum('bhl,bhle->bhe', w, k_c)
        return (C_new, n_new, m_new), (C, n, m)

    init = (jnp.zeros((B, H, Dh, Dh), f32), jnp.zeros((B, H, Dh), f32), jnp.zeros((B, H), f32))
    xs = (jnp.moveaxis(a, 2, 0), jnp.moveaxis(b_last, 2, 0), jnp.moveaxis(k, 2, 0), jnp.moveaxis(v, 2, 0))
    _, (C_prev, n_prev, m_prev) = lax.scan(step, init, xs)
    C_prev = jnp.moveaxis(C_prev, 0, 2)
    n_prev = jnp.moveaxis(n_prev, 0, 2)
    m_prev = jnp.moveaxis(m_prev, 0, 2)

    causal = jnp.tril(jnp.ones((L, L), dtype=bool))
    D = b[..., :, None] - b[..., None, :] + log_i[..., None, :]
    D = jnp.where(causal, D, -jnp.inf)
    inter = b + m_prev[..., None]
    m_t = jnp.maximum(inter, jnp.max(D, axis=-1))
    W = jnp.exp(D - m_t[..., None])
    SW = jnp.einsum('bhctd,bhcsd->bhcts', q, k) * W
    inter_w = jnp.exp(inter - m_t)
    num = inter_w[..., None] * jnp.einsum('bhcde,bhcte->bhctd', C_prev, q) + jnp.einsum('bhcts,bhcsd->bhctd', SW, v)
    den = inter_w * jnp.einsum('bhce,bhcte->bhct', n_prev, q) + jnp.sum(SW, axis=-1)
    h = num / jnp.maximum(jnp.abs(den), jnp.exp(-m_t))[..., None]
    return h.reshape(B, H, S, Dh)


def moba_attention(q, k, v):
    B, H, S, Dh = q.shape
    NB = -(-S // MOBA_BLOCK)
    K_SEL = min(MOBA_TOPK, NB)
    pad = NB * MOBA_BLOCK - S
    NQ = S // Q_BLOCK
    scale = Dh ** -0.5
    kp = jnp.pad(k, ((0, 0), (0, 0), (0, pad), (0, 0)))
    vp = jnp.pad(v, ((0, 0), (0, 0), (0, pad), (0, 0)))
    kb = kp.reshape(B, H, NB, MOBA_BLOCK, Dh)
    vb = vp.reshape(B, H, NB, MOBA_BLOCK, Dh)
    kmean = jnp.mean(kb.astype(jnp.float32), axis=3).astype(k.dtype)
    bi = jnp.arange(B)[:, None, None, None]
    hi = jnp.arange(H)[None, :, None, None]
    blk_ids = jnp.arange(NB)

    def one_block(qi):
        q0 = qi * Q_BLOCK
        qc = lax.dynamic_slice_in_dim(q, q0, Q_BLOCK, axis=2)
        cur = q0 // MOBA_BLOCK
        gate = jnp.einsum('bhqd,bhnd->bhqn', qc, kmean).astype(jnp.float32)
        gate = jnp.where(blk_ids < cur, gate, -jnp.inf)
        _, gidx = lax.top_k(gate, K_SEL)
        valid = gidx < cur
        kg = kb[bi, hi, gidx]
        vg = vb[bi, hi, gidx]
        s_sel = jnp.einsum('bhqd,bhqjkd->bhqjk', qc, kg).astype(jnp.float32) * scale
        s_sel = jnp.where(valid[..., None], s_sel, -jnp.inf).reshape(B, H, Q_BLOCK, K_SEL * MOBA_BLOCK)
        k_own = lax.dynamic_slice_in_dim(kp, cur * MOBA_BLOCK, MOBA_BLOCK, axis=2)
        v_own = lax.dynamic_slice_in_dim(vp, cur * MOBA_BLOCK, MOBA_BLOCK, axis=2)
        s_own = jnp.einsum('bhqd,bhkd->bhqk', qc, k_own).astype(jnp.float32) * scale
        qpos = q0 + jnp.arange(Q_BLOCK)
        kpos = cur * MOBA_BLOCK + jnp.arange(MOBA_BLOCK)
        s_own = jnp.where(kpos[None, :] <= qpos[:, None], s_own, -jnp.inf)
        p = jax.nn.softmax(jnp.concatenate([s_sel, s_own], axis=-1), axis=-1).astype(v.dtype)
        p_sel = p[..., :K_SEL * MOBA_BLOCK].reshape(B, H, Q_BLOCK, K_SEL, MOBA_BLOCK)
        p_own = p[..., K_SEL * MOBA_BLOCK:]
        return jnp.einsum('bhqjk,bhqjkd->bhqd', p_sel, vg) + jnp.einsum('bhqk,bhkd->bhqd', p_own, v_own)

    outs = lax.map(one_block, jnp.arange(NQ))
    return outs.transpose(1, 2, 0, 3, 4).reshape(B, H, S, Dh)


def setup_inputs(seed: int = 0) -> dict:
    key = jax.random.key(seed)
    ks = jax.random.split(key, 20)
    f32 = jnp.float32

    def nrm(k, shape, scale):
        return jax.random.normal(k, shape, f32) * scale

    x = nrm(ks[0], (BATCH, SEQ, D_MODEL), 1.0)
    c = nrm(ks[1], (BATCH, D_MODEL), 1.0)
    positions = jnp.broadcast_to(jnp.arange(SEQ, dtype=jnp.int32), (BATCH, SEQ))
    ada_w = nrm(ks[2], (DEPTH, D_MODEL, N_MOD * D_MODEL), 0.5 * D_MODEL ** -0.5)
    ada_b = nrm(ks[3], (DEPTH, N_MOD * D_MODEL), 0.02)
    norm1_g = 1.0 + nrm(ks[4], (DEPTH, D_MODEL), 0.02)
    norm2_g = 1.0 + nrm(ks[5], (DEPTH, D_MODEL), 0.02)
    normf_g = 1.0 + nrm(ks[6], (D_MODEL,), 0.02)
    w_in = nrm(ks[7], (DEPTH, D_MODEL, D_IN), D_MODEL ** -0.5)
    b_in = nrm(ks[8], (DEPTH, D_IN), 0.02)
    b_in = b_in.at[:, F_GATE_OFFSET:F_GATE_OFFSET + M_HEADS].add(jnp.linspace(3.0, 6.0, M_HEADS, dtype=f32))
    conv_w = nrm(ks[9], (DEPTH, M_CONV, 2 * M_WIDTH), M_CONV ** -0.5)
    conv_b = nrm(ks[10], (DEPTH, 2 * M_WIDTH), 0.02)
    m_norm_g = 1.0 + nrm(ks[11], (DEPTH, M_WIDTH), 0.02)
    p_mlstm = nrm(ks[12], (DEPTH, M_WIDTH, D_MODEL), M_WIDTH ** -0.5)
    p_moba = nrm(ks[13], (DEPTH, A_WIDTH, D_MODEL), A_WIDTH ** -0.5)
    w_out = nrm(ks[14], (DEPTH, D_MODEL, D_MODEL), D_MODEL ** -0.5)
    w_gate = nrm(ks[15], (DEPTH, D_MODEL, D_FF), D_MODEL ** -0.5)
    w_up = nrm(ks[16], (DEPTH, D_MODEL, D_FF), D_MODEL ** -0.5)
    w_down = nrm(ks[17], (DEPTH, D_FF, D_MODEL), D_FF ** -0.5)
    return {'x': x, 'c': c, 'positions': positions, 'ada_w': ada_w, 'ada_b': ada_b,
            'norm1_g': norm1_g, 'norm2_g': norm2_g, 'normf_g': normf_g, 'w_in': w_in, 'b_in': b_in,
            'conv_w': conv_w, 'conv_b': conv_b, 'm_norm_g': m_norm_g, 'p_mlstm': p_mlstm,
            'p_moba': p_moba, 'w_out': w_out, 'w_gate': w_gate, 'w_up': w_up, 'w_down': w_down}


def reference(x, c, positions, ada_w, ada_b, norm1_g, norm2_g, normf_g, w_in, b_in,
              conv_w, conv_b, m_norm_g, p_mlstm, p_moba, w_out, w_gate, w_up, w_down):
    B, S, D = x.shape
    split_points = np.cumsum(IN_SPLITS)[:-1].tolist()
    c_act = jax.nn.silu(c)
    for l in range(DEPTH):
        mod = jnp.dot(c_act, ada_w[l]) + ada_b[l]
        sh1, sc1, g1, sh2, sc2, g2 = jnp.split(mod, N_MOD, axis=-1)

        h = modulate(rms_norm(x, norm1_g[l]), sh1, sc1)
        z = jnp.einsum('bsd,de->bse', h, w_in[l]) + b_in[l]
        mq, mk, mv, mo, mi, mf, aq, ak, av, ga, gb = jnp.split(z, split_points, axis=-1)

        qk = jax.nn.silu(causal_dwconv(jnp.concatenate([mq, mk], axis=-1), conv_w[l], conv_b[l]))
        mq, mk = jnp.split(qk, 2, axis=-1)
        hm = mlstm_chunkwise(split_heads(mq, M_HEADS, M_HEAD_DIM), split_heads(mk, M_HEADS, M_HEAD_DIM),
                             split_heads(mv, M_HEADS, M_HEAD_DIM), mi.transpose(0, 2, 1), mf.transpose(0, 2, 1))
        hm = rms_norm(hm.transpose(0, 2, 1, 3), m_norm_g[l].reshape(M_HEADS, M_HEAD_DIM))
        ym = (hm.reshape(B, S, M_WIDTH) * jax.nn.sigmoid(mo)).astype(x.dtype)

        qa = rope(split_heads(aq, A_HEADS, A_HEAD_DIM), positions)
        ka = rope(split_heads(ak, A_HEADS, A_HEAD_DIM), positions)
        va = split_heads(av, A_HEADS, A_HEAD_DIM)
        ya = merge_heads(moba_attention(qa, ka, va))

        merged = (jax.nn.sigmoid(ga) * jnp.einsum('bsm,md->bsd', ym, p_mlstm[l])
                  + jax.nn.sigmoid(gb) * jnp.einsum('bsa,ad->bsd', ya, p_moba[l]))
        x = x + g1[:, None, :] * jnp.einsum('bsd,de->bse', merged, w_out[l])

        h = modulate(rms_norm(x, norm2_g[l]), sh2, sc2)
        f = jax.nn.silu(jnp.einsum('bsd,df->bsf', h, w_gate[l])) * jnp.einsum('bsd,df->bsf', h, w_up[l])
        x = x + g2[:, None, :] * jnp.einsum('bsf,fd->bsd', f, w_down[l])
    return rms_norm(x, normf_g)
```

```python
import math
from contextlib import ExitStack
import numpy as np
import concourse.bass as bass
import concourse.mybir as mybir
from concourse.bass_utils import run_bass_kernel_spmd

F32 = mybir.dt.float32
BF16 = mybir.dt.bfloat16
I32 = mybir.dt.int32
AF = mybir.ActivationFunctionType
ALU = mybir.AluOpType
AX = mybir.AxisListType

D = 1024
KT = 8
DFF = 2816
NFC = 22
DIN = 5640
OFF = dict(mk=0, mv=512, ak=1024, av=1536, mif=2048, mq=2056, mo=2568, aq=3080, ga=3592, gb=4616)
NF = 2056
EPS = 1e-6
NEG = -30000.0


import os as _os
LAT_PE = float(_os.environ.get('KLATPE', '250'))
LAT_X = float(_os.environ.get('KLATX', '300'))


class _Proxy:
    def __init__(self):
        self.call = None

    def __getattr__(self, name):
        def f(*a, **k):
            self.call = (name, a, k)
            return self
        return f


def _free_size(ap):
    try:
        sh = list(ap.shape)
        n = 1
        for v in sh[1:]:
            n *= int(v)
        return n
    except Exception:
        return 256


class Sched:
    def __init__(self, nc, n_dma_sems=8):
        self.nc = nc
        self.eng = {'pe': nc.tensor, 'act': nc.scalar, 'dve': nc.vector, 'pool': nc.gpsimd, 'sp': nc.sync}
        self.csem = {e: nc.alloc_semaphore("c_" + e) for e in ['pe', 'act', 'dve', 'pool']}
        self.ccnt = {e: 0 for e in self.csem}
        self.P = n_dma_sems
        self.dsem = {q: [nc.alloc_semaphore("d_%s%d" % (q, i)) for i in range(n_dma_sems)] for q in ['sp', 'pool']}
        self.dcnt = {q: 0 for q in self.dsem}
        self.known = {e: {} for e in self.eng}
        self.sems = {}
        for e in self.csem:
            self.sems["c_" + e] = self.csem[e]
        for q in self.dsem:
            for i in range(n_dma_sems):
                self.sems["d_%s%d" % (q, i)] = self.dsem[q][i]
        self.lastw = {}
        self.readers = {}
        self.nwait = 0
        self.nop = 0
        self.dead = False
        self.rec = []
        self.alias = {}
        import os
        self.reorder = os.environ.get('KREORDER', '1') == '1'

    def _res(self, keys):
        return [self.alias.get(k, k) if isinstance(k, str) else k for k in keys]

    def op(self, e, fn, reads=(), writes=(), n=None):
        if self.dead:
            return None
        px = _Proxy()
        fn(px)
        name, a, k = px.call
        if n is None:
            if name == 'matmul':
                n = _free_size(k.get('rhs', a[2] if len(a) > 2 else None))
            elif name == 'transpose':
                n = 128
            else:
                o = k.get('out', a[0] if a else None)
                n = _free_size(o)
        if e == 'pe':
            fp32 = False
            try:
                fp32 = (name == 'matmul' and k['rhs'].dtype == F32)
            except Exception:
                pass
            cost = (max(64, n) / 2.2 + 35) * (4 if fp32 else 1)
            lat = LAT_PE
        elif e == 'act':
            cost = 230 + n / 1.3
            lat = LAT_X
        elif e == 'dve':
            cost = 200 + n / 0.9
            lat = LAT_X
        else:
            cost = 350 + n / 0.55
            lat = LAT_X
        self.rec.append(dict(kind='op', e=e, call=(name, a, k), reads=self._res(reads), writes=self._res(writes), cost=cost, lat=lat))
        return None

    def dma(self, q, out, in_, reads=(), writes=()):
        if self.dead:
            return None
        self.rec.append(dict(kind='dma', e=q, call=(out, in_), reads=self._res(reads), writes=self._res(writes), cost=(120 if q == 'sp' else 700), lat=3000))
        return None

    def flush(self):
        rec = self.rec
        self.rec = []
        N = len(rec)
        if N == 0:
            return
        order = list(range(N))
        if self.reorder:
            lastw, readers = {}, {}
            preds = [set() for _ in range(N)]
            for i, r in enumerate(rec):
                for k in r['reads']:
                    if k in lastw:
                        preds[i].add(lastw[k])
                for k in r['writes']:
                    if k in lastw:
                        preds[i].add(lastw[k])
                    for j in readers.get(k, ()):
                        preds[i].add(j)
                preds[i].discard(i)
                for k in r['reads']:
                    if k not in r['writes']:
                        readers.setdefault(k, []).append(i)
                for k in r['writes']:
                    lastw[k] = i
                    readers[k] = []
            succ = [[] for _ in range(N)]
            indeg = [0] * N
            for i in range(N):
                indeg[i] = len(preds[i])
                for p in preds[i]:
                    succ[p].append(i)
            import heapq
            efree = {}
            fin = [0.0] * N
            rdy_t = [0.0] * N
            ready = {}
            for i in range(N):
                if indeg[i] == 0:
                    heapq.heappush(ready.setdefault(rec[i]['e'], []), (0.0, i))
            order = []
            WIN = 6
            while len(order) < N:
                best = None
                for e, hp in ready.items():
                    if not hp:
                        continue
                    cands = heapq.nsmallest(WIN, hp)
                    ef = efree.get(e, 0.0)
                    for (rt_, i) in cands:
                        st_ = max(ef, rt_)
                        key = (st_, i)
                        if best is None or key < best[0]:
                            best = (key, e, (rt_, i))
                (st_, i), e, item = best
                ready[e].remove(item)
                heapq.heapify(ready[e])
                r = rec[i]
                efree[e] = st_ + r['cost']
                fin[i] = st_ + r['cost'] + r['lat']
                order.append(i)
                for s_ in succ[i]:
                    indeg[s_] -= 1
                    same = (rec[s_]['e'] == e and e == 'pe')
                    t_ = (st_ + r['cost']) if same else fin[i]
                    if t_ > rdy_t[s_]:
                        rdy_t[s_] = t_
                    if indeg[s_] == 0:
                        heapq.heappush(ready.setdefault(rec[s_]['e'], []), (rdy_t[s_], s_))
        for i in order:
            r = rec[i]
            if r['kind'] == 'op':
                self._emit_op(r['e'], r['call'], r['reads'], r['writes'])
            else:
                self._emit_dma(r['e'], r['call'][0], r['call'][1], r['reads'], r['writes'])

    def _wait(self, e, toks):
        need = {}
        for (sname, val, prod) in toks:
            if e == 'pe' and prod == 'pe':
                continue
            if self.known[e].get(sname, 0) >= val:
                continue
            if need.get(sname, 0) < val:
                need[sname] = val
        for sname, val in need.items():
            self.eng[e].wait_ge(self.sems[sname], val)
            self.known[e][sname] = val
            self.nwait += 1

    def _deps(self, reads, writes):
        toks = []
        for k in reads:
            t = self.lastw.get(k)
            if t is not None:
                toks.append(t)
        for k in writes:
            t = self.lastw.get(k)
            if t is not None:
                toks.append(t)
            toks += self.readers.get(k, [])
        return toks

    def _commit(self, tok, reads, writes):
        for k in reads:
            if k not in writes:
                self.readers.setdefault(k, []).append(tok)
        for k in writes:
            self.lastw[k] = tok
            self.readers[k] = []

    def _emit_op(self, e, call, reads, writes):
        self._wait(e, self._deps(reads, writes))
        name, a, k = call
        inst = getattr(self.eng[e], name)(*a, **k)
        self.ccnt[e] += 1
        inst.then_inc(self.csem[e], 1)
        tok = ("c_" + e, self.ccnt[e], e)
        self._commit(tok, reads, writes)
        self.nop += 1

    def _emit_dma(self, q, out, in_, reads, writes):
        n = self.dcnt[q]
        s = self.dsem[q][n % self.P]
        sname = "d_%s%d" % (q, n % self.P)
        toks = self._deps(reads, writes)
        if n >= self.P:
            toks.append((sname, 16 * (n // self.P), 'dma'))
        self._wait(q, toks)
        self.eng[q].dma_start(out=out, in_=in_).then_inc(s, 16)
        self.dcnt[q] += 1
        tok = (sname, 16 * (n // self.P + 1), 'dma')
        self._commit(tok, reads, writes)
        self.nop += 1

    def all_tokens(self):
        toks = [("c_" + e, self.ccnt[e], e) for e in self.csem if self.ccnt[e] > 0]
        for q in self.dsem:
            for idx in range(self.P):
                if self.dcnt[q] > idx:
                    uses = (self.dcnt[q] - idx + self.P - 1) // self.P
                    toks.append(("d_%s%d" % (q, idx), 16 * uses, 'dma'))
        return toks

    def barrier(self, engines=None):
        if self.dead:
            return
        self.flush()
        toks = self.all_tokens()
        for e in (engines or list(self.eng.keys())):
            self._wait(e, toks)
        self.lastw = {}
        self.readers = {}


class TL:
    def __init__(self, t, key):
        self.t = t
        self.key = key

    def __getitem__(self, i):
        return self.t[i]


class RT:
    def __init__(self, S, name, tiles):
        self.S = S
        self.name = name
        self.tiles = tiles
        self.i = 0
        for k, t in enumerate(tiles):
            t.key = "%s#%d" % (name, k)
        S.alias[name] = tiles[0].key

    def rotate(self):
        self.i += 1
        self.S.alias[self.name] = self.tiles[self.i % len(self.tiles)].key

    @property
    def key(self):
        return self.tiles[self.i % len(self.tiles)].key

    def __getitem__(self, i):
        return self.tiles[self.i % len(self.tiles)].t[i]


class Rot:
    def __init__(self, tiles):
        self.tiles = tiles
        self.i = 0

    def next(self):
        t = self.tiles[self.i % len(self.tiles)]
        self.i += 1
        return t


def build_program(S_LEN, dbg=None, stop=None):
    NBLK = S_LEN // 256
    NOWN = NBLK // 4
    NT = S_LEN // 128
    NTO = NOWN * 2
    nc = bass.Bass("TRN2", target_bir_lowering=False)
    S = Sched(nc)
    dbg = dbg or []
    dbg_outs = {}

    def din(name, shape, dt=F32):
        return nc.dram_tensor(name, shape, dt, kind="ExternalInput").ap()

    xTf = din("xTf", [D, S_LEN])
    xTo = din("xTo", [D, NOWN * 260])
    cT_d = din("cT", [128, 8])
    posf_d = din("posf", [128, NT], I32)
    poso_d = din("poso", [128, NTO], I32)
    ada_w = din("ada_w", [D, 6 * D])
    ada_b_c = din("ada_b_c", [128, 48])
    n1g_d = din("n1g", [128, 8])
    n2g_d = din("n2g", [128, 8])
    nfg_d = din("nfg", [128, 8])
    w_in_p = din("w_in_p", [D, DIN])
    b_in_row = din("b_in_row", [1, DIN])
    b_in_col = din("b_in_col", [128, 24])
    cw_d = din("cw", [128, 32])
    cb_d = din("cb", [128, 8])
    mng_d = din("mng_bc", [128, 512])
    p_mlstm = din("p_mlstm", [512, D])
    p_moba = din("p_moba", [512, D])
    w_out = din("w_out", [D, D])
    w_gate = din("w_gate", [D, DFF])
    w_up = din("w_up", [D, DFF])
    w_down = din("w_down", [DFF, D])
    ident_d = din("ident", [128, 128])
    tri_d = din("tri", [128, 128])
    ones_d = din("ones", [128, 128])
    invf_d = din("invf", [128, 32])
    pastb_d = din("pastbias", [128, NOWN * 32])
    hv_d = din("hv", [128, NOWN])
    capsel_d = din("capsel", [128, 4])
    etab_d = din("etab", [128, S_LEN])
    outT = nc.dram_tensor("outT", [D, NOWN * 256], F32, kind="ExternalOutput").ap()

    KTs = nc.dram_tensor("KTs", [512, S_LEN], BF16, kind="Internal").ap()
    Vs = nc.dram_tensor("Vs", [8, 128, NT * 65], BF16, kind="Internal").ap()
    QTs = nc.dram_tensor("QTs", [8, 128, NOWN * 256], BF16, kind="Internal").ap()
    KTos = nc.dram_tensor("KTos", [8, 128, NOWN * 256], BF16, kind="Internal").ap()
    XSs = nc.dram_tensor("XSs", [NOWN, 128, 8 * 256], BF16, kind="Internal").ap()
    CCs = nc.dram_tensor("CCs", [NOWN, 128, 4 * 129], F32, kind="Internal").ap()
    Vos = nc.dram_tensor("Vos", [8, 128, NTO * 65], BF16, kind="Internal").ap()
    YMs = nc.dram_tensor("YMs", [NOWN, 128, 4 * 256], BF16, kind="Internal").ap()

    used_names = {}

    def uniq(name):
        k = used_names.get(name, 0)
        used_names[name] = k + 1
        return name if k == 0 else "%s_r%d" % (name, k)

    def sb(st, name, shape, dt=F32):
        return TL(st.enter_context(nc.sbuf_tensor(uniq(name), shape, dt)), name)

    def rt(st, name, shape, dt=F32, k=2):
        import os
        if os.environ.get('KRT', '1') == '0':
            k = 1
        return RT(S, name, [sb(st, "%s_b%d" % (name, i), shape, dt) for i in range(k)])

    def pst(st, name, shape, dt=F32):
        return TL(st.enter_context(nc.psum_tensor(uniq(name), shape, dt)), name)

    def dump(name, ap, shape, keys, is_bf16=False):
        if name not in dbg:
            return
        o = nc.dram_tensor("dbg_" + name, shape, F32, kind="ExternalOutput").ap()
        dbg_outs[name] = shape
        if S.dead:
            return
        with nc.sbuf_tensor("dbgt_" + name, shape, F32) as tmp:
            S.op('dve', lambda e: e.tensor_copy(out=tmp[:], in_=ap), reads=keys, writes=['dbgt_' + name])
            S.dma('sp', o, tmp[:], reads=['dbgt_' + name], writes=['dbgo_' + name])
            S.barrier()

    def chk_stop(tag):
        if stop == tag:
            S.barrier()
            S.dead = True

    G = ExitStack()
    psm = Rot([pst(G, "psm%d" % i, [128, 512]) for i in range(4)])
    pstr = Rot([pst(G, "pstr%d" % i, [128, 1024], BF16) for i in range(2)])
    pss = [pst(G, "pss%d" % i, [128, 512]) for i in range(2)]

    ident_f = sb(G, "ident_f", [128, 128]); tri_f = sb(G, "tri_f", [128, 128]); ones_f = sb(G, "ones_f", [128, 128])
    ident_b = sb(G, "ident_b", [128, 128], BF16); tri_b = sb(G, "tri_b", [128, 128], BF16); ones_b = sb(G, "ones_b", [128, 128], BF16)
    for (d_, f_, b_) in [(ident_d, ident_f, ident_b), (tri_d, tri_f, tri_b), (ones_d, ones_f, ones_b)]:
        S.dma('sp', f_[:], d_, writes=[f_.key])
        S.op('dve', lambda e, f_=f_, b_=b_: e.tensor_copy(out=b_[:], in_=f_[:]), reads=[f_.key], writes=[b_.key])
    eps_t = sb(G, "eps_t", [128, 1]); one_t = sb(G, "one_t", [128, 1])
    S.op('pool', lambda e: e.memset(eps_t[:], EPS), writes=['eps_t'])
    S.op('pool', lambda e: e.memset(one_t[:], 1.0), writes=['one_t'])

    def load_const(name, src, shape, dt=F32):
        t = sb(G, name, shape, dt)
        S.dma('sp', t[:], src, writes=[name])
        return t

    c_t = load_const("c_t", cT_d, [128, 8])
    adab = load_const("adab", ada_b_c, [128, 48])
    n1g = load_const("n1g_t", n1g_d, [128, 8]); n2g = load_const("n2g_t", n2g_d, [128, 8]); nfg = load_const("nfg_t", nfg_d, [128, 8])
    bincol = load_const("bincol", b_in_col, [128, 24])
    biasc = sb(G, "biasc", [128, 24])
    kmaxbc = sb(G, "kmaxbc", [128, 8])
    modc = sb(G, "modc", [128, 48])
    gmod1 = sb(G, "gmod1", [128, 8]); gmod2 = sb(G, "gmod2", [128, 8])
    sc = sb(G, "sc", [128, 8])
    G1 = ExitStack()
    G_save = G
    G = G1
    cw = load_const("cw_t", cw_d, [128, 32]); cb = load_const("cb_t", cb_d, [128, 8])
    mng = load_const("mng_t", mng_d, [128, 512])
    invf = load_const("invf_t", invf_d, [128, 32])
    pastb = load_const("pastb_t", pastb_d, [128, NOWN * 32])
    hv = load_const("hv_t", hv_d, [128, NOWN])
    capsel = load_const("capsel_t", capsel_d, [128, 4])
    sh1bc = sb(G1, "sh1bc", [128, 8, 128])
    bias_bc = sb(G1, "bias_bc", [128, 2568])
    kmT = sb(G1, "kmT", [128, 4, 2, 32], BF16)
    dgw = sb(G1, "dgw", [128, 8, 4, 128], BF16)
    ncb = sb(G1, "ncb", [128, 8]); lnq_t = sb(G1, "lnq_t", [128, 1]); zero_t = sb(G1, "zero_t", [128, 1])
    G = G_save

    for g_ in range(8):
        for j_ in range(4):
            S.op('act', lambda e, g_=g_, j_=j_: e.activation(out=dgw[:, g_, j_, :], in_=ident_f[:], func=AF.Copy, scale=cw[:, g_ * 4 + j_:g_ * 4 + j_ + 1]), reads=['ident_f', 'cw_t'], writes=['dgw'])
    S.op('dve', lambda e: e.tensor_scalar(out=ncb[:], in0=cb[:], scalar1=-1.0, scalar2=None, op0=ALU.mult), reads=['cb_t'], writes=['ncb'])
    S.op('pool', lambda e: e.memset(lnq_t[:], math.log(128.0 ** -0.5)), writes=['lnq_t'])
    S.op('pool', lambda e: e.memset(zero_t[:], 0.0), writes=['zero_t'])
    S.op('act', lambda e: e.activation(out=sc[:], in_=c_t[:], func=AF.Silu), reads=['c_t'], writes=['sc'])
    adaw_v = ada_w.rearrange("(k p) c -> p k c", p=128)

    def mod_groups(awr, g0, g1, key):
        for g in range(g0, g1):
            aw = awr.next()
            ps = psm.next()
            S.dma('sp', aw[:], adaw_v[:, :, g * 128:(g + 1) * 128], writes=[aw.key])
            for kt in range(KT):
                S.op('pe', lambda e, aw=aw, kt=kt, ps=ps: e.matmul(ps[:, 0:1], lhsT=aw[:, kt, :], rhs=sc[:, kt:kt + 1], start=(kt == 0), stop=(kt == KT - 1)),
                     reads=[aw.key, 'sc'], writes=[ps.key])
            S.op('dve', lambda e, g=g, ps=ps: e.tensor_tensor(out=modc[:, g:g + 1], in0=ps[:, 0:1], in1=adab[:, g:g + 1], op=ALU.add), reads=[ps.key, 'adab'], writes=[key])

    with ExitStack() as st:
        awr = Rot([sb(st, "aw%d" % i, [128, 8, 128]) for i in range(3)])
        mod_groups(awr, 0, 16, 'modc')
        S.op('dve', lambda e: e.scalar_tensor_tensor(out=gmod1[:], in0=modc[:, 8:16], scalar=1.0, in1=n1g[:], op0=ALU.add, op1=ALU.mult), reads=['modc', 'n1g_t'], writes=['gmod1'])
        S.barrier()
    C_SH1, C_G1, C_SH2, C_G2 = 0, 16, 24, 40
    dump("modc", modc[:], [128, 48], ['modc'])
    chk_stop('S')

    for kt in range(KT):
        S.op('act', lambda e, kt=kt: e.activation(out=sh1bc[:, kt, :], in_=ones_f[:], func=AF.Copy, scale=modc[:, C_SH1 + kt:C_SH1 + kt + 1]), reads=['ones_f', 'modc'], writes=['sh1bc'])

    def bbc(c):
        return c - 512 if c < NF else 1544 + (c - OFF['mo'])

    def prep_win(st, c0, n, dst, d0, kind, gi0=None):
        stg = Rot([sb(st, "wstg%d" % i, [128, 8, 256]) for i in range(2)])
        brow = Rot([sb(st, "brow%d" % i, [1, 256]) for i in range(2)])
        wv = w_in_p.rearrange("(k p) c -> p k c", p=128)
        for p0 in range(0, n, 256):
            pn = min(256, n - p0)
            sg = stg.next()
            S.dma('sp', sg[:, :, 0:pn], wv[:, :, c0 + p0:c0 + p0 + pn], writes=[sg.key])
            for kt in range(KT):
                S.op('act', lambda e, sg=sg, kt=kt, p0=p0, pn=pn: e.activation(out=dst[:, kt, d0 + p0:d0 + p0 + pn], in_=sg[:, kt, 0:pn], func=AF.Copy, scale=gmod1[:, kt:kt + 1]),
                     reads=[sg.key, 'gmod1'], writes=[dst.key])
            if kind == 'tm':
                ps = psm.next()
                br = brow.next()
                S.dma('sp', br[0:1, 0:pn], b_in_row[0:1, c0 + p0:c0 + p0 + pn], writes=[br.key])
                for kt in range(KT):
                    S.op('pe', lambda e, ps=ps, sg=sg, kt=kt, pn=pn: e.matmul(ps[:, 0:pn], lhsT=sh1bc[:, kt, :], rhs=sg[:, kt, 0:pn], start=(kt == 0), stop=False),
                         reads=['sh1bc', sg.key], writes=[ps.key])
                S.op('pe', lambda e, ps=ps, br=br, pn=pn: e.matmul(ps[:, 0:pn], lhsT=ones_f[0:1, :], rhs=br[0:1, 0:pn], start=False, stop=True),
                     reads=['ones_f', br.key], writes=[ps.key])
                b0 = bbc(c0 + p0)
                S.op('dve', lambda e, ps=ps, b0=b0, pn=pn: e.tensor_copy(out=bias_bc[:, b0:b0 + pn], in_=ps[:, 0:pn]), reads=[ps.key], writes=['bias_bc'])
            else:
                for gg in range(pn // 128):
                    gi = gi0 + (p0 // 128) + gg
                    ps = pss[1]
                    for kt in range(KT):
                        S.op('pe', lambda e, sg=sg, kt=kt, gg=gg, gi=gi: e.matmul(pss[1][:, gi:gi + 1], lhsT=sg[:, kt, gg * 128:(gg + 1) * 128], rhs=modc[:, C_SH1 + kt:C_SH1 + kt + 1], start=(kt == 0), stop=(kt == KT - 1)),
                             reads=[sg.key, 'modc'], writes=[ps.key])
                    S.op('dve', lambda e, gi=gi: e.tensor_tensor(out=biasc[:, gi:gi + 1], in0=pss[1][:, gi:gi + 1], in1=bincol[:, gi:gi + 1], op=ALU.add), reads=[pss[1].key, 'bincol'], writes=['biasc'])

    def front(xt, N, xs, sqr, lnr, rsr):
        sq = sqr.next(); lnv = lnr.next(); rs = rsr.next()
        S.op('act', lambda e: e.activation(out=sq[:, :, 0:N], in_=xt[:, :, 0:N], func=AF.Square), reads=[xt.key], writes=[sq.key])
        ps = psm.next()
        for kt in range(KT):
            S.op('pe', lambda e, kt=kt: e.matmul(ps[:, 0:N], lhsT=ones_b[:], rhs=sq[:, kt, 0:N], start=(kt == 0), stop=(kt == KT - 1)), reads=['ones_b', sq.key], writes=[ps.key])
        S.op('act', lambda e: e.activation(out=lnv[:, 0:N], in_=ps[:, 0:N], func=AF.Ln, scale=1.0 / D, bias=eps_t[:]), reads=[ps.key, 'eps_t'], writes=[lnv.key])
        S.op('act', lambda e: e.activation(out=rs[:, 0:N], in_=lnv[:, 0:N], func=AF.Exp, scale=-0.5), reads=[lnv.key], writes=[rs.key])
        S.op('dve', lambda e: e.tensor_tensor(out=xs[:, 0:4, 0:N], in0=xt[:, 0:4, 0:N], in1=rs[:, 0:N].unsqueeze(1).to_broadcast([128, 4, N]), op=ALU.mult), reads=[xt.key, rs.key], writes=[xs.key + "a"])
        for kt in range(4, 8):
            S.op('pool', lambda e, kt=kt: e.tensor_tensor(out=xs[:, kt, 0:N], in0=xt[:, kt, 0:N], in1=rs[:, 0:N], op=ALU.mult), reads=[xt.key, rs.key], writes=[xs.key + "b"])
        return rs

    def xskeys(xs):
        return [xs.key + "a", xs.key + "b"]

    def wsl(c0, n, WF, WO):
        if c0 < NF:
            return WF, c0
        return WO, c0 - NF

    def proj_fm(ps, W, wc, xs, x0, N):
        for kt in range(KT):
            S.op('pe', lambda e, kt=kt: e.matmul(ps[:, 0:N], lhsT=W[:, kt, wc:wc + 128], rhs=xs[:, kt, x0:x0 + N], start=(kt == 0), stop=(kt == KT - 1)),
                 reads=[W.key] + xskeys(xs), writes=[ps.key])

    def proj_tm(psap, pskey, W, wc, ncol, xs, x0):
        for kt in range(KT):
            S.op('pe', lambda e, kt=kt: e.matmul(psap, lhsT=xs[:, kt, x0:x0 + 128], rhs=W[:, kt, wc:wc + ncol], start=(kt == 0), stop=(kt == KT - 1)),
                 reads=[W.key] + xskeys(xs), writes=[pskey])

    def rope(kx, out3, okey, cos2, sinp, sinn, T, tA, tB):
        c2 = cos2[:, T, :].unsqueeze(1).to_broadcast([128, 8, 64])
        sp_ = sinp[:, T, :].unsqueeze(1).to_broadcast([128, 8, 32])
        sn_ = sinn[:, T, :].unsqueeze(1).to_broadcast([128, 8, 32])
        S.op('dve', lambda e: e.tensor_tensor(out=tA[:], in0=kx[:], in1=c2, op=ALU.mult), reads=[kx.key, cos2.key], writes=[tA.key])
        S.op('dve', lambda e: e.tensor_tensor(out=tB[:, :, 0:32], in0=kx[:, :, 32:64], in1=sn_, op=ALU.mult), reads=[kx.key, sinn.key], writes=[tB.key + "l"])
        S.op('dve', lambda e: e.tensor_tensor(out=tB[:, :, 32:64], in0=kx[:, :, 0:32], in1=sp_, op=ALU.mult), reads=[kx.key, sinp.key], writes=[tB.key + "h"])
        S.op('dve', lambda e: e.tensor_tensor(out=out3, in0=tA[:], in1=tB[:], op=ALU.add), reads=[tA.key, tB.key + "l", tB.key + "h"], writes=[okey])

    gcnt = [0]

    def gates(gif, t, gt):
        l, u, w, dec, tmp = gt['l'], gt['u'], gt['w'], gt['dec'], gt['tmp']
        r_ = gcnt[0] % 4
        gcnt[0] += 1
        gp = TL(pss[1].t[:, 64 * r_:64 * r_ + 64], "pss1")
        S.op('act', lambda e: e.activation(out=tmp[:], in_=gif[:, t, 4:8], func=AF.Exp, scale=-1.0), reads=[gif.key], writes=[tmp.key])
        S.op('act', lambda e: e.activation(out=l[:], in_=tmp[:], func=AF.Ln, bias=one_t[:]), reads=[tmp.key, 'one_t'], writes=[l.key])
        S.op('pe', lambda e: e.matmul(gp[:, 32:36], lhsT=ones_f[:], rhs=l[:], start=True, stop=True), reads=['ones_f', l.key], writes=[gp.key])
        S.op('pe', lambda e: e.matmul(gp[:, 36:40], lhsT=tri_f[:], rhs=l[:], start=True, stop=True), reads=['tri_f', l.key], writes=[gp.key])
        S.op('dve', lambda e: e.tensor_tensor(out=u[:], in0=gp[:, 36:40], in1=gif[:, t, 0:4], op=ALU.add), reads=[gp.key, gif.key], writes=[u.key])
        S.op('dve', lambda e: e.tensor_tensor(out=tmp[:], in0=u[:], in1=gp[:, 32:36], op=ALU.subtract), reads=[gp.key, u.key], writes=[tmp.key])
        S.op('act', lambda e: e.activation(out=w[:], in_=tmp[:], func=AF.Exp), reads=[tmp.key], writes=[w.key])
        S.op('act', lambda e: e.activation(out=dec[:], in_=gp[:, 32:36], func=AF.Exp, scale=-1.0), reads=[gp.key], writes=[dec.key])

    def state_update(vaug, ktm, t, gt, CT, wvr):
        wv = wvr.next()
        for h in range(4):
            S.op('act', lambda e, h=h: e.activation(out=wv[:, h, :], in_=vaug[:, t, h, :], func=AF.Copy, scale=gt['w'][:, h:h + 1]), reads=[vaug.key, gt['w'].key], writes=[wv.key])
        for hh in range(2):
            ps = psm.next()
            for h2 in range(2):
                h = hh * 2 + h2
                S.op('pe', lambda e, h=h, h2=h2, ps=ps: e.matmul(ps[:, h2 * 129:(h2 + 1) * 129], lhsT=ktm[:, t, h, :], rhs=wv[:, h, :], start=True, stop=True), reads=[ktm.key, wv.key], writes=[ps.key])
            for h2 in range(2):
                h = hh * 2 + h2
                S.op('dve', lambda e, h=h, h2=h2, ps=ps: e.scalar_tensor_tensor(out=CT[:, h, :], in0=CT[:, h, :], scalar=gt['dec'][:, h:h + 1], in1=ps[:, h2 * 129:(h2 + 1) * 129], op0=ALU.mult, op1=ALU.add),
                     reads=[ps.key, gt['dec'].key], writes=[CT.key])

    FO = ExitStack()
    WF = sb(FO, "WF", [128, 8, NF], BF16)
    Ccur = sb(FO, "Ccur", [128, 4, 129])
    S.op('pool', lambda e: e.memset(kmT[:], 0.0), writes=['kmT'])
    with ExitStack() as st:
        prep_win(st, OFF['mk'], 512, WF, OFF['mk'], 'fm', gi0=0)
        prep_win(st, OFF['mv'], 1544, WF, OFF['mv'], 'tm')
        S.barrier()

    def rope_tables(tb, pos_ap, NTB):
        S.dma('sp', tb['pi'][:], pos_ap, writes=[tb['pi'].key])
        S.op('dve', lambda e: e.tensor_copy(out=tb['pf'][:], in_=tb['pi'][:]), reads=[tb['pi'].key], writes=[tb['pf'].key])
        v, vi, vf = tb['v'], tb['vi'], tb['vf']
        S.op('dve', lambda e: e.tensor_tensor(out=v[:], in0=invf[:].unsqueeze(1).to_broadcast([128, NTB, 32]), in1=tb['pf'][:].unsqueeze(2).to_broadcast([128, NTB, 32]), op=ALU.mult),
             reads=['invf_t', tb['pf'].key], writes=[v.key])
        for which in ['sin', 'cos']:
            if which == 'cos':
                S.op('dve', lambda e: e.tensor_scalar(out=v[:], in0=v[:], scalar1=0.25, scalar2=None, op0=ALU.add), reads=[], writes=[v.key])
            S.op('dve', lambda e: e.tensor_copy(out=vi[:], in_=v[:]), reads=[v.key], writes=[vi.key])
            S.op('dve', lambda e: e.tensor_copy(out=vf[:], in_=vi[:]), reads=[vi.key], writes=[vf.key])
            S.op('dve', lambda e: e.tensor_tensor(out=vf[:], in0=v[:], in1=vf[:], op=ALU.subtract), reads=[v.key], writes=[vf.key])
            if which == 'sin':
                S.op('act', lambda e: e.activation(out=tb['sinp'][:], in_=vf[:], func=AF.Sin, scale=2 * math.pi), reads=[vf.key], writes=[tb['sinp'].key])
                S.op('dve', lambda e: e.tensor_scalar(out=tb['sinn'][:], in0=tb['sinp'][:], scalar1=-1.0, scalar2=None, op0=ALU.mult), reads=[tb['sinp'].key], writes=[tb['sinn'].key])
            else:
                S.op('act', lambda e: e.activation(out=tb['cos2'][:, :, 0:32], in_=vf[:], func=AF.Sin, scale=2 * math.pi), reads=[vf.key], writes=[tb['cos2'].key])
                S.op('dve', lambda e: e.tensor_copy(out=tb['cos2'][:, :, 32:64], in_=tb['cos2'][:, :, 0:32]), reads=[], writes=[tb['cos2'].key])

    def rope_tiles(st, sfx="", NTB=2, tmp=None):
        tb = {}
        tmp = tmp or {}
        for k_, shp, dt_ in [('pi', [128, NTB], I32), ('pf', [128, NTB], F32), ('v', [128, NTB, 32], F32), ('vi', [128, NTB, 32], I32), ('vf', [128, NTB, 32], F32)]:
            tb[k_] = tmp[k_] if k_ in tmp else sb(st, "rt_" + k_ + sfx, shp, dt_)
        tb['cos2'] = sb(st, "rt_cos2" + sfx, [128, NTB, 64]); tb['sinp'] = sb(st, "rt_sinp" + sfx, [128, NTB, 32]); tb['sinn'] = sb(st, "rt_sinn" + sfx, [128, NTB, 32])
        return tb

    def conv_group(zt, zoff, cg, acc, out_ap, okey, silr, oscale=1.0):
        pc = psm.next()
        for j in range(4):
            S.op('pe', lambda e, j=j: e.matmul(pc[:, 0:256], lhsT=dgw[:, cg, j, :], rhs=zt[:, zoff + j:zoff + j + 256], start=(j == 0), stop=(j == 3)), reads=['dgw', zt.key], writes=[pc.key])
        se = silr.next(); sl = silr.next()
        S.op('act', lambda e: e.activation(out=se[:], in_=pc[:, 0:256], func=AF.Exp, scale=-1.0, bias=ncb[:, cg:cg + 1]), reads=[pc.key, 'ncb'], writes=[se.key])
        S.op('act', lambda e: e.activation(out=sl[:], in_=se[:], func=AF.Ln, bias=one_t[:]), reads=[se.key, 'one_t'], writes=[sl.key])
        qb = lnq_t if oscale != 1.0 else zero_t
        S.op('act', lambda e: e.activation(out=se[:], in_=sl[:], func=AF.Exp, scale=-1.0, bias=qb[:]), reads=[sl.key, qb.key], writes=[se.key])
        S.op('dve', lambda e: e.scalar_tensor_tensor(out=out_ap, in0=pc[:, 0:256], scalar=cb[:, cg:cg + 1], in1=se[:], op0=ALU.add, op1=ALU.mult), reads=[pc.key, 'cb_t', se.key], writes=[okey])

    VSTB = 4 if NBLK >= 4 else NBLK
    chk_stop('W')
    with ExitStack() as st:
        awr = Rot([sb(st, "awb%d" % i, [128, 8, 128]) for i in range(2)])
        mod_groups(awr, 16, 48, 'modc_b')
        S.op('dve', lambda e: e.scalar_tensor_tensor(out=gmod2[:], in0=modc[:, 32:40], scalar=1.0, in1=n2g[:], op0=ALU.add, op1=ALU.mult), reads=['modc_b', 'n2g_t'], writes=['gmod2'])
        RCH = min(8, NBLK)
        tbs = [rope_tiles(st, "a", 2 * RCH)]
        tbs.append(rope_tiles(st, "b", 2 * RCH, tmp=tbs[0]))
        xtr = Rot([sb(st, "xt%d" % i, [128, 8, 260]) for i in range(2)])
        sqr = Rot([sb(st, "sq%d" % i, [128, 8, 260], BF16) for i in range(2)])
        lnr = Rot([sb(st, "ln%d" % i, [128, 260]) for i in range(2)])
        rsr = Rot([sb(st, "rs%d" % i, [128, 260]) for i in range(2)])
        xsr = Rot([sb(st, "xs%d" % i, [128, 8, 260], BF16) for i in range(2)])
        zgr = Rot([sb(st, "zg%d" % i, [128, 260], BF16) for i in range(3)])
        accr = Rot([None])
        silr = Rot([sb(st, "sil%d" % i, [128, 256]) for i in range(4)])
        hal = sb(st, "hal", [128, 4, 3], BF16)
        kT = rt(st, "kT", [128, 4, 256], BF16)
        ktm = rt(st, "ktm", [128, 2, 4, 128], BF16)
        vaugr = Rot([sb(st, "vaug%d" % i, [128, 2, 4, 129], BF16) for i in range(2)])
        gifr = Rot([sb(st, "gif%d" % i, [128, 2, 8]) for i in range(2)])
        kxr = Rot([sb(st, "kx%d" % i, [128, 8, 64]) for i in range(2)])
        tAr = Rot([sb(st, "ropeA%d" % i, [128, 8, 64]) for i in range(2)]); tBr = Rot([sb(st, "ropeB%d" % i, [128, 8, 64]) for i in range(2)])
        Ktm = rt(st, "Ktm", [128, 2, 8, 64], BF16)
        KTblk = Rot([sb(st, "KTblk%d" % i, [128, 4, 256], BF16) for i in range(2)])
        vst = Rot([sb(st, "vst%d" % i, [128, 2 * VSTB, 8, 65], BF16) for i in range(2)])
        CT = sb(st, "CT", [128, 4, 129])
        gts = [{k: sb(st, "g%d_" % r + k, [128, 4]) for k in ['l', 'u', 'w', 'dec', 'tmp']} for r in range(3)]
        gti = [0]
        wvr = Rot([sb(st, "wv%d" % i, [128, 4, 129], BF16) for i in range(2)])
        ksq = rt(st, "ksq", [128, 8, 64]); kn2 = rt(st, "kn2", [128, 8]); kmacc = sb(st, "kmacc", [128, 8])
        S.op('pool', lambda e: e.memset(hal[:], 0.0), writes=['hal'])
        S.op('pool', lambda e: e.memset(CT[:], 0.0), writes=['CT'])
        S.op('pool', lambda e: e.memset(kmacc[:], 0.0), writes=['kmacc'])
        for v_ in vst.tiles:
            S.op('pool', lambda e, v_=v_: e.memset(v_[:], 1.0), writes=[v_.key])
        for v_ in vaugr.tiles:
            S.op('pool', lambda e, v_=v_: e.memset(v_[:], 1.0), writes=[v_.key])
        xTf_v = xTf.rearrange("(k p) t -> p k t", p=128)
        KTs_v = KTs.rearrange("(hp r) t -> r hp t", r=128)
        vcur = None
        for n in range(NBLK):
            i_own = n // 4
            if n % 4 == 0:
                S.op('dve', lambda e, n=n: e.tensor_scalar(out=Ccur[:], in0=CT[:], scalar1=capsel[:, 0:1], scalar2=None, op0=ALU.mult), reads=['CT', 'capsel_t'], writes=['Ccur'])
            else:
                S.op('dve', lambda e, n=n: e.scalar_tensor_tensor(out=Ccur[:], in0=CT[:], scalar=capsel[:, n % 4:n % 4 + 1], in1=Ccur[:], op0=ALU.mult, op1=ALU.add), reads=['CT', 'capsel_t'], writes=['Ccur'])
            if n % 4 == 3:
                S.dma('pool', CCs[i_own].rearrange("p (h e) -> p h e", h=4), Ccur[:], reads=['Ccur'], writes=[('CCs', i_own)])
            xt = xtr.next(); xs = xsr.next()
            tb = tbs[(n // RCH) % 2]
            cos2, sinp, sinn = tb['cos2'], tb['sinp'], tb['sinn']
            kT.rotate(); ktm.rotate(); Ktm.rotate()
            S.dma('sp', xt[:, :, 0:256], xTf_v[:, :, n * 256:(n + 1) * 256], writes=[xt.key])
            if n % RCH == 0:
                rope_tables(tb, posf_d[:, 2 * n:2 * n + 2 * RCH], 2 * RCH)
            front(xt, 256, xs, sqr, lnr, rsr)
            for hd in range(4):
                ps = psm.next(); zg = zgr.next(); acc = accr.next()
                proj_fm(ps, WF, OFF['mk'] + hd * 128, xs, 0, 256)
                S.op('pool', lambda e, hd=hd, zg=zg: e.tensor_copy(out=zg[:, 0:3], in_=hal[:, hd, :]), reads=['hal'], writes=[zg.key])
                S.op('act', lambda e, hd=hd, ps=ps, zg=zg: e.activation(out=zg[:, 3:259], in_=ps[:, 0:256], func=AF.Identity, bias=biasc[:, hd:hd + 1]), reads=[ps.key, 'biasc'], writes=[zg.key])
                S.op('pool', lambda e, hd=hd, zg=zg: e.tensor_copy(out=hal[:, hd, :], in_=zg[:, 256:259]), reads=[zg.key], writes=['hal'])
                conv_group(zg, 0, 4 + hd, acc, kT[:, hd, :], 'kT', silr)
            pt = pstr.next()
            for t in range(2):
                for hd in range(4):
                    S.op('pe', lambda e, t=t, hd=hd, pt=pt: e.transpose(out=pt[:, (t * 4 + hd) * 128:(t * 4 + hd + 1) * 128], in_=kT[:, hd, t * 128:(t + 1) * 128], identity=ident_b[:]), reads=['kT', 'ident_b'], writes=[pt.key])
            S.op('act', lambda e, pt=pt: e.activation(out=ktm[:].rearrange("p t h e -> p (t h e)"), in_=pt[:], func=AF.Copy), reads=[pt.key], writes=['ktm'])
            vaug = vaugr.next(); gif = gifr.next()
            if n % VSTB == 0:
                vcur = vst.next()
            KTb = KTblk.next()
            ptK = pstr.next()
            for t in range(2):
                ps = psm.next()
                proj_tm(ps[:, 0:512], ps.key, WF, OFF['mv'], 512, xs, t * 128)
                S.op('dve', lambda e, t=t, ps=ps, vaug=vaug: e.tensor_tensor(out=vaug[:, t, :, 0:128], in0=ps[:, 0:512].rearrange("p (h d) -> p h d", h=4), in1=bias_bc[:, bbc(OFF['mv']):bbc(OFF['mv']) + 512].rearrange("p (h d) -> p h d", h=4), op=ALU.add),
                     reads=[ps.key, 'bias_bc'], writes=[vaug.key])
                gp = psm.next()
                proj_tm(gp[:, 0:8], gp.key, WF, OFF['mif'], 8, xs, t * 128)
                S.op('dve', lambda e, t=t, gif=gif, gp=gp: e.tensor_tensor(out=gif[:, t, :], in0=gp[:, 0:8], in1=bias_bc[:, bbc(OFF['mif']):bbc(OFF['mif']) + 8], op=ALU.add), reads=[gp.key, 'bias_bc'], writes=[gif.key])
                ps = psm.next(); kx = kxr.next()
                proj_tm(ps[:, 0:512], ps.key, WF, OFF['ak'], 512, xs, t * 128)
                S.op('dve', lambda e, ps=ps, kx=kx: e.tensor_tensor(out=kx[:].rearrange("p h d -> p (h d)"), in0=ps[:, 0:512], in1=bias_bc[:, bbc(OFF['ak']):bbc(OFF['ak']) + 512], op=ALU.add), reads=[ps.key, 'bias_bc'], writes=[kx.key])
                rope(kx, Ktm[:, t, :, :], 'Ktm', cos2, sinp, sinn, (n % RCH) * 2 + t, tAr.next(), tBr.next())
                ps = psm.next()
                proj_tm(ps[:, 0:512], ps.key, WF, OFF['av'], 512, xs, t * 128)
                slot = (n % VSTB) * 2 + t
                S.op('dve', lambda e, ps=ps, slot=slot, vc=vcur: e.tensor_tensor(out=vc[:, slot, :, 0:64], in0=ps[:, 0:512].rearrange("p (h d) -> p h d", h=8), in1=bias_bc[:, bbc(OFF['av']):bbc(OFF['av']) + 512].rearrange("p (h d) -> p h d", h=8), op=ALU.add),
                     reads=[ps.key, 'bias_bc'], writes=[vcur.key])
            for t in range(2):
                for hp in range(4):
                    S.op('pe', lambda e, t=t, hp=hp, ptK=ptK: e.transpose(out=ptK[:, (hp * 2 + t) * 128:(hp * 2 + t + 1) * 128], in_=Ktm[:, t, 2 * hp:2 * hp + 2, :].rearrange("p h d -> p (h d)"), identity=ident_b[:]),
                         reads=['Ktm', 'ident_b'], writes=[ptK.key])
            S.op('act', lambda e, ptK=ptK, KTb=KTb: e.activation(out=KTb[:].rearrange("p h t -> p (h t)"), in_=ptK[:], func=AF.Copy), reads=[ptK.key], writes=[KTb.key])
            S.dma('pool', KTs_v[:, :, n * 256:(n + 1) * 256], KTb[:], reads=[KTb.key], writes=[('KTs', n)])
            kp = pss[0]
            for hp in range(4):
                for t in range(2):
                    S.op('pe', lambda e, t=t, hp=hp: e.matmul(pss[0][:, 16 + hp:17 + hp], lhsT=Ktm[:, t, 2 * hp:2 * hp + 2, :].rearrange("p h d -> p (h d)"), rhs=ones_b[:, 0:1], start=(t == 0), stop=(t == 1)),
                         reads=['Ktm', 'ones_b'], writes=[kp.key])
            S.op('act', lambda e, n=n: e.activation(out=kmT[0:64, :, 0, n], in_=pss[0][0:64, 16:20], func=AF.Copy, scale=1.0 / 256), reads=[kp.key], writes=['kmT'])
            S.op('act', lambda e, n=n: e.activation(out=kmT[64:128, :, 1, n], in_=pss[0][64:128, 16:20], func=AF.Copy, scale=1.0 / 256), reads=[kp.key], writes=['kmT'])
            for t in range(2):
                ksq.rotate(); kn2.rotate()
                S.op('pool', lambda e, t=t: e.tensor_tensor(out=ksq[:], in0=Ktm[:, t, :, :], in1=Ktm[:, t, :, :], op=ALU.mult), reads=['Ktm'], writes=['ksq'])
                S.op('dve', lambda e: e.reduce_sum(out=kn2[:], in_=ksq[:], axis=AX.X), reads=['ksq'], writes=['kn2'])
                S.op('dve', lambda e: e.tensor_tensor(out=kmacc[:], in0=kmacc[:], in1=kn2[:], op=ALU.max), reads=['kn2'], writes=['kmacc'])
            if n % VSTB == VSTB - 1:
                b0 = n - (VSTB - 1)
                for h in range(8):
                    S.dma('pool', Vs[h, :, b0 * 2 * 65:(n + 1) * 2 * 65].rearrange("p (t d) -> p t d", d=65), vcur[:, 0:2 * VSTB, h, :], reads=[vcur.key], writes=[('Vs', h, b0)])
            for t in range(2):
                gt = gts[gti[0] % 3]; gti[0] += 1
                gates(gif, t, gt)
                state_update(vaug, ktm, t, gt, CT, wvr)
        ptf = pss[0]
        S.op('pe', lambda e: e.transpose(out=ptf[0:8, 0:128], in_=kmacc[:], identity=ident_f[:]), reads=['kmacc', 'ident_f'], writes=[ptf.key])
        kmx = sb(st, "kmx", [8, 1]); dg = sb(st, "dg", [8, 8])
        S.op('dve', lambda e: e.reduce_max(out=kmx[:], in_=ptf[0:8, 0:128], axis=AX.X), reads=[ptf.key], writes=['kmx'])
        S.op('dve', lambda e: e.tensor_scalar(out=dg[:], in0=ident_f[0:8, 0:8], scalar1=kmx[:, 0:1], scalar2=None, op0=ALU.mult), reads=['kmx', 'ident_f'], writes=['dg'])
        S.op('pe', lambda e: e.matmul(ptf[:, 256:264], lhsT=ones_f[0:8, :], rhs=dg[:], start=True, stop=True), reads=['ones_f', 'dg', ptf.key], writes=[ptf.key])
        S.op('act', lambda e: e.activation(out=kmaxbc[:], in_=ptf[:, 256:264], func=AF.Ln, bias=eps_t[:]), reads=[ptf.key, 'eps_t'], writes=['kmaxbc'])
        S.op('act', lambda e: e.activation(out=kmaxbc[:], in_=kmaxbc[:], func=AF.Exp, scale=0.5), reads=[], writes=['kmaxbc'])
        dump("kmT", kmT[:].rearrange("p h a n -> p (h a n)"), [128, 256], ['kmT'])
        dump("kmaxbc", kmaxbc[:], [128, 8], ['kmaxbc'])
        dump("CTfin", CT[:].rearrange("p h e -> p (h e)"), [128, 4 * 129], ['CT'])
        S.barrier()

    chk_stop('F')
    with ExitStack() as st:
        WO = sb(st, "WO", [128, 8, 1536], BF16)
        with ExitStack() as st2:
            prep_win(st2, OFF['mq'], 512, WO, OFF['mq'] - NF, 'fm', gi0=4)
            prep_win(st2, OFF['mo'], 1024, WO, OFF['mo'] - NF, 'tm')
            S.barrier()
        tb = {}
        tb['cos2'] = sb(st, "rt_cos2o", [128, NTO, 64]); tb['sinp'] = sb(st, "rt_sinpo", [128, NTO, 32]); tb['sinn'] = sb(st, "rt_sinno", [128, NTO, 32])
        cos2, sinp, sinn = tb['cos2'], tb['sinp'], tb['sinn']
        with ExitStack() as st2:
            for k_, shp, dt_ in [('pi', [128, NTO], I32), ('pf', [128, NTO], F32), ('v', [128, NTO, 32], F32), ('vi', [128, NTO, 32], I32), ('vf', [128, NTO, 32], F32)]:
                tb[k_] = sb(st2, "rt_" + k_ + "o", shp, dt_)
            rope_tables(tb, poso_d, NTO)
            S.barrier()
        xtr = Rot([sb(st, "xt%d" % i, [128, 8, 260]) for i in range(1)])
        sqr = Rot([sb(st, "sq%d" % i, [128, 8, 260], BF16) for i in range(1)])
        lnr = Rot([sb(st, "ln%d" % i, [128, 260]) for i in range(2)])
        rsr = Rot([sb(st, "rs%d" % i, [128, 260]) for i in range(2)])
        xsr = Rot([sb(st, "xs%d" % i, [128, 8, 260], BF16) for i in range(2)])
        zgr = Rot([sb(st, "zg%d" % i, [128, 260], BF16) for i in range(3)])
        accr = Rot([None])
        silr = Rot([sb(st, "sil%d" % i, [128, 256]) for i in range(4)])
        qkT = sb(st, "qkT", [128, 8, 256], BF16)
        ktm = sb(st, "ktm", [128, 2, 4, 128], BF16)
        vaug = rt(st, "vaug", [128, 2, 4, 129], BF16)
        gif = rt(st, "gif", [128, 2, 8])
        kxr = Rot([sb(st, "kx%d" % i, [128, 8, 64]) for i in range(2)])
        tAr = Rot([sb(st, "ropeA%d" % i, [128, 8, 64]) for i in range(2)]); tBr = Rot([sb(st, "ropeB%d" % i, [128, 8, 64]) for i in range(2)])
        Qtm = rt(st, "Qtm", [128, 8, 64], BF16); Ktm1 = rt(st, "Ktm1", [128, 8, 64], BF16)
        Qaug = rt(st, "Qaug", [128, 8, 97], BF16); Kaug = rt(st, "Kaug", [128, 8, 97], BF16)
        QTp = rt(st, "QTp", [128, 4, 128], BF16)
        gm = rt(st, "gm", [128, 8, 32]); t8 = rt(st, "t8", [128, 8, 8]); thr = rt(st, "thr", [128, 8]); msk = rt(st, "msk", [128, 8, 32])
        sqq = sb(st, "sqq", [128, 8, 64]); qn = rt(st, "qn", [128, 8])
        QTab = sb(st, "QTab", [128, 8, 256], BF16); KTab = sb(st, "KTab", [128, 8, 256], BF16)
        Vob = sb(st, "Vob", [128, 8, 2, 65], BF16)
        ymTb = sb(st, "ymTb", [128, 4, 256], BF16)
        mo_f = rt(st, "mo_f", [128, 512], k=1); sigmo = rt(st, "sigmo", [128, 512], BF16)
        CT = sb(st, "CT", [128, 4, 129]); CTb = rt(st, "CTb", [128, 4, 129], BF16)
        gts = [{k: sb(st, "g%d_" % r + k, [128, 4]) for k in ['l', 'u', 'w', 'dec', 'tmp']} for r in range(3)]
        gti = [0]
        wvr = Rot([sb(st, "wv%d" % i, [128, 4, 129], BF16) for i in range(2)])
        LFbc = rt(st, "LFbc", [128, 4, 128], k=1)
        EB = rt(st, "EB", [128, 4, 128], k=1); DT = rt(st, "DT", [128, 4, 128])
        SWT = rt(st, "SWT", [128, 4, 128], BF16); qsT = rt(st, "qsT", [128, 4, 128], BF16)
        absd = rt(st, "absd", [128, 4]); rr = rt(st, "rr", [128, 4])
        hsb = rt(st, "hsb", [128, 4, 128], k=1); hsq = sb(st, "hsq", [128, 4, 128]); ssq = rt(st, "ssq", [128, 4]); rsh = rt(st, "rsh", [128, 4])
        ym = rt(st, "ym", [128, 512], BF16, k=1)
        for _ in range(2):
            S.op('pool', lambda e: e.memset(vaug[:], 1.0), writes=['vaug'])
            S.op('pool', lambda e: e.memset(Kaug[:], 0.0), writes=['Kaug'])
            S.op('pool', lambda e: e.memset(Kaug[:, :, 96:97], 1.0), writes=['Kaug'])
            vaug.rotate(); Kaug.rotate()
        S.op('pool', lambda e: e.memset(Vob[:], 1.0), writes=['Vob'])
        S.op('pool', lambda e: e.memset(QTab[:], 0.0), writes=['QTab'])
        S.op('pool', lambda e: e.memset(KTab[:], 0.0), writes=['KTab'])
        xTo_v = xTo.rearrange("(k p) t -> p k t", p=128)
        for i in range(NOWN):
            xt = xtr.next(); xs = xsr.next()
            vaug.rotate(); gif.rotate()
            S.dma('sp', xt[:], xTo_v[:, :, i * 260:(i + 1) * 260], writes=[xt.key])
            S.dma('sp', CT[:], CCs[i].rearrange("p (h e) -> p h e", h=4), reads=[('CCs', i)], writes=['CT'])
            front(xt, 260, xs, sqr, lnr, rsr)
            S.dma('pool', XSs[i].rearrange("p (k t) -> p k t", k=8), xs[:, :, 4:260], reads=xskeys(xs), writes=[('XSs', i)])
            for g in range(8):
                ps = psm.next(); zg = zgr.next(); acc = accr.next()
                if g < 4:
                    proj_fm(ps, WO, OFF['mq'] - NF + g * 128, xs, 0, 260)
                    bcol = 4 + g
                else:
                    proj_fm(ps, WF, OFF['mk'] + (g - 4) * 128, xs, 0, 260)
                    bcol = g - 4
                S.op('act', lambda e, zg=zg, ps=ps, bcol=bcol: e.activation(out=zg[:], in_=ps[:, 0:260], func=AF.Identity, bias=biasc[:, bcol:bcol + 1]), reads=[ps.key, 'biasc'], writes=[zg.key])
                S.op('dve', lambda e, i=i, zg=zg: e.tensor_scalar(out=zg[:, 0:4], in0=zg[:, 0:4], scalar1=hv[:, i:i + 1], scalar2=None, op0=ALU.mult), reads=['hv_t'], writes=[zg.key])
                conv_group(zg, 1, g, acc, qkT[:, g, :], 'qkT', silr, oscale=(128.0 ** -0.5 if g < 4 else 1.0))
            pt = pstr.next()
            for t in range(2):
                for hd in range(4):
                    S.op('pe', lambda e, t=t, hd=hd, pt=pt: e.transpose(out=pt[:, (t * 4 + hd) * 128:(t * 4 + hd + 1) * 128], in_=qkT[:, 4 + hd, t * 128:(t + 1) * 128], identity=ident_b[:]), reads=['qkT', 'ident_b'], writes=[pt.key])
            S.op('act', lambda e, pt=pt: e.activation(out=ktm[:].rearrange("p t h e -> p (t h e)"), in_=pt[:], func=AF.Copy), reads=[pt.key], writes=['ktm'])
            S.op('act', lambda e: e.activation(out=CTb[:], in_=CT[:], func=AF.Copy), reads=['CT'], writes=['CTb'])
            if i == 0:
                chk_stop('O1a')
            for t in range(2):
                T = i * 2 + t
                x0 = 4 + t * 128
                for r_ in [Qtm, Ktm1, Qaug, Kaug, QTp, gm, t8, thr, msk, qn, mo_f, sigmo, LFbc, EB, DT, SWT, qsT, absd, rr, hsb, ssq, rsh, ym]:
                    r_.rotate()
                tA = tAr.next(); tB = tBr.next()
                gt = gts[gti[0] % 3]; gti[0] += 1
                ps = psm.next()
                proj_tm(ps[:, 0:512], ps.key, WF, OFF['mv'], 512, xs, x0)
                S.op('dve', lambda e, t=t, ps=ps: e.tensor_tensor(out=vaug[:, t, :, 0:128], in0=ps[:, 0:512].rearrange("p (h d) -> p h d", h=4), in1=bias_bc[:, bbc(OFF['mv']):bbc(OFF['mv']) + 512].rearrange("p (h d) -> p h d", h=4), op=ALU.add),
                     reads=[ps.key, 'bias_bc'], writes=['vaug'])
                gp = psm.next()
                proj_tm(gp[:, 0:8], gp.key, WF, OFF['mif'], 8, xs, x0)
                S.op('dve', lambda e, t=t, gp=gp: e.tensor_tensor(out=gif[:, t, :], in0=gp[:, 0:8], in1=bias_bc[:, bbc(OFF['mif']):bbc(OFF['mif']) + 8], op=ALU.add), reads=[gp.key, 'bias_bc'], writes=['gif'])
                ps = psm.next()
                proj_tm(ps[:, 0:512], ps.key, WO, OFF['mo'] - NF, 512, xs, x0)
                S.op('dve', lambda e, ps=ps: e.tensor_tensor(out=mo_f[:], in0=ps[:, 0:512], in1=bias_bc[:, bbc(OFF['mo']):bbc(OFF['mo']) + 512], op=ALU.add), reads=[ps.key, 'bias_bc'], writes=['mo_f'])
                S.op('act', lambda e: e.activation(out=mo_f[:], in_=mo_f[:], func=AF.Exp, scale=-1.0), reads=[], writes=['mo_f'])
                S.op('act', lambda e: e.activation(out=mo_f[:], in_=mo_f[:], func=AF.Ln, bias=one_t[:]), reads=['one_t'], writes=['mo_f'])
                S.op('act', lambda e: e.activation(out=sigmo[:], in_=mo_f[:], func=AF.Exp, scale=-1.0), reads=['mo_f'], writes=['sigmo'])
                ps = psm.next(); kx = kxr.next()
                proj_tm(ps[:, 0:512], ps.key, WO, OFF['aq'] - NF, 512, xs, x0)
                S.op('dve', lambda e, ps=ps, kx=kx: e.tensor_tensor(out=kx[:].rearrange("p h d -> p (h d)"), in0=ps[:, 0:512], in1=bias_bc[:, bbc(OFF['aq']):bbc(OFF['aq']) + 512], op=ALU.add), reads=[ps.key, 'bias_bc'], writes=[kx.key])
                rope(kx, Qtm[:], 'Qtm', cos2, sinp, sinn, T, tA, tB)
                ps = psm.next(); kx = kxr.next()
                proj_tm(ps[:, 0:512], ps.key, WF, OFF['ak'], 512, xs, x0)
                S.op('dve', lambda e, ps=ps, kx=kx: e.tensor_tensor(out=kx[:].rearrange("p h d -> p (h d)"), in0=ps[:, 0:512], in1=bias_bc[:, bbc(OFF['ak']):bbc(OFF['ak']) + 512], op=ALU.add), reads=[ps.key, 'bias_bc'], writes=[kx.key])
                rope(kx, Ktm1[:], 'Ktm1', cos2, sinp, sinn, T, tAr.next(), tBr.next())
                ps = psm.next()
                proj_tm(ps[:, 0:512], ps.key, WF, OFF['av'], 512, xs, x0)
                S.op('dve', lambda e, ps=ps, t=t: e.tensor_tensor(out=Vob[:, :, t, 0:64], in0=ps[:, 0:512].rearrange("p (h d) -> p h d", h=8), in1=bias_bc[:, bbc(OFF['av']):bbc(OFF['av']) + 512].rearrange("p (h d) -> p h d", h=8), op=ALU.add),
                     reads=[ps.key, 'bias_bc'], writes=['Vob'])
                if i == 0 and t == 0:
                    chk_stop('O1b')
                gates(gif, t, gt)
                for h in range(4):
                    S.op('act', lambda e, h=h: e.activation(out=LFbc[:, h, :], in_=ones_f[:], func=AF.Copy, scale=gt['l'][:, h:h + 1]), reads=['ones_f', gt['l'].key], writes=['LFbc'])
                pb = psm.next()
                for h in range(4):
                    S.op('pe', lambda e, h=h, pb=pb: e.matmul(pb[:, h * 128:(h + 1) * 128], lhsT=LFbc[:, h, :], rhs=tri_f[:], start=True, stop=True), reads=['LFbc', 'tri_f'], writes=[pb.key])
                S.op('act', lambda e, pb=pb: e.activation(out=EB[:].rearrange("p h t -> p (h t)"), in_=pb[:, 0:512], func=AF.Exp, scale=-1.0), reads=[pb.key], writes=['EB'])
                for h in range(4):
                    S.op('act', lambda e, h=h, pb=pb: e.activation(out=DT[:, h, :], in_=pb[:, h * 128:(h + 1) * 128], func=AF.Exp, scale=-1.0, bias=gt['u'][:, h:h + 1]), reads=[pb.key, gt['u'].key], writes=['DT'])
                for h in range(4):
                    S.op('pool', lambda e, h=h: e.tensor_tensor(out=DT[:, h, :], in0=DT[:, h, :], in1=tri_f[:], op=ALU.mult), reads=['tri_f'], writes=['DT'])
                pS = psm.next()
                for h in range(4):
                    S.op('pe', lambda e, h=h, t=t, pS=pS: e.matmul(pS[:, h * 128:(h + 1) * 128], lhsT=qkT[:, 4 + h, t * 128:(t + 1) * 128], rhs=qkT[:, h, t * 128:(t + 1) * 128], start=True, stop=True), reads=['qkT'], writes=[pS.key])
                S.op('dve', lambda e, pS=pS: e.tensor_tensor(out=SWT[:].rearrange("p h t -> p (h t)"), in0=DT[:].rearrange("p h t -> p (h t)"), in1=pS[:, 0:512], op=ALU.mult), reads=['DT', pS.key], writes=['SWT'])
                S.op('dve', lambda e, t=t: e.tensor_tensor(out=qsT[:], in0=qkT[:, 0:4, t * 128:(t + 1) * 128], in1=EB[:], op=ALU.mult), reads=['qkT', 'EB'], writes=['qsT'])
                pn = [psm.next(), psm.next()]
                for h in range(4):
                    pp = pn[h // 2]; o0 = (h % 2) * 129
                    S.op('pe', lambda e, h=h, t=t, pp=pp, o0=o0: e.matmul(pp[:, o0:o0 + 129], lhsT=SWT[:, h, :], rhs=vaug[:, t, h, :], start=True, stop=False), reads=['SWT', 'vaug'], writes=[pp.key])
                    S.op('pe', lambda e, h=h, pp=pp, o0=o0: e.matmul(pp[:, o0:o0 + 129], lhsT=qsT[:, h, :], rhs=CTb[:, h, :], start=False, stop=True), reads=['qsT', 'CTb'], writes=[pp.key])
                for h in range(4):
                    pp = pn[h // 2]; o0 = (h % 2) * 129
                    S.op('act', lambda e, h=h, pp=pp, o0=o0: e.activation(out=absd[:, h:h + 1], in_=pp[:, o0 + 128:o0 + 129], func=AF.Abs), reads=[pp.key], writes=['absd'])
                S.op('dve', lambda e: e.tensor_scalar(out=absd[:], in0=absd[:], scalar1=1.0, scalar2=None, op0=ALU.max), reads=[], writes=['absd'])
                S.op('dve', lambda e: e.reciprocal(out=rr[:], in_=absd[:]), reads=['absd'], writes=['rr'])
                for h in range(4):
                    pp = pn[h // 2]; o0 = (h % 2) * 129
                    S.op('act', lambda e, h=h, pp=pp, o0=o0: e.activation(out=hsb[:, h, :], in_=pp[:, o0:o0 + 128], func=AF.Copy, scale=rr[:, h:h + 1]), reads=[pp.key, 'rr'], writes=['hsb'])
                S.op('pool', lambda e: e.tensor_tensor(out=hsq[:], in0=hsb[:], in1=hsb[:], op=ALU.mult), reads=['hsb'], writes=['hsq'])
                S.op('dve', lambda e: e.reduce_sum(out=ssq[:], in_=hsq[:], axis=AX.X), reads=['hsq'], writes=['ssq'])
                S.op('act', lambda e: e.activation(out=ssq[:], in_=ssq[:], func=AF.Ln, scale=1.0 / 128, bias=eps_t[:]), reads=['eps_t'], writes=['ssq'])
                S.op('act', lambda e: e.activation(out=rsh[:], in_=ssq[:], func=AF.Exp, scale=-0.5), reads=['ssq'], writes=['rsh'])
                S.op('pool', lambda e: e.tensor_tensor(out=hsb[:].rearrange("p h d -> p (h d)"), in0=hsb[:].rearrange("p h d -> p (h d)"), in1=mng[:], op=ALU.mult), reads=['hsq', 'mng_t'], writes=['hsb'])
                for h in range(4):
                    S.op('dve', lambda e, h=h: e.scalar_tensor_tensor(out=ym[:, h * 128:(h + 1) * 128], in0=hsb[:, h, :], scalar=rsh[:, h:h + 1], in1=sigmo[:, h * 128:(h + 1) * 128], op0=ALU.mult, op1=ALU.mult),
                         reads=['hsb', 'rsh', 'sigmo'], writes=['ym'])
                pt = pstr.next()
                for c in range(4):
                    S.op('pe', lambda e, c=c, pt=pt: e.transpose(out=pt[:, c * 128:(c + 1) * 128], in_=ym[:, c * 128:(c + 1) * 128], identity=ident_b[:]), reads=['ym', 'ident_b'], writes=[pt.key])
                S.op('act', lambda e, pt=pt, t=t: e.activation(out=ymTb[:, :, t * 128:(t + 1) * 128], in_=pt[:, 0:512].rearrange("p (c t) -> p c t", c=4), func=AF.Copy), reads=[pt.key], writes=['ymTb'])
                state_update(vaug, ktm, t, gt, CT, wvr)
                CTb.rotate()
                S.op('act', lambda e: e.activation(out=CTb[:], in_=CT[:], func=AF.Copy), reads=['CT'], writes=['CTb'])
                if i == 0 and t == 0:
                    chk_stop('O1c')
                pt = pstr.next()
                for hp in range(4):
                    S.op('pe', lambda e, hp=hp, pt=pt: e.transpose(out=pt[:, hp * 128:(hp + 1) * 128], in_=Qtm[:, 2 * hp:2 * hp + 2, :].rearrange("p h d -> p (h d)"), identity=ident_b[:]), reads=['Qtm', 'ident_b'], writes=[pt.key])
                S.op('act', lambda e, pt=pt: e.activation(out=QTp[:].rearrange("p h t -> p (h t)"), in_=pt[:, 0:512], func=AF.Copy), reads=[pt.key], writes=['QTp'])
                pg = psm.next()
                for hp in range(4):
                    S.op('pe', lambda e, hp=hp, pg=pg: e.matmul(pg[:, hp * 64:(hp + 1) * 64], lhsT=QTp[:, hp, :], rhs=kmT[:, hp, :, :].rearrange("p a n -> p (a n)"), start=True, stop=True), reads=['QTp', 'kmT'], writes=[pg.key])
                S.op('dve', lambda e, pg=pg, i=i: e.tensor_tensor(out=gm[:], in0=pg[:, 0:256].rearrange("p (h n) -> p h n", h=8), in1=pastb[:, i * 32:(i + 1) * 32].unsqueeze(1).to_broadcast([128, 8, 32]), op=ALU.add), reads=[pg.key, 'pastb_t'], writes=['gm'])
                for h in range(8):
                    S.op('dve', lambda e, h=h: e.max(out=t8[:, h, :], in_=gm[:, h, :]), reads=['gm'], writes=['t8'])
                S.op('dve', lambda e: e.tensor_scalar(out=thr[:], in0=t8[:, :, 2], scalar1=-1e29, scalar2=None, op0=ALU.max), reads=['t8'], writes=['thr'])
                S.op('dve', lambda e: e.tensor_tensor(out=msk[:], in0=gm[:], in1=thr[:].unsqueeze(2).to_broadcast([128, 8, 32]), op=ALU.is_lt), reads=['gm', 'thr'], writes=['msk'])
                S.op('dve', lambda e: e.tensor_scalar(out=Qaug[:, :, 64:96], in0=msk[:], scalar1=NEG, scalar2=None, op0=ALU.mult), reads=['msk'], writes=['Qaug'])
                S.op('pool', lambda e: e.tensor_copy(out=Qaug[:, :, 0:64], in_=Qtm[:]), reads=['Qtm'], writes=['Qaug'])
                S.op('pool', lambda e: e.tensor_tensor(out=sqq[:], in0=Qtm[:], in1=Qtm[:], op=ALU.mult), reads=['Qtm'], writes=['sqq'])
                S.op('dve', lambda e: e.reduce_sum(out=qn[:], in_=sqq[:], axis=AX.X), reads=['sqq'], writes=['qn'])
                S.op('act', lambda e: e.activation(out=qn[:], in_=qn[:], func=AF.Ln, bias=eps_t[:]), reads=['eps_t'], writes=['qn'])
                S.op('act', lambda e: e.activation(out=qn[:], in_=qn[:], func=AF.Exp, scale=0.5), reads=[], writes=['qn'])
                S.op('dve', lambda e: e.scalar_tensor_tensor(out=Qaug[:, :, 96], in0=qn[:], scalar=-1.0, in1=kmaxbc[:], op0=ALU.mult, op1=ALU.mult), reads=['qn', 'kmaxbc'], writes=['Qaug'])
                S.op('pool', lambda e: e.tensor_copy(out=Kaug[:, :, 0:64], in_=Ktm1[:]), reads=['Ktm1'], writes=['Kaug'])
                for (aug, dstT) in [(Qaug, QTab), (Kaug, KTab)]:
                    pt = pstr.next()
                    for h in range(8):
                        S.op('pe', lambda e, h=h, pt=pt, aug=aug: e.transpose(out=pt[0:97, h * 128:(h + 1) * 128], in_=aug[:, h, :], identity=ident_b[:]), reads=[aug.key, 'ident_b'], writes=[pt.key])
                    S.op('act', lambda e, pt=pt, dstT=dstT, t=t: e.activation(out=dstT[0:97, :, t * 128:(t + 1) * 128], in_=pt[0:97, :].rearrange("p (h t) -> p h t", h=8), func=AF.Copy), reads=[pt.key], writes=[dstT.key])
            if i == 0:
                chk_stop('O1d')
            S.dma('pool', QTs[:, :, i * 256:(i + 1) * 256].rearrange("h r t -> r h t"), QTab[:, :, :], reads=['QTab'], writes=[('QTs', i)])
            S.dma('pool', KTos[:, :, i * 256:(i + 1) * 256].rearrange("h r t -> r h t"), KTab[:, :, :], reads=['KTab'], writes=[('KTos', i)])
            S.dma('pool', Vos[:, :, i * 130:(i + 1) * 130].rearrange("h p x -> p h x"), Vob[:].rearrange("p h t d -> p h (t d)"), reads=['Vob'], writes=[('Vos', i)])
            S.dma('pool', YMs[i].rearrange("p (c t) -> p c t", c=4), ymTb[:], reads=['ymTb'], writes=[('YMs', i)])
        S.barrier()
    FO.close()
    G1.close()

    chk_stop('O1')
    XA = ExitStack()
    xacc = sb(XA, "xacc", [128, 8, NOWN * 256])
    OY = ExitStack()
    yaT = sb(OY, "yaT", [128, 4, NOWN * 256], BF16)
    with ExitStack() as st:
        KTb = [sb(st, "KTbuf%d" % i, [128, S_LEN], BF16) for i in range(2)]
        Vb = [sb(st, "Vbuf%d" % i, [128, NT, 65], BF16) for i in range(2)]
        QTh = [sb(st, "QTh%d" % i, [128, NOWN * 256], BF16) for i in range(2)]
        KTo = [sb(st, "KToh%d" % i, [128, NOWN * 256], BF16) for i in range(2)]
        Voh = [sb(st, "Voh%d" % i, [128, NTO, 65], BF16) for i in range(2)]
        PTr = Rot([sb(st, "PT%d" % i, [128, 512], BF16) for i in range(3)])
        ya = sb(st, "ya", [128, NTO, 512], BF16)
        rden = sb(st, "rden", [128, 2])
        estg = sb(st, "estg", [128, 2048])
        for c0 in range(0, S_LEN, 2048):
            cn = min(2048, S_LEN - c0)
            S.dma('sp', estg[64:97, 0:cn], etab_d[64:97, c0:c0 + cn], writes=['estg'])
            for b2 in range(2):
                S.op('dve', lambda e, b2=b2, c0=c0, cn=cn: e.tensor_copy(out=KTb[b2][64:97, c0:c0 + cn], in_=estg[64:97, 0:cn]), reads=['estg'], writes=[KTb[b2].key + "aug"])
        for h in range(8):
            b2 = h % 2
            kt_, vb_, qt_, ko_, vo_ = KTb[b2], Vb[b2], QTh[b2], KTo[b2], Voh[b2]
            for c0 in range(0, S_LEN, 2048):
                cn = min(2048, S_LEN - c0)
                S.dma('sp', kt_[0:64, c0:c0 + cn], KTs[h * 64:(h + 1) * 64, c0:c0 + cn], reads=[('KTs', n) for n in range(c0 // 256, (c0 + cn) // 256)], writes=[kt_.key])
            S.dma('sp', vb_[:].rearrange("p t d -> p (t d)"), Vs[h], reads=[('Vs', h, b0) for b0 in range(0, NBLK, VSTB)], writes=[vb_.key])
            S.dma('sp', qt_[:, :], QTs[h], reads=[('QTs', i) for i in range(NOWN)], writes=[qt_.key])
            S.dma('sp', ko_[:, :], KTos[h], reads=[('KTos', i) for i in range(NOWN)], writes=[ko_.key])
            S.dma('sp', vo_[:].rearrange("p t d -> p (t d)"), Vos[h], reads=[('Vos', i) for i in range(NOWN)], writes=[vo_.key])
            for i in range(NOWN):
                units = [('p', n) for n in range(4 * i + 3)] + [('o', i)]
                acc = [pss[0], pss[1]]
                nmm = {0: 0, 1: 0}
                tot = {0: 2 * (4 * i + 3) + 1, 1: 2 * (4 * i + 3) + 2}

                def emit_S(u):
                    ps = psm.next()
                    for kt in range(2):
                        if u[0] == 'p':
                            lhs = kt_[0:97, (2 * u[1] + kt) * 128:(2 * u[1] + kt + 1) * 128]
                            rk = [kt_.key, kt_.key + "aug"]
                        else:
                            lhs = ko_[0:97, i * 256 + kt * 128:i * 256 + (kt + 1) * 128]
                            rk = [ko_.key]
                        S.op('pe', lambda e, ps=ps, kt=kt, lhs=lhs: e.matmul(ps[:, kt * 256:(kt + 1) * 256], lhsT=lhs, rhs=qt_[0:97, i * 256:(i + 1) * 256], start=True, stop=True), reads=rk + [qt_.key], writes=[ps.key])
                    return ps

                def emit_PV(u, ps):
                    PT = PTr.next()
                    S.op('act', lambda e: e.activation(out=PT[:], in_=ps[:, 0:512], func=AF.Exp, scale=0.125), reads=[ps.key], writes=[PT.key])
                    if u[0] == 'o':
                        S.op('pool', lambda e: e.tensor_tensor(out=PT[:, 0:128], in0=PT[:, 0:128], in1=tri_b[:], op=ALU.mult), reads=['tri_b'], writes=[PT.key])
                        S.op('pool', lambda e: e.tensor_tensor(out=PT[:, 384:512], in0=PT[:, 384:512], in1=tri_b[:], op=ALU.mult), reads=['tri_b'], writes=[PT.key])
                    for qt in range(2):
                        for kt in range(2):
                            if u[0] == 'o' and kt == 1 and qt == 0:
                                continue
                            if u[0] == 'p':
                                rhs = vb_[:, 2 * u[1] + kt, :]
                                rk = [vb_.key]
                            else:
                                rhs = vo_[:, 2 * i + kt, :]
                                rk = [vo_.key]
                            first = nmm[qt] == 0
                            nmm[qt] += 1
                            last = nmm[qt] == tot[qt]
                            S.op('pe', lambda e, qt=qt, kt=kt, rhs=rhs, first=first, last=last: e.matmul(acc[qt][:, 0:65], lhsT=PT[:, kt * 256 + qt * 128:kt * 256 + (qt + 1) * 128], rhs=rhs, start=first, stop=last),
                                 reads=[PT.key] + rk, writes=[acc[qt].key])

                prev = None
                for u in units:
                    ps = emit_S(u)
                    if prev is not None:
                        emit_PV(*prev)
                    prev = (u, ps)
                emit_PV(*prev)
                assert nmm[0] == tot[0] and nmm[1] == tot[1]
                for qt in range(2):
                    S.op('dve', lambda e, qt=qt: e.reciprocal(out=rden[:, qt:qt + 1], in_=acc[qt][:, 64:65]), reads=[acc[qt].key], writes=['rden'])
                    S.op('dve', lambda e, qt=qt: e.tensor_scalar(out=ya[:, 2 * i + qt, h * 64:(h + 1) * 64], in0=acc[qt][:, 0:64], scalar1=rden[:, qt:qt + 1], scalar2=None, op0=ALU.mult), reads=[acc[qt].key, 'rden'], writes=['ya'])
        for T in range(NTO):
            pt = pstr.next()
            for c in range(4):
                S.op('pe', lambda e, c=c, pt=pt, T=T: e.transpose(out=pt[:, c * 128:(c + 1) * 128], in_=ya[:, T, c * 128:(c + 1) * 128], identity=ident_b[:]), reads=['ya', 'ident_b'], writes=[pt.key])
            S.op('act', lambda e, pt=pt, T=T: e.activation(out=yaT[:, :, T * 128:(T + 1) * 128], in_=pt[:, 0:512].rearrange("p (c t) -> p c t", c=4), func=AF.Copy), reads=[pt.key], writes=['yaT'])
        dump("yaT", yaT[:].rearrange("p c t -> p (c t)"), [128, 4 * NOWN * 256], ['yaT'])
        S.barrier()

    chk_stop('O2')
    with ExitStack() as st:
        WG = sb(st, "WG", [128, 8, 2048], BF16)
        pm = sb(st, "pm", [128, 4, D], BF16); pa = sb(st, "pa", [128, 4, D], BF16); wo = sb(st, "wo", [128, 8, D], BF16)
        with ExitStack() as st2:
            prep_win(st2, OFF['ga'], 2048, WG, 0, 'fm', gi0=8)
            stg = Rot([sb(st2, "pstg%d" % i, [128, 4, 512]) for i in range(2)])
            for (src, dst, nk) in [(p_mlstm, pm, 4), (p_moba, pa, 4), (w_out, wo, 8)]:
                sv = src.rearrange("(k p) c -> p k c", p=128)
                for k0 in range(0, nk, 4):
                    for c0 in range(0, D, 512):
                        sg = stg.next()
                        S.dma('sp', sg[:], sv[:, k0:k0 + 4, c0:c0 + 512], writes=[sg.key])
                        S.op('act', lambda e, sg=sg, dst=dst, k0=k0, c0=c0: e.activation(out=dst[:, k0:k0 + 4, c0:c0 + 512], in_=sg[:], func=AF.Copy), reads=[sg.key], writes=[dst.key])
            S.barrier()
        xsr = Rot([sb(st, "xsb%d" % i, [128, 8, 256], BF16) for i in range(2)])
        xtr = Rot([sb(st, "xtb%d" % i, [128, 8, 260]) for i in range(2)])
        ymr = Rot([sb(st, "ymb%d" % i, [128, 4, 256], BF16) for i in range(2)])
        sgar = Rot([sb(st, "sga%d" % i, [128, 256]) for i in range(2)])
        sgbr = Rot([sb(st, "sgb%d" % i, [128, 256]) for i in range(2)])
        m1r = Rot([sb(st, "m1_%d" % i, [128, 256]) for i in range(2)])
        m2r = Rot([sb(st, "m2_%d" % i, [128, 256]) for i in range(2)])
        mgr = Rot([sb(st, "mg%d" % i, [128, 8, 256], BF16) for i in range(2)])
        xTo_v = xTo.rearrange("(k p) t -> p k t", p=128)
        for i in range(NOWN):
            xs = xsr.next(); xt = xtr.next(); mg = mgr.next(); ymb = ymr.next()
            S.dma('sp', xs[:], XSs[i].rearrange("p (k t) -> p k t", k=8), reads=[('XSs', i)], writes=[xs.key])
            S.dma('sp', ymb[:], YMs[i].rearrange("p (c t) -> p c t", c=4), reads=[('YMs', i)], writes=[ymb.key])
            S.dma('sp', xt[:], xTo_v[:, :, i * 260:(i + 1) * 260], writes=[xt.key])
            tk = slice(i * 256, (i + 1) * 256)
            for cg in range(8):
                sga = sgar.next(); sgb = sgbr.next(); m1 = m1r.next(); m2 = m2r.next()
                for (wc, gi, dst_) in [(cg * 128, 8 + cg, sga), (1024 + cg * 128, 16 + cg, sgb)]:
                    ps = psm.next()
                    for kt in range(KT):
                        S.op('pe', lambda e, kt=kt, ps=ps, wc=wc: e.matmul(ps[:, 0:256], lhsT=WG[:, kt, wc:wc + 128], rhs=xs[:, kt, :], start=(kt == 0), stop=(kt == KT - 1)), reads=['WG', xs.key], writes=[ps.key])
                    S.op('act', lambda e, ps=ps, gi=gi, dst_=dst_: e.activation(out=dst_[:], in_=ps[:, 0:256], func=AF.Sigmoid, bias=biasc[:, gi:gi + 1]), reads=[ps.key, 'biasc'], writes=[dst_.key])
                for (pw, rhs_fn, rkey, sg_, m_) in [(pm, lambda c: ymb[:, c, :], ymb.key, sga, m1), (pa, lambda c: yaT[:, c, tk], 'yaT', sgb, m2)]:
                    ps = psm.next()
                    for c in range(4):
                        S.op('pe', lambda e, c=c, ps=ps, pw=pw, rhs_fn=rhs_fn: e.matmul(ps[:, 0:256], lhsT=pw[:, c, cg * 128:(cg + 1) * 128], rhs=rhs_fn(c), start=(c == 0), stop=(c == 3)), reads=[pw.key, rkey], writes=[ps.key])
                    S.op('dve', lambda e, ps=ps, sg_=sg_, m_=m_: e.tensor_tensor(out=m_[:], in0=sg_[:], in1=ps[:, 0:256], op=ALU.mult), reads=[ps.key, sg_.key], writes=[m_.key])
                S.op('pool', lambda e, cg=cg, m1=m1, m2=m2, mg=mg: e.tensor_tensor(out=mg[:, cg, :], in0=m1[:], in1=m2[:], op=ALU.add), reads=[m1.key, m2.key], writes=[mg.key])
            for og in range(8):
                ps = psm.next()
                for cg in range(8):
                    S.op('pe', lambda e, cg=cg, ps=ps, og=og: e.matmul(ps[:, 0:256], lhsT=wo[:, cg, og * 128:(og + 1) * 128], rhs=mg[:, cg, :], start=(cg == 0), stop=(cg == 7)), reads=['wo', mg.key], writes=[ps.key])
                S.op('dve', lambda e, ps=ps, og=og: e.scalar_tensor_tensor(out=xacc[:, og, tk], in0=ps[:, 0:256], scalar=modc[:, C_G1 + og:C_G1 + og + 1], in1=xt[:, og, 4:260], op0=ALU.mult, op1=ALU.add), reads=[ps.key, 'modc_b', xt.key], writes=['xacc'])
        dump("xmid", xacc[:].rearrange("p k t -> p (k t)"), [128, 8 * NOWN * 256], ['xacc'])
        S.barrier()
    OY.close()

    chk_stop('O3')
    NTOK = NOWN * 256
    TG = 512 if NTOK % 512 == 0 else 256
    with ExitStack() as st:
        h2T = sb(st, "h2T", [128, 8, NTOK], BF16)
        sqr = Rot([sb(st, "sq%d" % i, [128, 8, 260], BF16) for i in range(1)])
        lnr = Rot([sb(st, "ln%d" % i, [128, 260]) for i in range(2)])
        rsr = Rot([sb(st, "rs%d" % i, [128, 260]) for i in range(2)])
        for i in range(NOWN):
            tk = slice(i * 256, (i + 1) * 256)
            sq = sqr.next(); lnv = lnr.next(); rs = rsr.next()
            S.op('act', lambda e, sq=sq, tk=tk: e.activation(out=sq[:, :, 0:256], in_=xacc[:, :, tk], func=AF.Square), reads=['xacc'], writes=[sq.key])
            ps = psm.next()
            for kt in range(KT):
                S.op('pe', lambda e, kt=kt, ps=ps, sq=sq: e.matmul(ps[:, 0:256], lhsT=ones_b[:], rhs=sq[:, kt, 0:256], start=(kt == 0), stop=(kt == KT - 1)), reads=['ones_b', sq.key], writes=[ps.key])
            S.op('act', lambda e, ps=ps, lnv=lnv: e.activation(out=lnv[:, 0:256], in_=ps[:, 0:256], func=AF.Ln, scale=1.0 / D, bias=eps_t[:]), reads=[ps.key, 'eps_t'], writes=[lnv.key])
            S.op('act', lambda e, lnv=lnv, rs=rs: e.activation(out=rs[:, 0:256], in_=lnv[:, 0:256], func=AF.Exp, scale=-0.5), reads=[lnv.key], writes=[rs.key])
            S.op('dve', lambda e, rs=rs, tk=tk: e.tensor_tensor(out=h2T[:, :, tk], in0=xacc[:, :, tk], in1=rs[:, 0:256].unsqueeze(1).to_broadcast([128, 8, 256]), op=ALU.mult), reads=['xacc', rs.key], writes=['h2T'])
        FG = 2
        gstg = Rot([sb(st, "gstg%d" % i, [128, 8, FG * 128]) for i in range(2)])
        ustg = Rot([sb(st, "ustg%d" % i, [128, 8, FG * 128]) for i in range(2)])
        dstg = Rot([sb(st, "dstg%d" % i, [128, FG, D]) for i in range(2)])
        gwr = Rot([sb(st, "gw%d" % i, [128, 8, FG * 128], BF16) for i in range(2)])
        uwr = Rot([sb(st, "uw%d" % i, [128, 8, FG * 128], BF16) for i in range(2)])
        dwr = Rot([sb(st, "dw%d" % i, [128, FG, D], BF16) for i in range(2)])
        bgur = Rot([sb(st, "bgu%d" % i, [128, 2 * FG]) for i in range(2)])
        sgr = Rot([sb(st, "sgl%d" % i, [128, TG]) for i in range(2)])
        fTr = Rot([sb(st, "fT%d" % i, [128, FG, TG], BF16) for i in range(2)])
        wg_v = w_gate.rearrange("(k p) c -> p k c", p=128)
        wu_v = w_up.rearrange("(k p) c -> p k c", p=128)
        wd_v = w_down.rearrange("(f p) c -> p f c", p=128)
        for fp in range(NFC // FG):
            gs = gstg.next(); us = ustg.next(); ds = dstg.next(); gw = gwr.next(); uw = uwr.next(); dw = dwr.next(); bgu = bgur.next()
            c0 = fp * FG * 128
            S.dma('sp', gs[:], wg_v[:, :, c0:c0 + FG * 128], writes=[gs.key])
            S.dma('sp', us[:], wu_v[:, :, c0:c0 + FG * 128], writes=[us.key])
            S.dma('sp', ds[:], wd_v[:, fp * FG:(fp + 1) * FG, :], writes=[ds.key])
            for kt in range(KT):
                S.op('act', lambda e, kt=kt, gs=gs, gw=gw: e.activation(out=gw[:, kt, :], in_=gs[:, kt, :], func=AF.Copy, scale=gmod2[:, kt:kt + 1]), reads=[gs.key, 'gmod2'], writes=[gw.key])
                S.op('act', lambda e, kt=kt, us=us, uw=uw: e.activation(out=uw[:, kt, :], in_=us[:, kt, :], func=AF.Copy, scale=gmod2[:, kt:kt + 1]), reads=[us.key, 'gmod2'], writes=[uw.key])
            S.op('pool', lambda e, ds=ds, dw=dw: e.tensor_copy(out=dw[:], in_=ds[:]), reads=[ds.key], writes=[dw.key])
            bp = pss[1]
            for which, sg in enumerate([gs, us]):
                for f in range(FG):
                    col = 64 + which * FG + f
                    for kt in range(KT):
                        S.op('pe', lambda e, kt=kt, sg=sg, f=f, col=col: e.matmul(bp[:, col:col + 1], lhsT=sg[:, kt, f * 128:(f + 1) * 128], rhs=modc[:, C_SH2 + kt:C_SH2 + kt + 1], start=(kt == 0), stop=(kt == KT - 1)), reads=[sg.key, 'modc_b'], writes=[bp.key])
            S.op('dve', lambda e, bgu=bgu: e.tensor_copy(out=bgu[:], in_=bp[:, 64:64 + 2 * FG]), reads=[bp.key], writes=[bgu.key])
            for tg in range(NTOK // TG):
                tk = slice(tg * TG, (tg + 1) * TG)
                fT = fTr.next()
                for f in range(FG):
                    pg = psm.next(); pu = psm.next(); sgl = sgr.next()
                    for kt in range(KT):
                        S.op('pe', lambda e, kt=kt, pg=pg, f=f: e.matmul(pg[:, 0:TG], lhsT=gw[:, kt, f * 128:(f + 1) * 128], rhs=h2T[:, kt, tk], start=(kt == 0), stop=(kt == KT - 1)), reads=[gw.key, 'h2T'], writes=[pg.key])
                    for kt in range(KT):
                        S.op('pe', lambda e, kt=kt, pu=pu, f=f: e.matmul(pu[:, 0:TG], lhsT=uw[:, kt, f * 128:(f + 1) * 128], rhs=h2T[:, kt, tk], start=(kt == 0), stop=(kt == KT - 1)), reads=[uw.key, 'h2T'], writes=[pu.key])
                    S.op('act', lambda e, pg=pg, f=f, sgl=sgl: e.activation(out=sgl[:], in_=pg[:, 0:TG], func=AF.Silu, bias=bgu[:, f:f + 1]), reads=[pg.key, bgu.key], writes=[sgl.key])
                    S.op('dve', lambda e, pu=pu, f=f, sgl=sgl, fT=fT: e.scalar_tensor_tensor(out=fT[:, f, :], in0=pu[:, 0:TG], scalar=bgu[:, FG + f:FG + f + 1], in1=sgl[:], op0=ALU.add, op1=ALU.mult), reads=[pu.key, bgu.key, sgl.key], writes=[fT.key])
                for og in range(8):
                    ps = psm.next()
                    for f in range(FG):
                        S.op('pe', lambda e, f=f, ps=ps, og=og: e.matmul(ps[:, 0:TG], lhsT=dw[:, f, og * 128:(og + 1) * 128], rhs=fT[:, f, :], start=(f == 0), stop=(f == FG - 1)), reads=[dw.key, fT.key], writes=[ps.key])
                    S.op('dve', lambda e, ps=ps, og=og: e.scalar_tensor_tensor(out=xacc[:, og, tk], in0=ps[:, 0:TG], scalar=modc[:, C_G2 + og:C_G2 + og + 1], in1=xacc[:, og, tk], op0=ALU.mult, op1=ALU.add), reads=[ps.key, 'modc_b'], writes=['xacc'])
        otr = Rot([sb(st, "ot%d" % i, [128, 8, 256]) for i in range(1)])
        outT_v = outT.rearrange("(k p) t -> p k t", p=128)
        out_toks = []
        for i in range(NOWN):
            tk = slice(i * 256, (i + 1) * 256)
            sq = sqr.next(); lnv = lnr.next(); rs = rsr.next(); ot = otr.next()
            S.op('act', lambda e, sq=sq, tk=tk: e.activation(out=sq[:, :, 0:256], in_=xacc[:, :, tk], func=AF.Square), reads=['xacc'], writes=[sq.key])
            ps = psm.next()
            for kt in range(KT):
                S.op('pe', lambda e, kt=kt, ps=ps, sq=sq: e.matmul(ps[:, 0:256], lhsT=ones_b[:], rhs=sq[:, kt, 0:256], start=(kt == 0), stop=(kt == KT - 1)), reads=['ones_b', sq.key], writes=[ps.key])
            S.op('act', lambda e, ps=ps, lnv=lnv: e.activation(out=lnv[:, 0:256], in_=ps[:, 0:256], func=AF.Ln, scale=1.0 / D, bias=eps_t[:]), reads=[ps.key, 'eps_t'], writes=[lnv.key])
            S.op('act', lambda e, lnv=lnv, rs=rs: e.activation(out=rs[:, 0:256], in_=lnv[:, 0:256], func=AF.Exp, scale=-0.5), reads=[lnv.key], writes=[rs.key])
            for kt in range(KT):
                S.op('dve', lambda e, kt=kt, rs=rs, ot=ot, tk=tk: e.scalar_tensor_tensor(out=ot[:, kt, :], in0=xacc[:, kt, tk], scalar=nfg[:, kt:kt + 1], in1=rs[:, 0:256], op0=ALU.mult, op1=ALU.mult), reads=['xacc', 'nfg_t', rs.key], writes=[ot.key])
            out_toks.append(S.dma('sp', outT_v[:, :, tk], ot[:], reads=[ot.key], writes=[('outT', i)]))
        S.barrier()
    XA.close()
    G.close()
    return nc, dbg_outs, S


_CONST_CACHE = {}


def _prep_inputs(S_LEN, inp):
    NBLK = S_LEN // 256
    NOWN = NBLK // 4
    NT = S_LEN // 128
    f32 = np.float32
    x = np.asarray(inp['x'], f32); c = np.asarray(inp['c'], f32); pos = np.asarray(inp['positions'], np.int32)
    perm = np.concatenate([np.arange(512, 1024), np.arange(1024, 1536), np.arange(2568, 3080), np.arange(3080, 3592),
                           np.arange(2048, 2056), np.arange(0, 512), np.arange(1536, 2048), np.arange(2056, 2568),
                           np.arange(3592, 4616), np.arange(4616, 5640)])
    w_in_p = np.ascontiguousarray(np.asarray(inp['w_in'], f32)[0][:, perm])
    b_in_p = np.asarray(inp['b_in'], f32)[0][perm]
    fm_offs = [OFF['mk'] + 128 * h for h in range(4)] + [OFF['mq'] + 128 * h for h in range(4)] + [OFF['ga'] + 128 * g for g in range(8)] + [OFF['gb'] + 128 * g for g in range(8)]
    b_in_col = np.stack([b_in_p[o:o + 128] for o in fm_offs], axis=1).astype(f32)

    def colT(v):
        return np.ascontiguousarray(np.asarray(v, f32).reshape(8, 128).T)

    conv_w = np.asarray(inp['conv_w'], f32)[0]
    cw = np.zeros((128, 32), f32)
    for g in range(8):
        for j in range(4):
            cw[:, g * 4 + j] = conv_w[j, g * 128:(g + 1) * 128]
    cbv = np.ascontiguousarray(np.asarray(inp['conv_b'], f32)[0].reshape(8, 128).T)
    half = 32
    inv_freq = (10000.0 ** (-np.arange(half, dtype=np.float64) / half)) / (2 * np.pi)
    common = dict(
        ada_w=np.ascontiguousarray(np.asarray(inp['ada_w'], f32)[0]),
        ada_b_c=np.ascontiguousarray(np.asarray(inp['ada_b'], f32)[0].reshape(48, 128).T),
        n1g=colT(inp['norm1_g'][0]), n2g=colT(inp['norm2_g'][0]), nfg=colT(inp['normf_g']),
        w_in_p=w_in_p, b_in_row=np.ascontiguousarray(b_in_p[None, :]), b_in_col=np.ascontiguousarray(b_in_col),
        cw=cw, cb=cbv, mng_bc=np.ascontiguousarray(np.broadcast_to(np.asarray(inp['m_norm_g'], f32)[0][None, :], (128, 512))),
        p_mlstm=np.ascontiguousarray(np.asarray(inp['p_mlstm'], f32)[0]), p_moba=np.ascontiguousarray(np.asarray(inp['p_moba'], f32)[0]),
        w_out=np.ascontiguousarray(np.asarray(inp['w_out'], f32)[0]), w_gate=np.ascontiguousarray(np.asarray(inp['w_gate'], f32)[0]),
        w_up=np.ascontiguousarray(np.asarray(inp['w_up'], f32)[0]), w_down=np.ascontiguousarray(np.asarray(inp['w_down'], f32)[0]),
        ident=np.eye(128, dtype=f32), tri=np.triu(np.ones((128, 128), f32)), ones=np.ones((128, 128), f32),
        invf=np.ascontiguousarray(np.broadcast_to(inv_freq.astype(f32)[None, :], (128, 32))),
    )
    etab = np.zeros((128, S_LEN), f32)
    for n in range(NBLK):
        etab[64 + n, n * 256:(n + 1) * 256] = 1.0
    etab[96, :] = 1.0
    in_maps = []
    for core in range(8):
        b, j = core // 4, core % 4
        m = dict(common)
        m['etab'] = etab
        m['xTf'] = np.ascontiguousarray(x[b].T)
        xo = np.zeros((D, NOWN, 260), f32)
        po = np.zeros((128, NOWN * 2), np.int32)
        pastbias = np.zeros((128, NOWN, 32), f32)
        hvv = np.ones((128, NOWN), f32)
        for i in range(NOWN):
            g = 4 * i + j
            s0 = g * 256
            xo[:, i, 4:260] = x[b, s0:s0 + 256].T
            if s0 > 0:
                xo[:, i, 0:4] = x[b, s0 - 4:s0].T
            else:
                hvv[:, i] = 0.0
            for t in range(2):
                po[:, i * 2 + t] = pos[b, s0 + t * 128:s0 + (t + 1) * 128]
            pastbias[:, i, g:] = -1e30
        m['xTo'] = np.ascontiguousarray(xo.reshape(D, NOWN * 260))
        m['cT'] = colT(c[b])
        m['posf'] = np.ascontiguousarray(pos[b].reshape(NT, 128).T)
        m['poso'] = po
        m['pastbias'] = np.ascontiguousarray(pastbias.reshape(128, NOWN * 32))
        m['hv'] = hvv
        cs = np.zeros((128, 4), f32); cs[:, j] = 1.0
        m['capsel'] = cs
        in_maps.append(m)
    return in_maps


def run(inp, S_LEN, dbg=None, stop=None):
    NBLK = S_LEN // 256
    NOWN = NBLK // 4
    nc, dbg_outs, S = build_program(S_LEN, dbg, stop)
    in_maps = _prep_inputs(S_LEN, inp)
    res = run_bass_kernel_spmd(nc, in_maps, core_ids=list(range(8)))
    B = 2
    out = np.zeros((B, S_LEN, D), np.float32)
    for core in range(8):
        b, j = core // 4, core % 4
        oT = np.asarray(res.results[core]["outT"])
        for i in range(NOWN):
            g = 4 * i + j
            out[b, g * 256:(g + 1) * 256, :] = oT[:, i * 256:(i + 1) * 256].T
    dbgres = {}
    for name in dbg_outs:
        dbgres[name] = [np.asarray(res.results[core]["dbg_" + name]) for core in range(8)]
    return out, dbgres


def kernel(**inputs):
    S_LEN = int(np.asarray(inputs['x']).shape[1])
    out, _ = run(inputs, S_LEN)
    return out
```

```python
import math
from contextlib import ExitStack
import numpy as np
import concourse.bass as bass
import concourse.mybir as mybir
from concourse.bass_utils import run_bass_kernel_spmd

F32 = mybir.dt.float32
BF16 = mybir.dt.bfloat16
I32 = mybir.dt.int32
AF = mybir.ActivationFunctionType
ALU = mybir.AluOpType
AX = mybir.AxisListType

D = 1024
KT = 8
DFF = 2816
NFC = 22
DIN = 5640
OFF = dict(mk=0, mv=512, ak=1024, av=1536, mif=2048, mq=2056, mo=2568, aq=3080, ga=3592, gb=4616)
NF = 2056
EPS = 1e-6
NEG = -30000.0


import os as _os
LAT_PE = float(_os.environ.get('KLATPE', '250'))
LAT_X = float(_os.environ.get('KLATX', '300'))


class _Proxy:
    def __init__(self):
        self.call = None

    def __getattr__(self, name):
        def f(*a, **k):
            self.call = (name, a, k)
            return self
        return f


def _free_size(ap):
    try:
        sh = list(ap.shape)
        n = 1
        for v in sh[1:]:
            n *= int(v)
        return n
    except Exception:
        return 256


class Sched:
    def __init__(self, nc, n_dma_sems=8):
        self.nc = nc
        self.eng = {'pe': nc.tensor, 'act': nc.scalar, 'dve': nc.vector, 'pool': nc.gpsimd, 'sp': nc.sync}
        self.csem = {e: nc.alloc_semaphore("c_" + e) for e in ['pe', 'act', 'dve', 'pool']}
        self.ccnt = {e: 0 for e in self.csem}
        self.P = n_dma_sems
        self.dsem = {q: [nc.alloc_semaphore("d_%s%d" % (q, i)) for i in range(n_dma_sems)] for q in ['sp', 'pool']}
        self.dcnt = {q: 0 for q in self.dsem}
        self.known = {e: {} for e in self.eng}
        self.sems = {}
        for e in self.csem:
            self.sems["c_" + e] = self.csem[e]
        for q in self.dsem:
            for i in range(n_dma_sems):
                self.sems["d_%s%d" % (q, i)] = self.dsem[q][i]
        self.lastw = {}
        self.readers = {}
        self.nwait = 0
        self.nop = 0
        self.dead = False
        self.rec = []
        self.alias = {}
        import os
        self.reorder = os.environ.get('KREORDER', '1') == '1'

    def _res(self, keys):
        return [self.alias.get(k, k) if isinstance(k, str) else k for k in keys]

    def op(self, e, fn, reads=(), writes=(), n=None):
        if self.dead:
            return None
        px = _Proxy()
        fn(px)
        name, a, k = px.call
        if n is None:
            if name == 'matmul':
                n = _free_size(k.get('rhs', a[2] if len(a) > 2 else None))
            elif name == 'transpose':
                n = 128
            else:
                o = k.get('out', a[0] if a else None)
                n = _free_size(o)
        if e == 'pe':
            fp32 = False
            try:
                fp32 = (name == 'matmul' and k['rhs'].dtype == F32)
            except Exception:
                pass
            cost = (max(64, n) / 2.2 + 35) * (4 if fp32 else 1)
            lat = LAT_PE
        elif e == 'act':
            cost = 230 + n / 1.3
            lat = LAT_X
        elif e == 'dve':
            cost = 200 + n / 0.9
            lat = LAT_X
        else:
            cost = 350 + n / 0.55
            lat = LAT_X
        self.rec.append(dict(kind='op', e=e, call=(name, a, k), reads=self._res(reads), writes=self._res(writes), cost=cost, lat=lat))
        return None

    def dma(self, q, out, in_, reads=(), writes=()):
        if self.dead:
            return None
        self.rec.append(dict(kind='dma', e=q, call=(out, in_), reads=self._res(reads), writes=self._res(writes), cost=(120 if q == 'sp' else 700), lat=3000))
        return None

    def flush(self):
        rec = self.rec
        self.rec = []
        N = len(rec)
        if N == 0:
            return
        order = list(range(N))
        if self.reorder:
            lastw, readers = {}, {}
            preds = [set() for _ in range(N)]
            for i, r in enumerate(rec):
                for k in r['reads']:
                    if k in lastw:
                        preds[i].add(lastw[k])
                for k in r['writes']:
                    if k in lastw:
                        preds[i].add(lastw[k])
                    for j in readers.get(k, ()):
                        preds[i].add(j)
                preds[i].discard(i)
                for k in r['reads']:
                    if k not in r['writes']:
                        readers.setdefault(k, []).append(i)
                for k in r['writes']:
                    lastw[k] = i
                    readers[k] = []
            succ = [[] for _ in range(N)]
            indeg = [0] * N
            for i in range(N):
                indeg[i] = len(preds[i])
                for p in preds[i]:
                    succ[p].append(i)
            import heapq
            efree = {}
            fin = [0.0] * N
            rdy_t = [0.0] * N
            ready = {}
            for i in range(N):
                if indeg[i] == 0:
                    heapq.heappush(ready.setdefault(rec[i]['e'], []), (0.0, i))
            order = []
            WIN = 6
            while len(order) < N:
                best = None
                for e, hp in ready.items():
                    if not hp:
                        continue
                    cands = heapq.nsmallest(WIN, hp)
                    ef = efree.get(e, 0.0)
                    for (rt_, i) in cands:
                        st_ = max(ef, rt_)
                        key = (st_, i)
                        if best is None or key < best[0]:
                            best = (key, e, (rt_, i))
                (st_, i), e, item = best
                ready[e].remove(item)
                heapq.heapify(ready[e])
                r = rec[i]
                efree[e] = st_ + r['cost']
                fin[i] = st_ + r['cost'] + r['lat']
                order.append(i)
                for s_ in succ[i]:
                    indeg[s_] -= 1
                    same = (rec[s_]['e'] == e and e == 'pe')
                    t_ = (st_ + r['cost']) if same else fin[i]
                    if t_ > rdy_t[s_]:
                        rdy_t[s_] = t_
                    if indeg[s_] == 0:
                        heapq.heappush(ready.setdefault(rec[s_]['e'], []), (rdy_t[s_], s_))
        for i in order:
            r = rec[i]
            if r['kind'] == 'op':
                self._emit_op(r['e'], r['call'], r['reads'], r['writes'])
            else:
                self._emit_dma(r['e'], r['call'][0], r['call'][1], r['reads'], r['writes'])

    def _wait(self, e, toks):
        need = {}
        for (sname, val, prod) in toks:
            if e == 'pe' and prod == 'pe':
                continue
            if self.known[e].get(sname, 0) >= val:
                continue
            if need.get(sname, 0) < val:
                need[sname] = val
        for sname, val in need.items():
            self.eng[e].wait_ge(self.sems[sname], val)
            self.known[e][sname] = val
            self.nwait += 1

    def _deps(self, reads, writes):
        toks = []
        for k in reads:
            t = self.lastw.get(k)
            if t is not None:
                toks.append(t)
        for k in writes:
            t = self.lastw.get(k)
            if t is not None:
                toks.append(t)
            toks += self.readers.get(k, [])
        return toks

    def _commit(self, tok, reads, writes):
        for k in reads:
            if k not in writes:
                self.readers.setdefault(k, []).append(tok)
        for k in writes:
            self.lastw[k] = tok
            self.readers[k] = []

    def _emit_op(self, e, call, reads, writes):
        self._wait(e, self._deps(reads, writes))
        name, a, k = call
        inst = getattr(self.eng[e], name)(*a, **k)
        self.ccnt[e] += 1
        inst.then_inc(self.csem[e], 1)
        tok = ("c_" + e, self.ccnt[e], e)
        self._commit(tok, reads, writes)
        self.nop += 1

    def _emit_dma(self, q, out, in_, reads, writes):
        n = self.dcnt[q]
        s = self.dsem[q][n % self.P]
        sname = "d_%s%d" % (q, n % self.P)
        toks = self._deps(reads, writes)
        if n >= self.P:
            toks.append((sname, 16 * (n // self.P), 'dma'))
        self._wait(q, toks)
        self.eng[q].dma_start(out=out, in_=in_).then_inc(s, 16)
        self.dcnt[q] += 1
        tok = (sname, 16 * (n // self.P + 1), 'dma')
        self._commit(tok, reads, writes)
        self.nop += 1

    def all_tokens(self):
        toks = [("c_" + e, self.ccnt[e], e) for e in self.csem if self.ccnt[e] > 0]
        for q in self.dsem:
            for idx in range(self.P):
                if self.dcnt[q] > idx:
                    uses = (self.dcnt[q] - idx + self.P - 1) // self.P
                    toks.append(("d_%s%d" % (q, idx), 16 * uses, 'dma'))
        return toks

    def barrier(self, engines=None):
        if self.dead:
            return
        self.flush()
        toks = self.all_tokens()
        for e in (engines or list(self.eng.keys())):
            self._wait(e, toks)
        self.lastw = {}
        self.readers = {}


class TL:
    def __init__(self, t, key):
        self.t = t
        self.key = key

    def __getitem__(self, i):
        return self.t[i]


class RT:
    def __init__(self, S, name, tiles):
        self.S = S
        self.name = name
        self.tiles = tiles
        self.i = 0
        for k, t in enumerate(tiles):
            t.key = "%s#%d" % (name, k)
        S.alias[name] = tiles[0].key

    def rotate(self):
        self.i += 1
        self.S.alias[self.name] = self.tiles[self.i % len(self.tiles)].key

    @property
    def key(self):
        return self.tiles[self.i % len(self.tiles)].key

    def __getitem__(self, i):
        return self.tiles[self.i % len(self.tiles)].t[i]


class Rot:
    def __init__(self, tiles):
        self.tiles = tiles
        self.i = 0

    def next(self):
        t = self.tiles[self.i % len(self.tiles)]
        self.i += 1
        return t


def build_program(S_LEN, dbg=None, stop=None):
    NBLK = S_LEN // 256
    NOWN = NBLK // 4
    NT = S_LEN // 128
    NTO = NOWN * 2
    nc = bass.Bass("TRN2", target_bir_lowering=False)
    S = Sched(nc)
    dbg = dbg or []
    dbg_outs = {}

    def din(name, shape, dt=F32):
        return nc.dram_tensor(name, shape, dt, kind="ExternalInput").ap()

    xTf = din("xTf", [D, S_LEN])
    xTo = din("xTo", [D, NOWN * 260])
    cT_d = din("cT", [128, 8])
    posf_d = din("posf", [128, NT], I32)
    poso_d = din("poso", [128, NTO], I32)
    ada_w = din("ada_w", [D, 6 * D])
    ada_b_c = din("ada_b_c", [128, 48])
    n1g_d = din("n1g", [128, 8])
    n2g_d = din("n2g", [128, 8])
    nfg_d = din("nfg", [128, 8])
    w_in_p = din("w_in_p", [D, DIN])
    b_in_row = din("b_in_row", [1, DIN])
    b_in_col = din("b_in_col", [128, 24])
    cw_d = din("cw", [128, 32])
    cb_d = din("cb", [128, 8])
    mng_d = din("mng_bc", [128, 512])
    p_mlstm = din("p_mlstm", [512, D])
    p_moba = din("p_moba", [512, D])
    w_out = din("w_out", [D, D])
    w_gate = din("w_gate", [D, DFF])
    w_up = din("w_up", [D, DFF])
    w_down = din("w_down", [DFF, D])
    ident_d = din("ident", [128, 128])
    tri_d = din("tri", [128, 128])
    ones_d = din("ones", [128, 128])
    invf_d = din("invf", [128, 32])
    pastb_d = din("pastbias", [128, NOWN * 32])
    hv_d = din("hv", [128, NOWN])
    capsel_d = din("capsel", [128, 4])
    etab_d = din("etab", [128, S_LEN])
    outT = nc.dram_tensor("outT", [D, NOWN * 256], F32, kind="ExternalOutput").ap()

    KTs = nc.dram_tensor("KTs", [512, S_LEN], BF16, kind="Internal").ap()
    Vs = nc.dram_tensor("Vs", [8, 128, NT * 65], BF16, kind="Internal").ap()
    QTs = nc.dram_tensor("QTs", [8, 128, NOWN * 256], BF16, kind="Internal").ap()
    KTos = nc.dram_tensor("KTos", [8, 128, NOWN * 256], BF16, kind="Internal").ap()
    XSs = nc.dram_tensor("XSs", [NOWN, 128, 8 * 256], BF16, kind="Internal").ap()
    CCs = nc.dram_tensor("CCs", [NOWN, 128, 4 * 129], F32, kind="Internal").ap()
    Vos = nc.dram_tensor("Vos", [8, 128, NTO * 65], BF16, kind="Internal").ap()
    YMs = nc.dram_tensor("YMs", [NOWN, 128, 4 * 256], BF16, kind="Internal").ap()

    used_names = {}

    def uniq(name):
        k = used_names.get(name, 0)
        used_names[name] = k + 1
        return name if k == 0 else "%s_r%d" % (name, k)

    def sb(st, name, shape, dt=F32):
        return TL(st.enter_context(nc.sbuf_tensor(uniq(name), shape, dt)), name)

    def rt(st, name, shape, dt=F32, k=2):
        import os
        if os.environ.get('KRT', '1') == '0':
            k = 1
        return RT(S, name, [sb(st, "%s_b%d" % (name, i), shape, dt) for i in range(k)])

    def pst(st, name, shape, dt=F32):
        return TL(st.enter_context(nc.psum_tensor(uniq(name), shape, dt)), name)

    def dump(name, ap, shape, keys, is_bf16=False):
        if name not in dbg:
            return
        o = nc.dram_tensor("dbg_" + name, shape, F32, kind="ExternalOutput").ap()
        dbg_outs[name] = shape
        if S.dead:
            return
        with nc.sbuf_tensor("dbgt_" + name, shape, F32) as tmp:
            S.op('dve', lambda e: e.tensor_copy(out=tmp[:], in_=ap), reads=keys, writes=['dbgt_' + name])
            S.dma('sp', o, tmp[:], reads=['dbgt_' + name], writes=['dbgo_' + name])
            S.barrier()

    def chk_stop(tag):
        if stop == tag:
            S.barrier()
            S.dead = True

    G = ExitStack()
    NPSM = int(_os.environ.get('KNPSM', '5'))
    psm_t = [pst(G, "psm%d" % i, [128, 512]) for i in range(NPSM)]
    pstr = Rot([pst(G, "pstr%d" % i, [128, 1024], BF16) for i in range(6 - NPSM)])
    pss = [pst(G, "pss%d" % i, [128, 512]) for i in range(2)]
    if _os.environ.get('KPSB', '1') == '1':
        psm = Rot(psm_t[0:4])
        psb = Rot([psm_t[4], pss[0]])
    else:
        psm = Rot(psm_t + [pss[0]])
        psb = psm
    psmo = Rot(psm_t)

    ident_f = sb(G, "ident_f", [128, 128]); tri_f = sb(G, "tri_f", [128, 128]); ones_f = sb(G, "ones_f", [128, 128])
    ident_b = sb(G, "ident_b", [128, 128], BF16); tri_b = sb(G, "tri_b", [128, 128], BF16); ones_b = sb(G, "ones_b", [128, 128], BF16)
    for (d_, f_, b_) in [(ident_d, ident_f, ident_b), (tri_d, tri_f, tri_b), (ones_d, ones_f, ones_b)]:
        S.dma('sp', f_[:], d_, writes=[f_.key])
        S.op('dve', lambda e, f_=f_, b_=b_: e.tensor_copy(out=b_[:], in_=f_[:]), reads=[f_.key], writes=[b_.key])
    eps_t = sb(G, "eps_t", [128, 1]); one_t = sb(G, "one_t", [128, 1])
    S.op('pool', lambda e: e.memset(eps_t[:], EPS), writes=['eps_t'])
    S.op('pool', lambda e: e.memset(one_t[:], 1.0), writes=['one_t'])

    def load_const(name, src, shape, dt=F32):
        t = sb(G, name, shape, dt)
        S.dma('sp', t[:], src, writes=[name])
        return t

    c_t = load_const("c_t", cT_d, [128, 8])
    adab = load_const("adab", ada_b_c, [128, 48])
    n1g = load_const("n1g_t", n1g_d, [128, 8]); n2g = load_const("n2g_t", n2g_d, [128, 8]); nfg = load_const("nfg_t", nfg_d, [128, 8])
    bincol = load_const("bincol", b_in_col, [128, 24])
    biasc = sb(G, "biasc", [128, 24])
    kmaxbc = sb(G, "kmaxbc", [128, 8])
    modc = sb(G, "modc", [128, 48])
    gmod1 = sb(G, "gmod1", [128, 8]); gmod2 = sb(G, "gmod2", [128, 8])
    sc = sb(G, "sc", [128, 8])
    G1 = ExitStack()
    G_save = G
    G = G1
    cw = load_const("cw_t", cw_d, [128, 32]); cb = load_const("cb_t", cb_d, [128, 8])
    mng = load_const("mng_t", mng_d, [128, 512])
    invf = load_const("invf_t", invf_d, [128, 32])
    pastb = load_const("pastb_t", pastb_d, [128, NOWN * 32])
    hv = load_const("hv_t", hv_d, [128, NOWN])
    capsel = load_const("capsel_t", capsel_d, [128, 4])
    sh1bc = sb(G1, "sh1bc", [128, 8, 128])
    bias_bc = sb(G1, "bias_bc", [128, 2568])
    kmT = sb(G1, "kmT", [128, 4, 2, 32], BF16)
    dgw = sb(G1, "dgw", [128, 8, 4, 128], BF16)
    ncb = sb(G1, "ncb", [128, 8]); lnq_t = sb(G1, "lnq_t", [128, 1]); zero_t = sb(G1, "zero_t", [128, 1])
    G = G_save

    for g_ in range(8):
        for j_ in range(4):
            S.op('act', lambda e, g_=g_, j_=j_: e.activation(out=dgw[:, g_, j_, :], in_=ident_f[:], func=AF.Copy, scale=cw[:, g_ * 4 + j_:g_ * 4 + j_ + 1]), reads=['ident_f', 'cw_t'], writes=['dgw'])
    S.op('dve', lambda e: e.tensor_scalar(out=ncb[:], in0=cb[:], scalar1=-1.0, scalar2=None, op0=ALU.mult), reads=['cb_t'], writes=['ncb'])
    S.op('pool', lambda e: e.memset(lnq_t[:], math.log(128.0 ** -0.5)), writes=['lnq_t'])
    S.op('pool', lambda e: e.memset(zero_t[:], 0.0), writes=['zero_t'])
    S.op('act', lambda e: e.activation(out=sc[:], in_=c_t[:], func=AF.Silu), reads=['c_t'], writes=['sc'])
    adaw_v = ada_w.rearrange("(k p) c -> p k c", p=128)

    def mod_groups(awr, g0, g1, key):
        for g in range(g0, g1):
            aw = awr.next()
            ps = psb.next()
            S.dma('sp', aw[:], adaw_v[:, :, g * 128:(g + 1) * 128], writes=[aw.key])
            for kt in range(KT):
                S.op('pe', lambda e, aw=aw, kt=kt, ps=ps: e.matmul(ps[:, 0:1], lhsT=aw[:, kt, :], rhs=sc[:, kt:kt + 1], start=(kt == 0), stop=(kt == KT - 1)),
                     reads=[aw.key, 'sc'], writes=[ps.key])
            S.op('dve', lambda e, g=g, ps=ps: e.tensor_tensor(out=modc[:, g:g + 1], in0=ps[:, 0:1], in1=adab[:, g:g + 1], op=ALU.add), reads=[ps.key, 'adab'], writes=[key])

    with ExitStack() as st:
        awr = Rot([sb(st, "aw%d" % i, [128, 8, 128]) for i in range(3)])
        mod_groups(awr, 0, 16, 'modc')
        S.op('dve', lambda e: e.scalar_tensor_tensor(out=gmod1[:], in0=modc[:, 8:16], scalar=1.0, in1=n1g[:], op0=ALU.add, op1=ALU.mult), reads=['modc', 'n1g_t'], writes=['gmod1'])
        S.barrier()
    C_SH1, C_G1, C_SH2, C_G2 = 0, 16, 24, 40
    dump("modc", modc[:], [128, 48], ['modc'])
    chk_stop('S')

    for kt in range(KT):
        S.op('act', lambda e, kt=kt: e.activation(out=sh1bc[:, kt, :], in_=ones_f[:], func=AF.Copy, scale=modc[:, C_SH1 + kt:C_SH1 + kt + 1]), reads=['ones_f', 'modc'], writes=['sh1bc'])

    def bbc(c):
        return c - 512 if c < NF else 1544 + (c - OFF['mo'])

    def prep_win(st, c0, n, dst, d0, kind, gi0=None):
        stg = Rot([sb(st, "wstg%d" % i, [128, 8, 256]) for i in range(2)])
        brow = Rot([sb(st, "brow%d" % i, [1, 256]) for i in range(2)])
        wv = w_in_p.rearrange("(k p) c -> p k c", p=128)
        for p0 in range(0, n, 256):
            pn = min(256, n - p0)
            sg = stg.next()
            S.dma('sp', sg[:, :, 0:pn], wv[:, :, c0 + p0:c0 + p0 + pn], writes=[sg.key])
            for kt in range(KT):
                S.op('act', lambda e, sg=sg, kt=kt, p0=p0, pn=pn: e.activation(out=dst[:, kt, d0 + p0:d0 + p0 + pn], in_=sg[:, kt, 0:pn], func=AF.Copy, scale=gmod1[:, kt:kt + 1]),
                     reads=[sg.key, 'gmod1'], writes=[dst.key])
            if kind == 'tm':
                ps = psm.next()
                br = brow.next()
                S.dma('sp', br[0:1, 0:pn], b_in_row[0:1, c0 + p0:c0 + p0 + pn], writes=[br.key])
                for kt in range(KT):
                    S.op('pe', lambda e, ps=ps, sg=sg, kt=kt, pn=pn: e.matmul(ps[:, 0:pn], lhsT=sh1bc[:, kt, :], rhs=sg[:, kt, 0:pn], start=(kt == 0), stop=False),
                         reads=['sh1bc', sg.key], writes=[ps.key])
                S.op('pe', lambda e, ps=ps, br=br, pn=pn: e.matmul(ps[:, 0:pn], lhsT=ones_f[0:1, :], rhs=br[0:1, 0:pn], start=False, stop=True),
                     reads=['ones_f', br.key], writes=[ps.key])
                b0 = bbc(c0 + p0)
                S.op('dve', lambda e, ps=ps, b0=b0, pn=pn: e.tensor_copy(out=bias_bc[:, b0:b0 + pn], in_=ps[:, 0:pn]), reads=[ps.key], writes=['bias_bc'])
            else:
                for gg in range(pn // 128):
                    gi = gi0 + (p0 // 128) + gg
                    ps = pss[1]
                    for kt in range(KT):
                        S.op('pe', lambda e, sg=sg, kt=kt, gg=gg, gi=gi: e.matmul(pss[1][:, gi:gi + 1], lhsT=sg[:, kt, gg * 128:(gg + 1) * 128], rhs=modc[:, C_SH1 + kt:C_SH1 + kt + 1], start=(kt == 0), stop=(kt == KT - 1)),
                             reads=[sg.key, 'modc'], writes=[ps.key])
                    S.op('dve', lambda e, gi=gi: e.tensor_tensor(out=biasc[:, gi:gi + 1], in0=pss[1][:, gi:gi + 1], in1=bincol[:, gi:gi + 1], op=ALU.add), reads=[pss[1].key, 'bincol'], writes=['biasc'])

    def front(xt, N, xs, sqr, lnr, rsr):
        sq = sqr.next(); lnv = lnr.next(); rs = rsr.next()
        S.op('act', lambda e: e.activation(out=sq[:, :, 0:N], in_=xt[:, :, 0:N], func=AF.Square), reads=[xt.key], writes=[sq.key])
        ps = psm.next()
        for kt in range(KT):
            S.op('pe', lambda e, kt=kt: e.matmul(ps[:, 0:N], lhsT=ones_b[:], rhs=sq[:, kt, 0:N], start=(kt == 0), stop=(kt == KT - 1)), reads=['ones_b', sq.key], writes=[ps.key])
        S.op('act', lambda e: e.activation(out=lnv[:, 0:N], in_=ps[:, 0:N], func=AF.Ln, scale=1.0 / D, bias=eps_t[:]), reads=[ps.key, 'eps_t'], writes=[lnv.key])
        S.op('act', lambda e: e.activation(out=rs[:, 0:N], in_=lnv[:, 0:N], func=AF.Exp, scale=-0.5), reads=[lnv.key], writes=[rs.key])
        S.op('dve', lambda e: e.tensor_tensor(out=xs[:, 0:4, 0:N], in0=xt[:, 0:4, 0:N], in1=rs[:, 0:N].unsqueeze(1).to_broadcast([128, 4, N]), op=ALU.mult), reads=[xt.key, rs.key], writes=[xs.key + "a"])
        for kt in range(4, 8):
            S.op('pool', lambda e, kt=kt: e.tensor_tensor(out=xs[:, kt, 0:N], in0=xt[:, kt, 0:N], in1=rs[:, 0:N], op=ALU.mult), reads=[xt.key, rs.key], writes=[xs.key + "b"])
        return rs

    def xskeys(xs):
        return [xs.key + "a", xs.key + "b"]

    def wsl(c0, n, WF, WO):
        if c0 < NF:
            return WF, c0
        return WO, c0 - NF

    def proj_fm(ps, W, wc, xs, x0, N):
        for kt in range(KT):
            S.op('pe', lambda e, kt=kt: e.matmul(ps[:, 0:N], lhsT=W[:, kt, wc:wc + 128], rhs=xs[:, kt, x0:x0 + N], start=(kt == 0), stop=(kt == KT - 1)),
                 reads=[W.key] + xskeys(xs), writes=[ps.key])

    def proj_tm(psap, pskey, W, wc, ncol, xs, x0):
        for kt in range(KT):
            S.op('pe', lambda e, kt=kt: e.matmul(psap, lhsT=xs[:, kt, x0:x0 + 128], rhs=W[:, kt, wc:wc + ncol], start=(kt == 0), stop=(kt == KT - 1)),
                 reads=[W.key] + xskeys(xs), writes=[pskey])

    def rope(kx, out3, okey, cos2, sinp, sinn, T, tA, tB):
        c2 = cos2[:, T, :].unsqueeze(1).to_broadcast([128, 8, 64])
        sp_ = sinp[:, T, :].unsqueeze(1).to_broadcast([128, 8, 32])
        sn_ = sinn[:, T, :].unsqueeze(1).to_broadcast([128, 8, 32])
        S.op('dve', lambda e: e.tensor_tensor(out=tA[:], in0=kx[:], in1=c2, op=ALU.mult), reads=[kx.key, cos2.key], writes=[tA.key])
        S.op('dve', lambda e: e.tensor_tensor(out=tB[:, :, 0:32], in0=kx[:, :, 32:64], in1=sn_, op=ALU.mult), reads=[kx.key, sinn.key], writes=[tB.key + "l"])
        S.op('dve', lambda e: e.tensor_tensor(out=tB[:, :, 32:64], in0=kx[:, :, 0:32], in1=sp_, op=ALU.mult), reads=[kx.key, sinp.key], writes=[tB.key + "h"])
        S.op('dve', lambda e: e.tensor_tensor(out=out3, in0=tA[:], in1=tB[:], op=ALU.add), reads=[tA.key, tB.key + "l", tB.key + "h"], writes=[okey])

    gcnt = [0]

    def gates(gif, t, gt):
        l, u, w, dec, tmp = gt['l'], gt['u'], gt['w'], gt['dec'], gt['tmp']
        r_ = gcnt[0] % 4
        gcnt[0] += 1
        gp = TL(pss[1].t[:, 64 * r_:64 * r_ + 64], "pss1")
        S.op('act', lambda e: e.activation(out=tmp[:], in_=gif[:, t, 4:8], func=AF.Exp, scale=-1.0), reads=[gif.key], writes=[tmp.key])
        S.op('act', lambda e: e.activation(out=l[:], in_=tmp[:], func=AF.Ln, bias=one_t[:]), reads=[tmp.key, 'one_t'], writes=[l.key])
        S.op('pe', lambda e: e.matmul(gp[:, 32:36], lhsT=ones_f[:], rhs=l[:], start=True, stop=True), reads=['ones_f', l.key], writes=[gp.key])
        S.op('pe', lambda e: e.matmul(gp[:, 36:40], lhsT=tri_f[:], rhs=l[:], start=True, stop=True), reads=['tri_f', l.key], writes=[gp.key])
        S.op('dve', lambda e: e.tensor_tensor(out=u[:], in0=gp[:, 36:40], in1=gif[:, t, 0:4], op=ALU.add), reads=[gp.key, gif.key], writes=[u.key])
        S.op('dve', lambda e: e.tensor_tensor(out=tmp[:], in0=u[:], in1=gp[:, 32:36], op=ALU.subtract), reads=[gp.key, u.key], writes=[tmp.key])
        S.op('act', lambda e: e.activation(out=w[:], in_=tmp[:], func=AF.Exp), reads=[tmp.key], writes=[w.key])
        S.op('act', lambda e: e.activation(out=dec[:], in_=gp[:, 32:36], func=AF.Exp, scale=-1.0), reads=[gp.key], writes=[dec.key])

    def state_update(vaug, ktm, t, gt, CT, wvr):
        wv = wvr.next()
        for h in range(4):
            S.op('act', lambda e, h=h: e.activation(out=wv[:, h, :], in_=vaug[:, t, h, :], func=AF.Copy, scale=gt['w'][:, h:h + 1]), reads=[vaug.key, gt['w'].key], writes=[wv.key])
        for hh in range(2):
            ps = psb.next()
            for h2 in range(2):
                h = hh * 2 + h2
                S.op('pe', lambda e, h=h, h2=h2, ps=ps: e.matmul(ps[:, h2 * 129:(h2 + 1) * 129], lhsT=ktm[:, t, h, :], rhs=wv[:, h, :], start=True, stop=True), reads=[ktm.key, wv.key], writes=[ps.key])
            for h2 in range(2):
                h = hh * 2 + h2
                S.op('dve', lambda e, h=h, h2=h2, ps=ps: e.scalar_tensor_tensor(out=CT[:, h, :], in0=CT[:, h, :], scalar=gt['dec'][:, h:h + 1], in1=ps[:, h2 * 129:(h2 + 1) * 129], op0=ALU.mult, op1=ALU.add),
                     reads=[ps.key, gt['dec'].key], writes=[CT.key])

    FO = ExitStack()
    WF = sb(FO, "WF", [128, 8, NF], BF16)
    Ccur = sb(FO, "Ccur", [128, 4, 129])
    S.op('pool', lambda e: e.memset(kmT[:], 0.0), writes=['kmT'])
    with ExitStack() as st:
        prep_win(st, OFF['mk'], 512, WF, OFF['mk'], 'fm', gi0=0)
        prep_win(st, OFF['mv'], 1544, WF, OFF['mv'], 'tm')
        S.barrier()

    def rope_tables(tb, pos_ap, NTB):
        S.dma('sp', tb['pi'][:], pos_ap, writes=[tb['pi'].key])
        S.op('dve', lambda e: e.tensor_copy(out=tb['pf'][:], in_=tb['pi'][:]), reads=[tb['pi'].key], writes=[tb['pf'].key])
        v, vi, vf = tb['v'], tb['vi'], tb['vf']
        S.op('dve', lambda e: e.tensor_tensor(out=v[:], in0=invf[:].unsqueeze(1).to_broadcast([128, NTB, 32]), in1=tb['pf'][:].unsqueeze(2).to_broadcast([128, NTB, 32]), op=ALU.mult),
             reads=['invf_t', tb['pf'].key], writes=[v.key])
        for which in ['sin', 'cos']:
            if which == 'cos':
                S.op('dve', lambda e: e.tensor_scalar(out=v[:], in0=v[:], scalar1=0.25, scalar2=None, op0=ALU.add), reads=[], writes=[v.key])
            S.op('dve', lambda e: e.tensor_copy(out=vi[:], in_=v[:]), reads=[v.key], writes=[vi.key])
            S.op('dve', lambda e: e.tensor_copy(out=vf[:], in_=vi[:]), reads=[vi.key], writes=[vf.key])
            S.op('dve', lambda e: e.tensor_tensor(out=vf[:], in0=v[:], in1=vf[:], op=ALU.subtract), reads=[v.key], writes=[vf.key])
            if which == 'sin':
                S.op('act', lambda e: e.activation(out=tb['sinp'][:], in_=vf[:], func=AF.Sin, scale=2 * math.pi), reads=[vf.key], writes=[tb['sinp'].key])
                S.op('dve', lambda e: e.tensor_scalar(out=tb['sinn'][:], in0=tb['sinp'][:], scalar1=-1.0, scalar2=None, op0=ALU.mult), reads=[tb['sinp'].key], writes=[tb['sinn'].key])
            else:
                S.op('act', lambda e: e.activation(out=tb['cos2'][:, :, 0:32], in_=vf[:], func=AF.Sin, scale=2 * math.pi), reads=[vf.key], writes=[tb['cos2'].key])
                S.op('dve', lambda e: e.tensor_copy(out=tb['cos2'][:, :, 32:64], in_=tb['cos2'][:, :, 0:32]), reads=[], writes=[tb['cos2'].key])

    def rope_tiles(st, sfx="", NTB=2, tmp=None):
        tb = {}
        tmp = tmp or {}
        for k_, shp, dt_ in [('pi', [128, NTB], I32), ('pf', [128, NTB], F32), ('v', [128, NTB, 32], F32), ('vi', [128, NTB, 32], I32), ('vf', [128, NTB, 32], F32)]:
            tb[k_] = tmp[k_] if k_ in tmp else sb(st, "rt_" + k_ + sfx, shp, dt_)
        tb['cos2'] = sb(st, "rt_cos2" + sfx, [128, NTB, 64]); tb['sinp'] = sb(st, "rt_sinp" + sfx, [128, NTB, 32]); tb['sinn'] = sb(st, "rt_sinn" + sfx, [128, NTB, 32])
        return tb

    def conv_group(zt, zoff, cg, acc, out_ap, okey, silr, oscale=1.0):
        pc = psb.next()
        for j in range(4):
            S.op('pe', lambda e, j=j: e.matmul(pc[:, 0:256], lhsT=dgw[:, cg, j, :], rhs=zt[:, zoff + j:zoff + j + 256], start=(j == 0), stop=(j == 3)), reads=['dgw', zt.key], writes=[pc.key])
        se = silr.next(); sl = silr.next()
        S.op('act', lambda e: e.activation(out=se[:], in_=pc[:, 0:256], func=AF.Exp, scale=-1.0, bias=ncb[:, cg:cg + 1]), reads=[pc.key, 'ncb'], writes=[se.key])
        S.op('act', lambda e: e.activation(out=sl[:], in_=se[:], func=AF.Ln, bias=one_t[:]), reads=[se.key, 'one_t'], writes=[sl.key])
        qb = lnq_t if oscale != 1.0 else zero_t
        S.op('act', lambda e: e.activation(out=se[:], in_=sl[:], func=AF.Exp, scale=-1.0, bias=qb[:]), reads=[sl.key, qb.key], writes=[se.key])
        S.op('dve', lambda e: e.scalar_tensor_tensor(out=out_ap, in0=pc[:, 0:256], scalar=cb[:, cg:cg + 1], in1=se[:], op0=ALU.add, op1=ALU.mult), reads=[pc.key, 'cb_t', se.key], writes=[okey])

    VSTB = 4 if NBLK >= 4 else NBLK
    chk_stop('W')
    with ExitStack() as st:
        awr = Rot([sb(st, "awb%d" % i, [128, 8, 128]) for i in range(2)])
        mod_groups(awr, 16, 48, 'modc_b')
        S.op('dve', lambda e: e.scalar_tensor_tensor(out=gmod2[:], in0=modc[:, 32:40], scalar=1.0, in1=n2g[:], op0=ALU.add, op1=ALU.mult), reads=['modc_b', 'n2g_t'], writes=['gmod2'])
        RCH = min(8, NBLK)
        tbs = [rope_tiles(st, "a", 2 * RCH)]
        tbs.append(rope_tiles(st, "b", 2 * RCH, tmp=tbs[0]))
        xtr = Rot([sb(st, "xt%d" % i, [128, 8, 260]) for i in range(2)])
        sqr = Rot([sb(st, "sq%d" % i, [128, 8, 260], BF16) for i in range(2)])
        lnr = Rot([sb(st, "ln%d" % i, [128, 260]) for i in range(2)])
        rsr = Rot([sb(st, "rs%d" % i, [128, 260]) for i in range(2)])
        xsr = Rot([sb(st, "xs%d" % i, [128, 8, 260], BF16) for i in range(2)])
        zgr = Rot([sb(st, "zg%d" % i, [128, 260], BF16) for i in range(3)])
        accr = Rot([None])
        silr = Rot([sb(st, "sil%d" % i, [128, 256]) for i in range(4)])
        hal = sb(st, "hal", [128, 4, 3], BF16)
        kT = rt(st, "kT", [128, 4, 256], BF16)
        ktm = rt(st, "ktm", [128, 2, 4, 128], BF16)
        vaugr = Rot([sb(st, "vaug%d" % i, [128, 2, 4, 129], BF16) for i in range(2)])
        gifr = Rot([sb(st, "gif%d" % i, [128, 2, 8]) for i in range(2)])
        kxr = Rot([sb(st, "kx%d" % i, [128, 8, 64]) for i in range(2)])
        tAr = Rot([sb(st, "ropeA%d" % i, [128, 8, 64]) for i in range(2)]); tBr = Rot([sb(st, "ropeB%d" % i, [128, 8, 64]) for i in range(2)])
        Ktm = rt(st, "Ktm", [128, 2, 8, 64], BF16)
        KTblk = Rot([sb(st, "KTblk%d" % i, [128, 4, 256], BF16) for i in range(2)])
        vst = Rot([sb(st, "vst%d" % i, [128, 2 * VSTB, 8, 65], BF16) for i in range(2)])
        CT = sb(st, "CT", [128, 4, 129])
        gts = [{k: sb(st, "g%d_" % r + k, [128, 4]) for k in ['l', 'u', 'w', 'dec', 'tmp']} for r in range(3)]
        gti = [0]
        wvr = Rot([sb(st, "wv%d" % i, [128, 4, 129], BF16) for i in range(2)])
        ksq = rt(st, "ksq", [128, 8, 64]); kn2 = rt(st, "kn2", [128, 8]); kmacc = sb(st, "kmacc", [128, 8])
        S.op('pool', lambda e: e.memset(hal[:], 0.0), writes=['hal'])
        S.op('pool', lambda e: e.memset(CT[:], 0.0), writes=['CT'])
        S.op('pool', lambda e: e.memset(kmacc[:], 0.0), writes=['kmacc'])
        for v_ in vst.tiles:
            S.op('pool', lambda e, v_=v_: e.memset(v_[:], 1.0), writes=[v_.key])
        for v_ in vaugr.tiles:
            S.op('pool', lambda e, v_=v_: e.memset(v_[:], 1.0), writes=[v_.key])
        xTf_v = xTf.rearrange("(k p) t -> p k t", p=128)
        KTs_v = KTs.rearrange("(hp r) t -> r hp t", r=128)
        vcur = None
        for n in range(NBLK):
            i_own = n // 4
            if n % 4 == 0:
                S.op('dve', lambda e, n=n: e.tensor_scalar(out=Ccur[:], in0=CT[:], scalar1=capsel[:, 0:1], scalar2=None, op0=ALU.mult), reads=['CT', 'capsel_t'], writes=['Ccur'])
            else:
                S.op('dve', lambda e, n=n: e.scalar_tensor_tensor(out=Ccur[:], in0=CT[:], scalar=capsel[:, n % 4:n % 4 + 1], in1=Ccur[:], op0=ALU.mult, op1=ALU.add), reads=['CT', 'capsel_t'], writes=['Ccur'])
            if n % 4 == 3:
                S.dma('pool', CCs[i_own].rearrange("p (h e) -> p h e", h=4), Ccur[:], reads=['Ccur'], writes=[('CCs', i_own)])
            xt = xtr.next(); xs = xsr.next()
            tb = tbs[(n // RCH) % 2]
            cos2, sinp, sinn = tb['cos2'], tb['sinp'], tb['sinn']
            kT.rotate(); ktm.rotate(); Ktm.rotate()
            S.dma('sp', xt[:, :, 0:256], xTf_v[:, :, n * 256:(n + 1) * 256], writes=[xt.key])
            if n % RCH == 0:
                rope_tables(tb, posf_d[:, 2 * n:2 * n + 2 * RCH], 2 * RCH)
            front(xt, 256, xs, sqr, lnr, rsr)
            for hd in range(4):
                ps = psm.next(); zg = zgr.next(); acc = accr.next()
                proj_fm(ps, WF, OFF['mk'] + hd * 128, xs, 0, 256)
                S.op('pool', lambda e, hd=hd, zg=zg: e.tensor_copy(out=zg[:, 0:3], in_=hal[:, hd, :]), reads=['hal'], writes=[zg.key])
                S.op('act', lambda e, hd=hd, ps=ps, zg=zg: e.activation(out=zg[:, 3:259], in_=ps[:, 0:256], func=AF.Identity, bias=biasc[:, hd:hd + 1]), reads=[ps.key, 'biasc'], writes=[zg.key])
                S.op('pool', lambda e, hd=hd, zg=zg: e.tensor_copy(out=hal[:, hd, :], in_=zg[:, 256:259]), reads=[zg.key], writes=['hal'])
                conv_group(zg, 0, 4 + hd, acc, kT[:, hd, :], 'kT', silr)
            pt = pstr.next()
            for t in range(2):
                for hd in range(4):
                    S.op('pe', lambda e, t=t, hd=hd, pt=pt: e.transpose(out=pt[:, (t * 4 + hd) * 128:(t * 4 + hd + 1) * 128], in_=kT[:, hd, t * 128:(t + 1) * 128], identity=ident_b[:]), reads=['kT', 'ident_b'], writes=[pt.key])
            S.op('act', lambda e, pt=pt: e.activation(out=ktm[:].rearrange("p t h e -> p (t h e)"), in_=pt[:], func=AF.Copy), reads=[pt.key], writes=['ktm'])
            vaug = vaugr.next(); gif = gifr.next()
            if n % VSTB == 0:
                vcur = vst.next()
            KTb = KTblk.next()
            ptK = pstr.next()
            for t in range(2):
                ps = psm.next()
                proj_tm(ps[:, 0:512], ps.key, WF, OFF['mv'], 512, xs, t * 128)
                S.op('dve', lambda e, t=t, ps=ps, vaug=vaug: e.tensor_tensor(out=vaug[:, t, :, 0:128], in0=ps[:, 0:512].rearrange("p (h d) -> p h d", h=4), in1=bias_bc[:, bbc(OFF['mv']):bbc(OFF['mv']) + 512].rearrange("p (h d) -> p h d", h=4), op=ALU.add),
                     reads=[ps.key, 'bias_bc'], writes=[vaug.key])
                gp = psm.next()
                proj_tm(gp[:, 0:8], gp.key, WF, OFF['mif'], 8, xs, t * 128)
                S.op('dve', lambda e, t=t, gif=gif, gp=gp: e.tensor_tensor(out=gif[:, t, :], in0=gp[:, 0:8], in1=bias_bc[:, bbc(OFF['mif']):bbc(OFF['mif']) + 8], op=ALU.add), reads=[gp.key, 'bias_bc'], writes=[gif.key])
                ps = psm.next(); kx = kxr.next()
                proj_tm(ps[:, 0:512], ps.key, WF, OFF['ak'], 512, xs, t * 128)
                S.op('dve', lambda e, ps=ps, kx=kx: e.tensor_tensor(out=kx[:].rearrange("p h d -> p (h d)"), in0=ps[:, 0:512], in1=bias_bc[:, bbc(OFF['ak']):bbc(OFF['ak']) + 512], op=ALU.add), reads=[ps.key, 'bias_bc'], writes=[kx.key])
                rope(kx, Ktm[:, t, :, :], 'Ktm', cos2, sinp, sinn, (n % RCH) * 2 + t, tAr.next(), tBr.next())
                ps = psm.next()
                proj_tm(ps[:, 0:512], ps.key, WF, OFF['av'], 512, xs, t * 128)
                slot = (n % VSTB) * 2 + t
                S.op('dve', lambda e, ps=ps, slot=slot, vc=vcur: e.tensor_tensor(out=vc[:, slot, :, 0:64], in0=ps[:, 0:512].rearrange("p (h d) -> p h d", h=8), in1=bias_bc[:, bbc(OFF['av']):bbc(OFF['av']) + 512].rearrange("p (h d) -> p h d", h=8), op=ALU.add),
                     reads=[ps.key, 'bias_bc'], writes=[vcur.key])
            for t in range(2):
                for hp in range(4):
                    S.op('pe', lambda e, t=t, hp=hp, ptK=ptK: e.transpose(out=ptK[:, (hp * 2 + t) * 128:(hp * 2 + t + 1) * 128], in_=Ktm[:, t, 2 * hp:2 * hp + 2, :].rearrange("p h d -> p (h d)"), identity=ident_b[:]),
                         reads=['Ktm', 'ident_b'], writes=[ptK.key])
            S.op('act', lambda e, ptK=ptK, KTb=KTb: e.activation(out=KTb[:].rearrange("p h t -> p (h t)"), in_=ptK[:], func=AF.Copy), reads=[ptK.key], writes=[KTb.key])
            S.dma('pool', KTs_v[:, :, n * 256:(n + 1) * 256], KTb[:], reads=[KTb.key], writes=[('KTs', n)])
            kp = psb.next()
            for hp in range(4):
                for t in range(2):
                    S.op('pe', lambda e, t=t, hp=hp, kp=kp: e.matmul(kp[:, 16 + hp:17 + hp], lhsT=Ktm[:, t, 2 * hp:2 * hp + 2, :].rearrange("p h d -> p (h d)"), rhs=ones_b[:, 0:1], start=(t == 0), stop=(t == 1)),
                         reads=['Ktm', 'ones_b'], writes=[kp.key])
            S.op('act', lambda e, n=n: e.activation(out=kmT[0:64, :, 0, n], in_=kp[0:64, 16:20], func=AF.Copy, scale=1.0 / 256), reads=[kp.key], writes=['kmT'])
            S.op('act', lambda e, n=n: e.activation(out=kmT[64:128, :, 1, n], in_=kp[64:128, 16:20], func=AF.Copy, scale=1.0 / 256), reads=[kp.key], writes=['kmT'])
            for t in range(2):
                ksq.rotate(); kn2.rotate()
                S.op('pool', lambda e, t=t: e.tensor_tensor(out=ksq[:], in0=Ktm[:, t, :, :], in1=Ktm[:, t, :, :], op=ALU.mult), reads=['Ktm'], writes=['ksq'])
                S.op('dve', lambda e: e.reduce_sum(out=kn2[:], in_=ksq[:], axis=AX.X), reads=['ksq'], writes=['kn2'])
                S.op('dve', lambda e: e.tensor_tensor(out=kmacc[:], in0=kmacc[:], in1=kn2[:], op=ALU.max), reads=['kn2'], writes=['kmacc'])
            if n % VSTB == VSTB - 1:
                b0 = n - (VSTB - 1)
                for h in range(8):
                    S.dma('pool', Vs[h, :, b0 * 2 * 65:(n + 1) * 2 * 65].rearrange("p (t d) -> p t d", d=65), vcur[:, 0:2 * VSTB, h, :], reads=[vcur.key], writes=[('Vs', h, b0)])
            for t in range(2):
                gt = gts[gti[0] % 3]; gti[0] += 1
                gates(gif, t, gt)
                state_update(vaug, ktm, t, gt, CT, wvr)
        ptf = psm.next()
        S.op('pe', lambda e: e.transpose(out=ptf[0:8, 0:128], in_=kmacc[:], identity=ident_f[:]), reads=['kmacc', 'ident_f'], writes=[ptf.key])
        kmx = sb(st, "kmx", [8, 1]); dg = sb(st, "dg", [8, 8])
        S.op('dve', lambda e: e.reduce_max(out=kmx[:], in_=ptf[0:8, 0:128], axis=AX.X), reads=[ptf.key], writes=['kmx'])
        S.op('dve', lambda e: e.tensor_scalar(out=dg[:], in0=ident_f[0:8, 0:8], scalar1=kmx[:, 0:1], scalar2=None, op0=ALU.mult), reads=['kmx', 'ident_f'], writes=['dg'])
        S.op('pe', lambda e: e.matmul(ptf[:, 256:264], lhsT=ones_f[0:8, :], rhs=dg[:], start=True, stop=True), reads=['ones_f', 'dg', ptf.key], writes=[ptf.key])
        S.op('act', lambda e: e.activation(out=kmaxbc[:], in_=ptf[:, 256:264], func=AF.Ln, bias=eps_t[:]), reads=[ptf.key, 'eps_t'], writes=['kmaxbc'])
        S.op('act', lambda e: e.activation(out=kmaxbc[:], in_=kmaxbc[:], func=AF.Exp, scale=0.5), reads=[], writes=['kmaxbc'])
        dump("kmT", kmT[:].rearrange("p h a n -> p (h a n)"), [128, 256], ['kmT'])
        dump("kmaxbc", kmaxbc[:], [128, 8], ['kmaxbc'])
        dump("CTfin", CT[:].rearrange("p h e -> p (h e)"), [128, 4 * 129], ['CT'])
        S.barrier()

    chk_stop('F')
    with ExitStack() as st:
        WO = sb(st, "WO", [128, 8, 1536], BF16)
        with ExitStack() as st2:
            prep_win(st2, OFF['mq'], 512, WO, OFF['mq'] - NF, 'fm', gi0=4)
            prep_win(st2, OFF['mo'], 1024, WO, OFF['mo'] - NF, 'tm')
            S.barrier()
        tb = {}
        tb['cos2'] = sb(st, "rt_cos2o", [128, NTO, 64]); tb['sinp'] = sb(st, "rt_sinpo", [128, NTO, 32]); tb['sinn'] = sb(st, "rt_sinno", [128, NTO, 32])
        cos2, sinp, sinn = tb['cos2'], tb['sinp'], tb['sinn']
        with ExitStack() as st2:
            for k_, shp, dt_ in [('pi', [128, NTO], I32), ('pf', [128, NTO], F32), ('v', [128, NTO, 32], F32), ('vi', [128, NTO, 32], I32), ('vf', [128, NTO, 32], F32)]:
                tb[k_] = sb(st2, "rt_" + k_ + "o", shp, dt_)
            rope_tables(tb, poso_d, NTO)
            S.barrier()
        xtr = Rot([sb(st, "xt%d" % i, [128, 8, 260]) for i in range(1)])
        sqr = Rot([sb(st, "sq%d" % i, [128, 8, 260], BF16) for i in range(1)])
        lnr = Rot([sb(st, "ln%d" % i, [128, 260]) for i in range(2)])
        rsr = Rot([sb(st, "rs%d" % i, [128, 260]) for i in range(2)])
        xsr = Rot([sb(st, "xs%d" % i, [128, 8, 260], BF16) for i in range(2)])
        zgr = Rot([sb(st, "zg%d" % i, [128, 260], BF16) for i in range(3)])
        accr = Rot([None])
        silr = Rot([sb(st, "sil%d" % i, [128, 256]) for i in range(4)])
        qkT = sb(st, "qkT", [128, 8, 256], BF16)
        ktm = sb(st, "ktm", [128, 2, 4, 128], BF16)
        vaug = rt(st, "vaug", [128, 2, 4, 129], BF16)
        gif = rt(st, "gif", [128, 2, 8])
        kxr = Rot([sb(st, "kx%d" % i, [128, 8, 64]) for i in range(2)])
        tAr = Rot([sb(st, "ropeA%d" % i, [128, 8, 64]) for i in range(2)]); tBr = Rot([sb(st, "ropeB%d" % i, [128, 8, 64]) for i in range(2)])
        Qtm = rt(st, "Qtm", [128, 8, 64], BF16); Ktm1 = rt(st, "Ktm1", [128, 8, 64], BF16)
        Qaug = rt(st, "Qaug", [128, 8, 97], BF16); Kaug = rt(st, "Kaug", [128, 8, 97], BF16)
        QTp = rt(st, "QTp", [128, 4, 128], BF16)
        gm = rt(st, "gm", [128, 8, 32]); t8 = rt(st, "t8", [128, 8, 8]); thr = rt(st, "thr", [128, 8]); msk = rt(st, "msk", [128, 8, 32])
        sqq = sb(st, "sqq", [128, 8, 64]); qn = rt(st, "qn", [128, 8])
        QTab = sb(st, "QTab", [128, 8, 256], BF16); KTab = sb(st, "KTab", [128, 8, 256], BF16)
        Vob = sb(st, "Vob", [128, 8, 2, 65], BF16)
        ymTb = sb(st, "ymTb", [128, 4, 256], BF16)
        mo_f = rt(st, "mo_f", [128, 512], k=1); sigmo = rt(st, "sigmo", [128, 512], BF16)
        CT = sb(st, "CT", [128, 4, 129]); CTb = rt(st, "CTb", [128, 4, 129], BF16)
        gts = [{k: sb(st, "g%d_" % r + k, [128, 4]) for k in ['l', 'u', 'w', 'dec', 'tmp']} for r in range(3)]
        gti = [0]
        wvr = Rot([sb(st, "wv%d" % i, [128, 4, 129], BF16) for i in range(2)])
        LFbc = rt(st, "LFbc", [128, 4, 128], k=1)
        EB = rt(st, "EB", [128, 4, 128], k=1); DT = rt(st, "DT", [128, 4, 128])
        SWT = rt(st, "SWT", [128, 4, 128], BF16); qsT = rt(st, "qsT", [128, 4, 128], BF16)
        absd = rt(st, "absd", [128, 4]); rr = rt(st, "rr", [128, 4])
        hsb = rt(st, "hsb", [128, 4, 128], k=1); hsq = sb(st, "hsq", [128, 4, 128]); ssq = rt(st, "ssq", [128, 4]); rsh = rt(st, "rsh", [128, 4])
        ym = rt(st, "ym", [128, 512], BF16, k=1)
        for _ in range(2):
            S.op('pool', lambda e: e.memset(vaug[:], 1.0), writes=['vaug'])
            S.op('pool', lambda e: e.memset(Kaug[:], 0.0), writes=['Kaug'])
            S.op('pool', lambda e: e.memset(Kaug[:, :, 96:97], 1.0), writes=['Kaug'])
            vaug.rotate(); Kaug.rotate()
        S.op('pool', lambda e: e.memset(Vob[:], 1.0), writes=['Vob'])
        S.op('pool', lambda e: e.memset(QTab[:], 0.0), writes=['QTab'])
        S.op('pool', lambda e: e.memset(KTab[:], 0.0), writes=['KTab'])
        xTo_v = xTo.rearrange("(k p) t -> p k t", p=128)
        for i in range(NOWN):
            xt = xtr.next(); xs = xsr.next()
            vaug.rotate(); gif.rotate()
            S.dma('sp', xt[:], xTo_v[:, :, i * 260:(i + 1) * 260], writes=[xt.key])
            S.dma('sp', CT[:], CCs[i].rearrange("p (h e) -> p h e", h=4), reads=[('CCs', i)], writes=['CT'])
            front(xt, 260, xs, sqr, lnr, rsr)
            S.dma('pool', XSs[i].rearrange("p (k t) -> p k t", k=8), xs[:, :, 4:260], reads=xskeys(xs), writes=[('XSs', i)])
            for g in range(8):
                ps = psm.next(); zg = zgr.next(); acc = accr.next()
                if g < 4:
                    proj_fm(ps, WO, OFF['mq'] - NF + g * 128, xs, 0, 260)
                    bcol = 4 + g
                else:
                    proj_fm(ps, WF, OFF['mk'] + (g - 4) * 128, xs, 0, 260)
                    bcol = g - 4
                S.op('act', lambda e, zg=zg, ps=ps, bcol=bcol: e.activation(out=zg[:], in_=ps[:, 0:260], func=AF.Identity, bias=biasc[:, bcol:bcol + 1]), reads=[ps.key, 'biasc'], writes=[zg.key])
                S.op('dve', lambda e, i=i, zg=zg: e.tensor_scalar(out=zg[:, 0:4], in0=zg[:, 0:4], scalar1=hv[:, i:i + 1], scalar2=None, op0=ALU.mult), reads=['hv_t'], writes=[zg.key])
                conv_group(zg, 1, g, acc, qkT[:, g, :], 'qkT', silr, oscale=(128.0 ** -0.5 if g < 4 else 1.0))
            pt = pstr.next()
            for t in range(2):
                for hd in range(4):
                    S.op('pe', lambda e, t=t, hd=hd, pt=pt: e.transpose(out=pt[:, (t * 4 + hd) * 128:(t * 4 + hd + 1) * 128], in_=qkT[:, 4 + hd, t * 128:(t + 1) * 128], identity=ident_b[:]), reads=['qkT', 'ident_b'], writes=[pt.key])
            S.op('act', lambda e, pt=pt: e.activation(out=ktm[:].rearrange("p t h e -> p (t h e)"), in_=pt[:], func=AF.Copy), reads=[pt.key], writes=['ktm'])
            S.op('act', lambda e: e.activation(out=CTb[:], in_=CT[:], func=AF.Copy), reads=['CT'], writes=['CTb'])
            if i == 0:
                chk_stop('O1a')
            for t in range(2):
                T = i * 2 + t
                x0 = 4 + t * 128
                for r_ in [Qtm, Ktm1, Qaug, Kaug, QTp, gm, t8, thr, msk, qn, mo_f, sigmo, LFbc, EB, DT, SWT, qsT, absd, rr, hsb, ssq, rsh, ym]:
                    r_.rotate()
                tA = tAr.next(); tB = tBr.next()
                gt = gts[gti[0] % 3]; gti[0] += 1
                ps = psm.next()
                proj_tm(ps[:, 0:512], ps.key, WF, OFF['mv'], 512, xs, x0)
                S.op('dve', lambda e, t=t, ps=ps: e.tensor_tensor(out=vaug[:, t, :, 0:128], in0=ps[:, 0:512].rearrange("p (h d) -> p h d", h=4), in1=bias_bc[:, bbc(OFF['mv']):bbc(OFF['mv']) + 512].rearrange("p (h d) -> p h d", h=4), op=ALU.add),
                     reads=[ps.key, 'bias_bc'], writes=['vaug'])
                gp = psm.next()
                proj_tm(gp[:, 0:8], gp.key, WF, OFF['mif'], 8, xs, x0)
                S.op('dve', lambda e, t=t, gp=gp: e.tensor_tensor(out=gif[:, t, :], in0=gp[:, 0:8], in1=bias_bc[:, bbc(OFF['mif']):bbc(OFF['mif']) + 8], op=ALU.add), reads=[gp.key, 'bias_bc'], writes=['gif'])
                ps = psm.next()
                proj_tm(ps[:, 0:512], ps.key, WO, OFF['mo'] - NF, 512, xs, x0)
                S.op('dve', lambda e, ps=ps: e.tensor_tensor(out=mo_f[:], in0=ps[:, 0:512], in1=bias_bc[:, bbc(OFF['mo']):bbc(OFF['mo']) + 512], op=ALU.add), reads=[ps.key, 'bias_bc'], writes=['mo_f'])
                S.op('act', lambda e: e.activation(out=mo_f[:], in_=mo_f[:], func=AF.Exp, scale=-1.0), reads=[], writes=['mo_f'])
                S.op('act', lambda e: e.activation(out=mo_f[:], in_=mo_f[:], func=AF.Ln, bias=one_t[:]), reads=['one_t'], writes=['mo_f'])
                S.op('act', lambda e: e.activation(out=sigmo[:], in_=mo_f[:], func=AF.Exp, scale=-1.0), reads=['mo_f'], writes=['sigmo'])
                ps = psm.next(); kx = kxr.next()
                proj_tm(ps[:, 0:512], ps.key, WO, OFF['aq'] - NF, 512, xs, x0)
                S.op('dve', lambda e, ps=ps, kx=kx: e.tensor_tensor(out=kx[:].rearrange("p h d -> p (h d)"), in0=ps[:, 0:512], in1=bias_bc[:, bbc(OFF['aq']):bbc(OFF['aq']) + 512], op=ALU.add), reads=[ps.key, 'bias_bc'], writes=[kx.key])
                rope(kx, Qtm[:], 'Qtm', cos2, sinp, sinn, T, tA, tB)
                ps = psm.next(); kx = kxr.next()
                proj_tm(ps[:, 0:512], ps.key, WF, OFF['ak'], 512, xs, x0)
                S.op('dve', lambda e, ps=ps, kx=kx: e.tensor_tensor(out=kx[:].rearrange("p h d -> p (h d)"), in0=ps[:, 0:512], in1=bias_bc[:, bbc(OFF['ak']):bbc(OFF['ak']) + 512], op=ALU.add), reads=[ps.key, 'bias_bc'], writes=[kx.key])
                rope(kx, Ktm1[:], 'Ktm1', cos2, sinp, sinn, T, tAr.next(), tBr.next())
                ps = psm.next()
                proj_tm(ps[:, 0:512], ps.key, WF, OFF['av'], 512, xs, x0)
                S.op('dve', lambda e, ps=ps, t=t: e.tensor_tensor(out=Vob[:, :, t, 0:64], in0=ps[:, 0:512].rearrange("p (h d) -> p h d", h=8), in1=bias_bc[:, bbc(OFF['av']):bbc(OFF['av']) + 512].rearrange("p (h d) -> p h d", h=8), op=ALU.add),
                     reads=[ps.key, 'bias_bc'], writes=['Vob'])
                if i == 0 and t == 0:
                    chk_stop('O1b')
                gates(gif, t, gt)
                for h in range(4):
                    S.op('act', lambda e, h=h: e.activation(out=LFbc[:, h, :], in_=ones_f[:], func=AF.Copy, scale=gt['l'][:, h:h + 1]), reads=['ones_f', gt['l'].key], writes=['LFbc'])
                pb = psb.next()
                for h in range(4):
                    S.op('pe', lambda e, h=h, pb=pb: e.matmul(pb[:, h * 128:(h + 1) * 128], lhsT=LFbc[:, h, :], rhs=tri_f[:], start=True, stop=True), reads=['LFbc', 'tri_f'], writes=[pb.key])
                S.op('act', lambda e, pb=pb: e.activation(out=EB[:].rearrange("p h t -> p (h t)"), in_=pb[:, 0:512], func=AF.Exp, scale=-1.0), reads=[pb.key], writes=['EB'])
                for h in range(4):
                    S.op('act', lambda e, h=h, pb=pb: e.activation(out=DT[:, h, :], in_=pb[:, h * 128:(h + 1) * 128], func=AF.Exp, scale=-1.0, bias=gt['u'][:, h:h + 1]), reads=[pb.key, gt['u'].key], writes=['DT'])
                for h in range(4):
                    S.op('pool', lambda e, h=h: e.tensor_tensor(out=DT[:, h, :], in0=DT[:, h, :], in1=tri_f[:], op=ALU.mult), reads=['tri_f'], writes=['DT'])
                pS = psb.next()
                for h in range(4):
                    S.op('pe', lambda e, h=h, t=t, pS=pS: e.matmul(pS[:, h * 128:(h + 1) * 128], lhsT=qkT[:, 4 + h, t * 128:(t + 1) * 128], rhs=qkT[:, h, t * 128:(t + 1) * 128], start=True, stop=True), reads=['qkT'], writes=[pS.key])
                S.op('dve', lambda e, pS=pS: e.tensor_tensor(out=SWT[:].rearrange("p h t -> p (h t)"), in0=DT[:].rearrange("p h t -> p (h t)"), in1=pS[:, 0:512], op=ALU.mult), reads=['DT', pS.key], writes=['SWT'])
                S.op('dve', lambda e, t=t: e.tensor_tensor(out=qsT[:], in0=qkT[:, 0:4, t * 128:(t + 1) * 128], in1=EB[:], op=ALU.mult), reads=['qkT', 'EB'], writes=['qsT'])
                pn = [psb.next(), psb.next()]
                for h in range(4):
                    pp = pn[h // 2]; o0 = (h % 2) * 129
                    S.op('pe', lambda e, h=h, t=t, pp=pp, o0=o0: e.matmul(pp[:, o0:o0 + 129], lhsT=SWT[:, h, :], rhs=vaug[:, t, h, :], start=True, stop=False), reads=['SWT', 'vaug'], writes=[pp.key])
                    S.op('pe', lambda e, h=h, pp=pp, o0=o0: e.matmul(pp[:, o0:o0 + 129], lhsT=qsT[:, h, :], rhs=CTb[:, h, :], start=False, stop=True), reads=['qsT', 'CTb'], writes=[pp.key])
                for h in range(4):
                    pp = pn[h // 2]; o0 = (h % 2) * 129
                    S.op('act', lambda e, h=h, pp=pp, o0=o0: e.activation(out=absd[:, h:h + 1], in_=pp[:, o0 + 128:o0 + 129], func=AF.Abs), reads=[pp.key], writes=['absd'])
                S.op('dve', lambda e: e.tensor_scalar(out=absd[:], in0=absd[:], scalar1=1.0, scalar2=None, op0=ALU.max), reads=[], writes=['absd'])
                S.op('dve', lambda e: e.reciprocal(out=rr[:], in_=absd[:]), reads=['absd'], writes=['rr'])
                for h in range(4):
                    pp = pn[h // 2]; o0 = (h % 2) * 129
                    S.op('act', lambda e, h=h, pp=pp, o0=o0: e.activation(out=hsb[:, h, :], in_=pp[:, o0:o0 + 128], func=AF.Copy, scale=rr[:, h:h + 1]), reads=[pp.key, 'rr'], writes=['hsb'])
                S.op('pool', lambda e: e.tensor_tensor(out=hsq[:], in0=hsb[:], in1=hsb[:], op=ALU.mult), reads=['hsb'], writes=['hsq'])
                S.op('dve', lambda e: e.reduce_sum(out=ssq[:], in_=hsq[:], axis=AX.X), reads=['hsq'], writes=['ssq'])
                S.op('act', lambda e: e.activation(out=ssq[:], in_=ssq[:], func=AF.Ln, scale=1.0 / 128, bias=eps_t[:]), reads=['eps_t'], writes=['ssq'])
                S.op('act', lambda e: e.activation(out=rsh[:], in_=ssq[:], func=AF.Exp, scale=-0.5), reads=['ssq'], writes=['rsh'])
                S.op('pool', lambda e: e.tensor_tensor(out=hsb[:].rearrange("p h d -> p (h d)"), in0=hsb[:].rearrange("p h d -> p (h d)"), in1=mng[:], op=ALU.mult), reads=['hsq', 'mng_t'], writes=['hsb'])
                for h in range(4):
                    S.op('dve', lambda e, h=h: e.scalar_tensor_tensor(out=ym[:, h * 128:(h + 1) * 128], in0=hsb[:, h, :], scalar=rsh[:, h:h + 1], in1=sigmo[:, h * 128:(h + 1) * 128], op0=ALU.mult, op1=ALU.mult),
                         reads=['hsb', 'rsh', 'sigmo'], writes=['ym'])
                pt = pstr.next()
                for c in range(4):
                    S.op('pe', lambda e, c=c, pt=pt: e.transpose(out=pt[:, c * 128:(c + 1) * 128], in_=ym[:, c * 128:(c + 1) * 128], identity=ident_b[:]), reads=['ym', 'ident_b'], writes=[pt.key])
                S.op('act', lambda e, pt=pt, t=t: e.activation(out=ymTb[:, :, t * 128:(t + 1) * 128], in_=pt[:, 0:512].rearrange("p (c t) -> p c t", c=4), func=AF.Copy), reads=[pt.key], writes=['ymTb'])
                state_update(vaug, ktm, t, gt, CT, wvr)
                CTb.rotate()
                S.op('act', lambda e: e.activation(out=CTb[:], in_=CT[:], func=AF.Copy), reads=['CT'], writes=['CTb'])
                if i == 0 and t == 0:
                    chk_stop('O1c')
                pt = pstr.next()
                for hp in range(4):
                    S.op('pe', lambda e, hp=hp, pt=pt: e.transpose(out=pt[:, hp * 128:(hp + 1) * 128], in_=Qtm[:, 2 * hp:2 * hp + 2, :].rearrange("p h d -> p (h d)"), identity=ident_b[:]), reads=['Qtm', 'ident_b'], writes=[pt.key])
                S.op('act', lambda e, pt=pt: e.activation(out=QTp[:].rearrange("p h t -> p (h t)"), in_=pt[:, 0:512], func=AF.Copy), reads=[pt.key], writes=['QTp'])
                pg = psb.next()
                for hp in range(4):
                    S.op('pe', lambda e, hp=hp, pg=pg: e.matmul(pg[:, hp * 64:(hp + 1) * 64], lhsT=QTp[:, hp, :], rhs=kmT[:, hp, :, :].rearrange("p a n -> p (a n)"), start=True, stop=True), reads=['QTp', 'kmT'], writes=[pg.key])
                S.op('dve', lambda e, pg=pg, i=i: e.tensor_tensor(out=gm[:], in0=pg[:, 0:256].rearrange("p (h n) -> p h n", h=8), in1=pastb[:, i * 32:(i + 1) * 32].unsqueeze(1).to_broadcast([128, 8, 32]), op=ALU.add), reads=[pg.key, 'pastb_t'], writes=['gm'])
                for h in range(8):
                    S.op('dve', lambda e, h=h: e.max(out=t8[:, h, :], in_=gm[:, h, :]), reads=['gm'], writes=['t8'])
                S.op('dve', lambda e: e.tensor_scalar(out=thr[:], in0=t8[:, :, 2], scalar1=-1e29, scalar2=None, op0=ALU.max), reads=['t8'], writes=['thr'])
                S.op('dve', lambda e: e.tensor_tensor(out=msk[:], in0=gm[:], in1=thr[:].unsqueeze(2).to_broadcast([128, 8, 32]), op=ALU.is_lt), reads=['gm', 'thr'], writes=['msk'])
                S.op('dve', lambda e: e.tensor_scalar(out=Qaug[:, :, 64:96], in0=msk[:], scalar1=NEG, scalar2=None, op0=ALU.mult), reads=['msk'], writes=['Qaug'])
                S.op('pool', lambda e: e.tensor_copy(out=Qaug[:, :, 0:64], in_=Qtm[:]), reads=['Qtm'], writes=['Qaug'])
                S.op('pool', lambda e: e.tensor_tensor(out=sqq[:], in0=Qtm[:], in1=Qtm[:], op=ALU.mult), reads=['Qtm'], writes=['sqq'])
                S.op('dve', lambda e: e.reduce_sum(out=qn[:], in_=sqq[:], axis=AX.X), reads=['sqq'], writes=['qn'])
                S.op('act', lambda e: e.activation(out=qn[:], in_=qn[:], func=AF.Ln, bias=eps_t[:]), reads=['eps_t'], writes=['qn'])
                S.op('act', lambda e: e.activation(out=qn[:], in_=qn[:], func=AF.Exp, scale=0.5), reads=[], writes=['qn'])
                S.op('dve', lambda e: e.scalar_tensor_tensor(out=Qaug[:, :, 96], in0=qn[:], scalar=-1.0, in1=kmaxbc[:], op0=ALU.mult, op1=ALU.mult), reads=['qn', 'kmaxbc'], writes=['Qaug'])
                S.op('pool', lambda e: e.tensor_copy(out=Kaug[:, :, 0:64], in_=Ktm1[:]), reads=['Ktm1'], writes=['Kaug'])
                for (aug, dstT) in [(Qaug, QTab), (Kaug, KTab)]:
                    pt = pstr.next()
                    for h in range(8):
                        S.op('pe', lambda e, h=h, pt=pt, aug=aug: e.transpose(out=pt[0:97, h * 128:(h + 1) * 128], in_=aug[:, h, :], identity=ident_b[:]), reads=[aug.key, 'ident_b'], writes=[pt.key])
                    S.op('act', lambda e, pt=pt, dstT=dstT, t=t: e.activation(out=dstT[0:97, :, t * 128:(t + 1) * 128], in_=pt[0:97, :].rearrange("p (h t) -> p h t", h=8), func=AF.Copy), reads=[pt.key], writes=[dstT.key])
            if i == 0:
                chk_stop('O1d')
            S.dma('pool', QTs[:, :, i * 256:(i + 1) * 256].rearrange("h r t -> r h t"), QTab[:, :, :], reads=['QTab'], writes=[('QTs', i)])
            S.dma('pool', KTos[:, :, i * 256:(i + 1) * 256].rearrange("h r t -> r h t"), KTab[:, :, :], reads=['KTab'], writes=[('KTos', i)])
            S.dma('pool', Vos[:, :, i * 130:(i + 1) * 130].rearrange("h p x -> p h x"), Vob[:].rearrange("p h t d -> p h (t d)"), reads=['Vob'], writes=[('Vos', i)])
            S.dma('pool', YMs[i].rearrange("p (c t) -> p c t", c=4), ymTb[:], reads=['ymTb'], writes=[('YMs', i)])
        S.barrier()
    FO.close()
    G1.close()

    chk_stop('O1')
    XA = ExitStack()
    xacc = sb(XA, "xacc", [128, 8, NOWN * 256])
    OY = ExitStack()
    yaT = sb(OY, "yaT", [128, 4, NOWN * 256], BF16)
    with ExitStack() as st:
        KTb = [sb(st, "KTbuf%d" % i, [128, S_LEN], BF16) for i in range(2)]
        Vb = [sb(st, "Vbuf%d" % i, [128, NT, 65], BF16) for i in range(2)]
        QTh = [sb(st, "QTh%d" % i, [128, NOWN * 256], BF16) for i in range(2)]
        KTo = [sb(st, "KToh%d" % i, [128, NOWN * 256], BF16) for i in range(2)]
        Voh = [sb(st, "Voh%d" % i, [128, NTO, 65], BF16) for i in range(2)]
        PTr = Rot([sb(st, "PT%d" % i, [128, 512], BF16) for i in range(3)])
        ya = sb(st, "ya", [128, NTO, 512], BF16)
        rden = sb(st, "rden", [128, 2])
        estg = sb(st, "estg", [128, 2048])
        for c0 in range(0, S_LEN, 2048):
            cn = min(2048, S_LEN - c0)
            S.dma('sp', estg[64:97, 0:cn], etab_d[64:97, c0:c0 + cn], writes=['estg'])
            for b2 in range(2):
                S.op('dve', lambda e, b2=b2, c0=c0, cn=cn: e.tensor_copy(out=KTb[b2][64:97, c0:c0 + cn], in_=estg[64:97, 0:cn]), reads=['estg'], writes=[KTb[b2].key + "aug"])
        for h in range(8):
            b2 = h % 2
            kt_, vb_, qt_, ko_, vo_ = KTb[b2], Vb[b2], QTh[b2], KTo[b2], Voh[b2]
            for c0 in range(0, S_LEN, 2048):
                cn = min(2048, S_LEN - c0)
                S.dma('sp', kt_[0:64, c0:c0 + cn], KTs[h * 64:(h + 1) * 64, c0:c0 + cn], reads=[('KTs', n) for n in range(c0 // 256, (c0 + cn) // 256)], writes=[kt_.key])
            S.dma('sp', vb_[:].rearrange("p t d -> p (t d)"), Vs[h], reads=[('Vs', h, b0) for b0 in range(0, NBLK, VSTB)], writes=[vb_.key])
            S.dma('sp', qt_[:, :], QTs[h], reads=[('QTs', i) for i in range(NOWN)], writes=[qt_.key])
            S.dma('sp', ko_[:, :], KTos[h], reads=[('KTos', i) for i in range(NOWN)], writes=[ko_.key])
            S.dma('sp', vo_[:].rearrange("p t d -> p (t d)"), Vos[h], reads=[('Vos', i) for i in range(NOWN)], writes=[vo_.key])
            for i in range(NOWN):
                units = [('p', n) for n in range(4 * i + 3)] + [('o', i)]
                acc = [pss[0], pss[1]]
                nmm = {0: 0, 1: 0}
                tot = {0: 2 * (4 * i + 3) + 1, 1: 2 * (4 * i + 3) + 2}

                def emit_S(u):
                    ps = psmo.next()
                    for kt in range(2):
                        if u[0] == 'p':
                            lhs = kt_[0:97, (2 * u[1] + kt) * 128:(2 * u[1] + kt + 1) * 128]
                            rk = [kt_.key, kt_.key + "aug"]
                        else:
                            lhs = ko_[0:97, i * 256 + kt * 128:i * 256 + (kt + 1) * 128]
                            rk = [ko_.key]
                        S.op('pe', lambda e, ps=ps, kt=kt, lhs=lhs: e.matmul(ps[:, kt * 256:(kt + 1) * 256], lhsT=lhs, rhs=qt_[0:97, i * 256:(i + 1) * 256], start=True, stop=True), reads=rk + [qt_.key], writes=[ps.key])
                    return ps

                def emit_PV(u, ps):
                    PT = PTr.next()
                    S.op('act', lambda e: e.activation(out=PT[:], in_=ps[:, 0:512], func=AF.Exp, scale=0.125), reads=[ps.key], writes=[PT.key])
                    if u[0] == 'o':
                        S.op('pool', lambda e: e.tensor_tensor(out=PT[:, 0:128], in0=PT[:, 0:128], in1=tri_b[:], op=ALU.mult), reads=['tri_b'], writes=[PT.key])
                        S.op('pool', lambda e: e.tensor_tensor(out=PT[:, 384:512], in0=PT[:, 384:512], in1=tri_b[:], op=ALU.mult), reads=['tri_b'], writes=[PT.key])
                    for qt in range(2):
                        for kt in range(2):
                            if u[0] == 'o' and kt == 1 and qt == 0:
                                continue
                            if u[0] == 'p':
                                rhs = vb_[:, 2 * u[1] + kt, :]
                                rk = [vb_.key]
                            else:
                                rhs = vo_[:, 2 * i + kt, :]
                                rk = [vo_.key]
                            first = nmm[qt] == 0
                            nmm[qt] += 1
                            last = nmm[qt] == tot[qt]
                            S.op('pe', lambda e, qt=qt, kt=kt, rhs=rhs, first=first, last=last: e.matmul(acc[qt][:, 0:65], lhsT=PT[:, kt * 256 + qt * 128:kt * 256 + (qt + 1) * 128], rhs=rhs, start=first, stop=last),
                                 reads=[PT.key] + rk, writes=[acc[qt].key])

                prev = None
                for u in units:
                    ps = emit_S(u)
                    if prev is not None:
                        emit_PV(*prev)
                    prev = (u, ps)
                emit_PV(*prev)
                assert nmm[0] == tot[0] and nmm[1] == tot[1]
                for qt in range(2):
                    S.op('dve', lambda e, qt=qt: e.reciprocal(out=rden[:, qt:qt + 1], in_=acc[qt][:, 64:65]), reads=[acc[qt].key], writes=['rden'])
                    S.op('dve', lambda e, qt=qt: e.tensor_scalar(out=ya[:, 2 * i + qt, h * 64:(h + 1) * 64], in0=acc[qt][:, 0:64], scalar1=rden[:, qt:qt + 1], scalar2=None, op0=ALU.mult), reads=[acc[qt].key, 'rden'], writes=['ya'])
        for T in range(NTO):
            pt = pstr.next()
            for c in range(4):
                S.op('pe', lambda e, c=c, pt=pt, T=T: e.transpose(out=pt[:, c * 128:(c + 1) * 128], in_=ya[:, T, c * 128:(c + 1) * 128], identity=ident_b[:]), reads=['ya', 'ident_b'], writes=[pt.key])
            S.op('act', lambda e, pt=pt, T=T: e.activation(out=yaT[:, :, T * 128:(T + 1) * 128], in_=pt[:, 0:512].rearrange("p (c t) -> p c t", c=4), func=AF.Copy), reads=[pt.key], writes=['yaT'])
        dump("yaT", yaT[:].rearrange("p c t -> p (c t)"), [128, 4 * NOWN * 256], ['yaT'])
        S.barrier()

    chk_stop('O2')
    with ExitStack() as st:
        WG = sb(st, "WG", [128, 8, 2048], BF16)
        pm = sb(st, "pm", [128, 4, D], BF16); pa = sb(st, "pa", [128, 4, D], BF16); wo = sb(st, "wo", [128, 8, D], BF16)
        with ExitStack() as st2:
            prep_win(st2, OFF['ga'], 2048, WG, 0, 'fm', gi0=8)
            stg = Rot([sb(st2, "pstg%d" % i, [128, 4, 512]) for i in range(2)])
            for (src, dst, nk) in [(p_mlstm, pm, 4), (p_moba, pa, 4), (w_out, wo, 8)]:
                sv = src.rearrange("(k p) c -> p k c", p=128)
                for k0 in range(0, nk, 4):
                    for c0 in range(0, D, 512):
                        sg = stg.next()
                        S.dma('sp', sg[:], sv[:, k0:k0 + 4, c0:c0 + 512], writes=[sg.key])
                        S.op('act', lambda e, sg=sg, dst=dst, k0=k0, c0=c0: e.activation(out=dst[:, k0:k0 + 4, c0:c0 + 512], in_=sg[:], func=AF.Copy), reads=[sg.key], writes=[dst.key])
            S.barrier()
        xsr = Rot([sb(st, "xsb%d" % i, [128, 8, 256], BF16) for i in range(2)])
        xtr = Rot([sb(st, "xtb%d" % i, [128, 8, 260]) for i in range(2)])
        ymr = Rot([sb(st, "ymb%d" % i, [128, 4, 256], BF16) for i in range(2)])
        sgar = Rot([sb(st, "sga%d" % i, [128, 256]) for i in range(2)])
        sgbr = Rot([sb(st, "sgb%d" % i, [128, 256]) for i in range(2)])
        m1r = Rot([sb(st, "m1_%d" % i, [128, 256]) for i in range(2)])
        m2r = Rot([sb(st, "m2_%d" % i, [128, 256]) for i in range(2)])
        mgr = Rot([sb(st, "mg%d" % i, [128, 8, 256], BF16) for i in range(2)])
        xTo_v = xTo.rearrange("(k p) t -> p k t", p=128)
        for i in range(NOWN):
            xs = xsr.next(); xt = xtr.next(); mg = mgr.next(); ymb = ymr.next()
            S.dma('sp', xs[:], XSs[i].rearrange("p (k t) -> p k t", k=8), reads=[('XSs', i)], writes=[xs.key])
            S.dma('sp', ymb[:], YMs[i].rearrange("p (c t) -> p c t", c=4), reads=[('YMs', i)], writes=[ymb.key])
            S.dma('sp', xt[:], xTo_v[:, :, i * 260:(i + 1) * 260], writes=[xt.key])
            tk = slice(i * 256, (i + 1) * 256)
            for cg in range(8):
                sga = sgar.next(); sgb = sgbr.next(); m1 = m1r.next(); m2 = m2r.next()
                for (wc, gi, dst_) in [(cg * 128, 8 + cg, sga), (1024 + cg * 128, 16 + cg, sgb)]:
                    ps = psm.next()
                    for kt in range(KT):
                        S.op('pe', lambda e, kt=kt, ps=ps, wc=wc: e.matmul(ps[:, 0:256], lhsT=WG[:, kt, wc:wc + 128], rhs=xs[:, kt, :], start=(kt == 0), stop=(kt == KT - 1)), reads=['WG', xs.key], writes=[ps.key])
                    S.op('act', lambda e, ps=ps, gi=gi, dst_=dst_: e.activation(out=dst_[:], in_=ps[:, 0:256], func=AF.Sigmoid, bias=biasc[:, gi:gi + 1]), reads=[ps.key, 'biasc'], writes=[dst_.key])
                for (pw, rhs_fn, rkey, sg_, m_) in [(pm, lambda c: ymb[:, c, :], ymb.key, sga, m1), (pa, lambda c: yaT[:, c, tk], 'yaT', sgb, m2)]:
                    ps = psm.next()
                    for c in range(4):
                        S.op('pe', lambda e, c=c, ps=ps, pw=pw, rhs_fn=rhs_fn: e.matmul(ps[:, 0:256], lhsT=pw[:, c, cg * 128:(cg + 1) * 128], rhs=rhs_fn(c), start=(c == 0), stop=(c == 3)), reads=[pw.key, rkey], writes=[ps.key])
                    S.op('dve', lambda e, ps=ps, sg_=sg_, m_=m_: e.tensor_tensor(out=m_[:], in0=sg_[:], in1=ps[:, 0:256], op=ALU.mult), reads=[ps.key, sg_.key], writes=[m_.key])
                S.op('pool', lambda e, cg=cg, m1=m1, m2=m2, mg=mg: e.tensor_tensor(out=mg[:, cg, :], in0=m1[:], in1=m2[:], op=ALU.add), reads=[m1.key, m2.key], writes=[mg.key])
            for og in range(8):
                ps = psm.next()
                for cg in range(8):
                    S.op('pe', lambda e, cg=cg, ps=ps, og=og: e.matmul(ps[:, 0:256], lhsT=wo[:, cg, og * 128:(og + 1) * 128], rhs=mg[:, cg, :], start=(cg == 0), stop=(cg == 7)), reads=['wo', mg.key], writes=[ps.key])
                S.op('dve', lambda e, ps=ps, og=og: e.scalar_tensor_tensor(out=xacc[:, og, tk], in0=ps[:, 0:256], scalar=modc[:, C_G1 + og:C_G1 + og + 1], in1=xt[:, og, 4:260], op0=ALU.mult, op1=ALU.add), reads=[ps.key, 'modc_b', xt.key], writes=['xacc'])
        dump("xmid", xacc[:].rearrange("p k t -> p (k t)"), [128, 8 * NOWN * 256], ['xacc'])
        S.barrier()
    OY.close()

    chk_stop('O3')
    NTOK = NOWN * 256
    TG = 512 if NTOK % 512 == 0 else 256
    with ExitStack() as st:
        h2T = sb(st, "h2T", [128, 8, NTOK], BF16)
        sqr = Rot([sb(st, "sq%d" % i, [128, 8, 260], BF16) for i in range(1)])
        lnr = Rot([sb(st, "ln%d" % i, [128, 260]) for i in range(2)])
        rsr = Rot([sb(st, "rs%d" % i, [128, 260]) for i in range(2)])
        for i in range(NOWN):
            tk = slice(i * 256, (i + 1) * 256)
            sq = sqr.next(); lnv = lnr.next(); rs = rsr.next()
            S.op('act', lambda e, sq=sq, tk=tk: e.activation(out=sq[:, :, 0:256], in_=xacc[:, :, tk], func=AF.Square), reads=['xacc'], writes=[sq.key])
            ps = psm.next()
            for kt in range(KT):
                S.op('pe', lambda e, kt=kt, ps=ps, sq=sq: e.matmul(ps[:, 0:256], lhsT=ones_b[:], rhs=sq[:, kt, 0:256], start=(kt == 0), stop=(kt == KT - 1)), reads=['ones_b', sq.key], writes=[ps.key])
            S.op('act', lambda e, ps=ps, lnv=lnv: e.activation(out=lnv[:, 0:256], in_=ps[:, 0:256], func=AF.Ln, scale=1.0 / D, bias=eps_t[:]), reads=[ps.key, 'eps_t'], writes=[lnv.key])
            S.op('act', lambda e, lnv=lnv, rs=rs: e.activation(out=rs[:, 0:256], in_=lnv[:, 0:256], func=AF.Exp, scale=-0.5), reads=[lnv.key], writes=[rs.key])
            S.op('dve', lambda e, rs=rs, tk=tk: e.tensor_tensor(out=h2T[:, :, tk], in0=xacc[:, :, tk], in1=rs[:, 0:256].unsqueeze(1).to_broadcast([128, 8, 256]), op=ALU.mult), reads=['xacc', rs.key], writes=['h2T'])
        FG = 2
        gstg = Rot([sb(st, "gstg%d" % i, [128, 8, FG * 128]) for i in range(2)])
        ustg = Rot([sb(st, "ustg%d" % i, [128, 8, FG * 128]) for i in range(2)])
        dstg = Rot([sb(st, "dstg%d" % i, [128, FG, D]) for i in range(2)])
        gwr = Rot([sb(st, "gw%d" % i, [128, 8, FG * 128], BF16) for i in range(2)])
        uwr = Rot([sb(st, "uw%d" % i, [128, 8, FG * 128], BF16) for i in range(2)])
        dwr = Rot([sb(st, "dw%d" % i, [128, FG, D], BF16) for i in range(2)])
        bgur = Rot([sb(st, "bgu%d" % i, [128, 2 * FG]) for i in range(2)])
        sgr = Rot([sb(st, "sgl%d" % i, [128, TG]) for i in range(2)])
        fTr = Rot([sb(st, "fT%d" % i, [128, FG, TG], BF16) for i in range(2)])
        wg_v = w_gate.rearrange("(k p) c -> p k c", p=128)
        wu_v = w_up.rearrange("(k p) c -> p k c", p=128)
        wd_v = w_down.rearrange("(f p) c -> p f c", p=128)
        for fp in range(NFC // FG):
            gs = gstg.next(); us = ustg.next(); ds = dstg.next(); gw = gwr.next(); uw = uwr.next(); dw = dwr.next(); bgu = bgur.next()
            c0 = fp * FG * 128
            S.dma('sp', gs[:], wg_v[:, :, c0:c0 + FG * 128], writes=[gs.key])
            S.dma('sp', us[:], wu_v[:, :, c0:c0 + FG * 128], writes=[us.key])
            S.dma('sp', ds[:], wd_v[:, fp * FG:(fp + 1) * FG, :], writes=[ds.key])
            for kt in range(KT):
                S.op('act', lambda e, kt=kt, gs=gs, gw=gw: e.activation(out=gw[:, kt, :], in_=gs[:, kt, :], func=AF.Copy, scale=gmod2[:, kt:kt + 1]), reads=[gs.key, 'gmod2'], writes=[gw.key])
                S.op('act', lambda e, kt=kt, us=us, uw=uw: e.activation(out=uw[:, kt, :], in_=us[:, kt, :], func=AF.Copy, scale=gmod2[:, kt:kt + 1]), reads=[us.key, 'gmod2'], writes=[uw.key])
            S.op('pool', lambda e, ds=ds, dw=dw: e.tensor_copy(out=dw[:], in_=ds[:]), reads=[ds.key], writes=[dw.key])
            bp = pss[1]
            for which, sg in enumerate([gs, us]):
                for f in range(FG):
                    col = 64 + which * FG + f
                    for kt in range(KT):
                        S.op('pe', lambda e, kt=kt, sg=sg, f=f, col=col: e.matmul(bp[:, col:col + 1], lhsT=sg[:, kt, f * 128:(f + 1) * 128], rhs=modc[:, C_SH2 + kt:C_SH2 + kt + 1], start=(kt == 0), stop=(kt == KT - 1)), reads=[sg.key, 'modc_b'], writes=[bp.key])
            S.op('dve', lambda e, bgu=bgu: e.tensor_copy(out=bgu[:], in_=bp[:, 64:64 + 2 * FG]), reads=[bp.key], writes=[bgu.key])
            for tg in range(NTOK // TG):
                tk = slice(tg * TG, (tg + 1) * TG)
                fT = fTr.next()
                for f in range(FG):
                    pg = psm.next(); pu = psm.next(); sgl = sgr.next()
                    for kt in range(KT):
                        S.op('pe', lambda e, kt=kt, pg=pg, f=f: e.matmul(pg[:, 0:TG], lhsT=gw[:, kt, f * 128:(f + 1) * 128], rhs=h2T[:, kt, tk], start=(kt == 0), stop=(kt == KT - 1)), reads=[gw.key, 'h2T'], writes=[pg.key])
                    for kt in range(KT):
                        S.op('pe', lambda e, kt=kt, pu=pu, f=f: e.matmul(pu[:, 0:TG], lhsT=uw[:, kt, f * 128:(f + 1) * 128], rhs=h2T[:, kt, tk], start=(kt == 0), stop=(kt == KT - 1)), reads=[uw.key, 'h2T'], writes=[pu.key])
                    S.op('act', lambda e, pg=pg, f=f, sgl=sgl: e.activation(out=sgl[:], in_=pg[:, 0:TG], func=AF.Silu, bias=bgu[:, f:f + 1]), reads=[pg.key, bgu.key], writes=[sgl.key])
                    S.op('dve', lambda e, pu=pu, f=f, sgl=sgl, fT=fT: e.scalar_tensor_tensor(out=fT[:, f, :], in0=pu[:, 0:TG], scalar=bgu[:, FG + f:FG + f + 1], in1=sgl[:], op0=ALU.add, op1=ALU.mult), reads=[pu.key, bgu.key, sgl.key], writes=[fT.key])
                for og in range(8):
                    ps = psm.next()
                    for f in range(FG):
                        S.op('pe', lambda e, f=f, ps=ps, og=og: e.matmul(ps[:, 0:TG], lhsT=dw[:, f, og * 128:(og + 1) * 128], rhs=fT[:, f, :], start=(f == 0), stop=(f == FG - 1)), reads=[dw.key, fT.key], writes=[ps.key])
                    S.op('dve', lambda e, ps=ps, og=og: e.scalar_tensor_tensor(out=xacc[:, og, tk], in0=ps[:, 0:TG], scalar=modc[:, C_G2 + og:C_G2 + og + 1], in1=xacc[:, og, tk], op0=ALU.mult, op1=ALU.add), reads=[ps.key, 'modc_b'], writes=['xacc'])
        otr = Rot([sb(st, "ot%d" % i, [128, 8, 256]) for i in range(1)])
        outT_v = outT.rearrange("(k p) t -> p k t", p=128)
        out_toks = []
        for i in range(NOWN):
            tk = slice(i * 256, (i + 1) * 256)
            sq = sqr.next(); lnv = lnr.next(); rs = rsr.next(); ot = otr.next()
            S.op('act', lambda e, sq=sq, tk=tk: e.activation(out=sq[:, :, 0:256], in_=xacc[:, :, tk], func=AF.Square), reads=['xacc'], writes=[sq.key])
            ps = psm.next()
            for kt in range(KT):
                S.op('pe', lambda e, kt=kt, ps=ps, sq=sq: e.matmul(ps[:, 0:256], lhsT=ones_b[:], rhs=sq[:, kt, 0:256], start=(kt == 0), stop=(kt == KT - 1)), reads=['ones_b', sq.key], writes=[ps.key])
            S.op('act', lambda e, ps=ps, lnv=lnv: e.activation(out=lnv[:, 0:256], in_=ps[:, 0:256], func=AF.Ln, scale=1.0 / D, bias=eps_t[:]), reads=[ps.key, 'eps_t'], writes=[lnv.key])
            S.op('act', lambda e, lnv=lnv, rs=rs: e.activation(out=rs[:, 0:256], in_=lnv[:, 0:256], func=AF.Exp, scale=-0.5), reads=[lnv.key], writes=[rs.key])
            for kt in range(KT):
                S.op('dve', lambda e, kt=kt, rs=rs, ot=ot, tk=tk: e.scalar_tensor_tensor(out=ot[:, kt, :], in0=xacc[:, kt, tk], scalar=nfg[:, kt:kt + 1], in1=rs[:, 0:256], op0=ALU.mult, op1=ALU.mult), reads=['xacc', 'nfg_t', rs.key], writes=[ot.key])
            out_toks.append(S.dma('sp', outT_v[:, :, tk], ot[:], reads=[ot.key], writes=[('outT', i)]))
        S.barrier()
    XA.close()
    G.close()
    return nc, dbg_outs, S


_CONST_CACHE = {}


def _prep_inputs(S_LEN, inp):
    NBLK = S_LEN // 256
    NOWN = NBLK // 4
    NT = S_LEN // 128
    f32 = np.float32
    x = np.asarray(inp['x'], f32); c = np.asarray(inp['c'], f32); pos = np.asarray(inp['positions'], np.int32)
    perm = np.concatenate([np.arange(512, 1024), np.arange(1024, 1536), np.arange(2568, 3080), np.arange(3080, 3592),
                           np.arange(2048, 2056), np.arange(0, 512), np.arange(1536, 2048), np.arange(2056, 2568),
                           np.arange(3592, 4616), np.arange(4616, 5640)])
    w_in_p = np.ascontiguousarray(np.asarray(inp['w_in'], f32)[0][:, perm])
    b_in_p = np.asarray(inp['b_in'], f32)[0][perm]
    fm_offs = [OFF['mk'] + 128 * h for h in range(4)] + [OFF['mq'] + 128 * h for h in range(4)] + [OFF['ga'] + 128 * g for g in range(8)] + [OFF['gb'] + 128 * g for g in range(8)]
    b_in_col = np.stack([b_in_p[o:o + 128] for o in fm_offs], axis=1).astype(f32)

    def colT(v):
        return np.ascontiguousarray(np.asarray(v, f32).reshape(8, 128).T)

    conv_w = np.asarray(inp['conv_w'], f32)[0]
    cw = np.zeros((128, 32), f32)
    for g in range(8):
        for j in range(4):
            cw[:, g * 4 + j] = conv_w[j, g * 128:(g + 1) * 128]
    cbv = np.ascontiguousarray(np.asarray(inp['conv_b'], f32)[0].reshape(8, 128).T)
    half = 32
    inv_freq = (10000.0 ** (-np.arange(half, dtype=np.float64) / half)) / (2 * np.pi)
    common = dict(
        ada_w=np.ascontiguousarray(np.asarray(inp['ada_w'], f32)[0]),
        ada_b_c=np.ascontiguousarray(np.asarray(inp['ada_b'], f32)[0].reshape(48, 128).T),
        n1g=colT(inp['norm1_g'][0]), n2g=colT(inp['norm2_g'][0]), nfg=colT(inp['normf_g']),
        w_in_p=w_in_p, b_in_row=np.ascontiguousarray(b_in_p[None, :]), b_in_col=np.ascontiguousarray(b_in_col),
        cw=cw, cb=cbv, mng_bc=np.ascontiguousarray(np.broadcast_to(np.asarray(inp['m_norm_g'], f32)[0][None, :], (128, 512))),
        p_mlstm=np.ascontiguousarray(np.asarray(inp['p_mlstm'], f32)[0]), p_moba=np.ascontiguousarray(np.asarray(inp['p_moba'], f32)[0]),
        w_out=np.ascontiguousarray(np.asarray(inp['w_out'], f32)[0]), w_gate=np.ascontiguousarray(np.asarray(inp['w_gate'], f32)[0]),
        w_up=np.ascontiguousarray(np.asarray(inp['w_up'], f32)[0]), w_down=np.ascontiguousarray(np.asarray(inp['w_down'], f32)[0]),
        ident=np.eye(128, dtype=f32), tri=np.triu(np.ones((128, 128), f32)), ones=np.ones((128, 128), f32),
        invf=np.ascontiguousarray(np.broadcast_to(inv_freq.astype(f32)[None, :], (128, 32))),
    )
    etab = np.zeros((128, S_LEN), f32)
    for n in range(NBLK):
        etab[64 + n, n * 256:(n + 1) * 256] = 1.0
    etab[96, :] = 1.0
    in_maps = []
    for core in range(8):
        b, j = core // 4, core % 4
        m = dict(common)
        m['etab'] = etab
        m['xTf'] = np.ascontiguousarray(x[b].T)
        xo = np.zeros((D, NOWN, 260), f32)
        po = np.zeros((128, NOWN * 2), np.int32)
        pastbias = np.zeros((128, NOWN, 32), f32)
        hvv = np.ones((128, NOWN), f32)
        for i in range(NOWN):
            g = 4 * i + j
            s0 = g * 256
            xo[:, i, 4:260] = x[b, s0:s0 + 256].T
            if s0 > 0:
                xo[:, i, 0:4] = x[b, s0 - 4:s0].T
            else:
                hvv[:, i] = 0.0
            for t in range(2):
                po[:, i * 2 + t] = pos[b, s0 + t * 128:s0 + (t + 1) * 128]
            pastbias[:, i, g:] = -1e30
        m['xTo'] = np.ascontiguousarray(xo.reshape(D, NOWN * 260))
        m['cT'] = colT(c[b])
        m['posf'] = np.ascontiguousarray(pos[b].reshape(NT, 128).T)
        m['poso'] = po
        m['pastbias'] = np.ascontiguousarray(pastbias.reshape(128, NOWN * 32))
        m['hv'] = hvv
        cs = np.zeros((128, 4), f32); cs[:, j] = 1.0
        m['capsel'] = cs
        in_maps.append(m)
    return in_maps


def run(inp, S_LEN, dbg=None, stop=None):
    NBLK = S_LEN // 256
    NOWN = NBLK // 4
    nc, dbg_outs, S = build_program(S_LEN, dbg, stop)
    in_maps = _prep_inputs(S_LEN, inp)
    res = run_bass_kernel_spmd(nc, in_maps, core_ids=list(range(8)))
    B = 2
    out = np.zeros((B, S_LEN, D), np.float32)
    for core in range(8):
        b, j = core // 4, core % 4
        oT = np.asarray(res.results[core]["outT"])
        for i in range(NOWN):
            g = 4 * i + j
            out[b, g * 256:(g + 1) * 256, :] = oT[:, i * 256:(i + 1) * 256].T
    dbgres = {}
    for name in dbg_outs:
        dbgres[name] = [np.asarray(res.results[core]["dbg_" + name]) for core in range(8)]
    return out, dbgres


def kernel(**inputs):
    S_LEN = int(np.asarray(inputs['x']).shape[1])
    out, _ = run(inputs, S_LEN)
    return out
```

```python
import math
from contextlib import ExitStack
import numpy as np
import concourse.bass as bass
import concourse.mybir as mybir
from concourse.bass_utils import run_bass_kernel_spmd

F32 = mybir.dt.float32
BF16 = mybir.dt.bfloat16
I32 = mybir.dt.int32
AF = mybir.ActivationFunctionType
ALU = mybir.AluOpType
AX = mybir.AxisListType

D = 1024
KT = 8
DFF = 2816
NFC = 22
DIN = 5640
OFF = dict(mk=0, mv=512, ak=1024, av=1536, mif=2048, mq=2056, mo=2568, aq=3080, ga=3592, gb=4616)
NF = 2056
EPS = 1e-6
NEG = -30000.0


import os as _os
LAT_PE = float(_os.environ.get('KLATPE', '250'))
LAT_X = float(_os.environ.get('KLATX', '300'))


class _Proxy:
    def __init__(self):
        self.call = None

    def __getattr__(self, name):
        def f(*a, **k):
            self.call = (name, a, k)
            return self
        return f


def _free_size(ap):
    try:
        sh = list(ap.shape)
        n = 1
        for v in sh[1:]:
            n *= int(v)
        return n
    except Exception:
        return 256


class Sched:
    def __init__(self, nc, n_dma_sems=8):
        self.nc = nc
        self.eng = {'pe': nc.tensor, 'act': nc.scalar, 'dve': nc.vector, 'pool': nc.gpsimd, 'sp': nc.sync}
        self.csem = {e: nc.alloc_semaphore("c_" + e) for e in ['pe', 'act', 'dve', 'pool']}
        self.ccnt = {e: 0 for e in self.csem}
        self.P = n_dma_sems
        self.dsem = {q: [nc.alloc_semaphore("d_%s%d" % (q, i)) for i in range(n_dma_sems)] for q in ['sp', 'pool']}
        self.dcnt = {q: 0 for q in self.dsem}
        self.known = {e: {} for e in self.eng}
        self.sems = {}
        for e in self.csem:
            self.sems["c_" + e] = self.csem[e]
        for q in self.dsem:
            for i in range(n_dma_sems):
                self.sems["d_%s%d" % (q, i)] = self.dsem[q][i]
        self.lastw = {}
        self.readers = {}
        self.nwait = 0
        self.nop = 0
        self.dead = False
        self.rec = []
        self.alias = {}
        import os
        self.reorder = os.environ.get('KREORDER', '1') == '1'

    def _res(self, keys):
        return [self.alias.get(k, k) if isinstance(k, str) else k for k in keys]

    def op(self, e, fn, reads=(), writes=(), n=None):
        if self.dead:
            return None
        px = _Proxy()
        fn(px)
        name, a, k = px.call
        if n is None:
            if name == 'matmul':
                n = _free_size(k.get('rhs', a[2] if len(a) > 2 else None))
            elif name == 'transpose':
                n = 128
            else:
                o = k.get('out', a[0] if a else None)
                n = _free_size(o)
        if e == 'pe':
            fp32 = False
            try:
                fp32 = (name == 'matmul' and k['rhs'].dtype == F32)
            except Exception:
                pass
            cost = (max(64, n) / 2.2 + 35) * (4 if fp32 else 1)
            lat = LAT_PE
        elif e == 'act':
            cost = 230 + n / 1.3
            lat = LAT_X
        elif e == 'dve':
            cost = 200 + n / 0.9
            lat = LAT_X
        else:
            cost = 350 + n / 0.55
            lat = LAT_X
        self.rec.append(dict(kind='op', e=e, call=(name, a, k), reads=self._res(reads), writes=self._res(writes), cost=cost, lat=lat))
        return None

    def dma(self, q, out, in_, reads=(), writes=()):
        if self.dead:
            return None
        self.rec.append(dict(kind='dma', e=q, call=(out, in_), reads=self._res(reads), writes=self._res(writes), cost=(120 if q == 'sp' else 700), lat=3000))
        return None

    def flush(self):
        rec = self.rec
        self.rec = []
        N = len(rec)
        if N == 0:
            return
        order = list(range(N))
        if self.reorder:
            lastw, readers = {}, {}
            preds = [set() for _ in range(N)]
            for i, r in enumerate(rec):
                for k in r['reads']:
                    if k in lastw:
                        preds[i].add(lastw[k])
                for k in r['writes']:
                    if k in lastw:
                        preds[i].add(lastw[k])
                    for j in readers.get(k, ()):
                        preds[i].add(j)
                preds[i].discard(i)
                for k in r['reads']:
                    if k not in r['writes']:
                        readers.setdefault(k, []).append(i)
                for k in r['writes']:
                    lastw[k] = i
                    readers[k] = []
            succ = [[] for _ in range(N)]
            indeg = [0] * N
            for i in range(N):
                indeg[i] = len(preds[i])
                for p in preds[i]:
                    succ[p].append(i)
            import heapq
            efree = {}
            fin = [0.0] * N
            rdy_t = [0.0] * N
            ready = {}
            for i in range(N):
                if indeg[i] == 0:
                    heapq.heappush(ready.setdefault(rec[i]['e'], []), (0.0, i))
            order = []
            WIN = 6
            while len(order) < N:
                best = None
                for e, hp in ready.items():
                    if not hp:
                        continue
                    cands = heapq.nsmallest(WIN, hp)
                    ef = efree.get(e, 0.0)
                    for (rt_, i) in cands:
                        st_ = max(ef, rt_)
                        key = (st_, i)
                        if best is None or key < best[0]:
                            best = (key, e, (rt_, i))
                (st_, i), e, item = best
                ready[e].remove(item)
                heapq.heapify(ready[e])
                r = rec[i]
                efree[e] = st_ + r['cost']
                fin[i] = st_ + r['cost'] + r['lat']
                order.append(i)
                for s_ in succ[i]:
                    indeg[s_] -= 1
                    same = (rec[s_]['e'] == e and e == 'pe')
                    t_ = (st_ + r['cost']) if same else fin[i]
                    if t_ > rdy_t[s_]:
                        rdy_t[s_] = t_
                    if indeg[s_] == 0:
                        heapq.heappush(ready.setdefault(rec[s_]['e'], []), (rdy_t[s_], s_))
        for i in order:
            r = rec[i]
            if r['kind'] == 'op':
                self._emit_op(r['e'], r['call'], r['reads'], r['writes'])
            else:
                self._emit_dma(r['e'], r['call'][0], r['call'][1], r['reads'], r['writes'])

    def _wait(self, e, toks):
        need = {}
        for (sname, val, prod) in toks:
            if e == 'pe' and prod == 'pe':
                continue
            if self.known[e].get(sname, 0) >= val:
                continue
            if need.get(sname, 0) < val:
                need[sname] = val
        for sname, val in need.items():
            self.eng[e].wait_ge(self.sems[sname], val)
            self.known[e][sname] = val
            self.nwait += 1

    def _deps(self, reads, writes):
        toks = []
        for k in reads:
            t = self.lastw.get(k)
            if t is not None:
                toks.append(t)
        for k in writes:
            t = self.lastw.get(k)
            if t is not None:
                toks.append(t)
            toks += self.readers.get(k, [])
        return toks

    def _commit(self, tok, reads, writes):
        for k in reads:
            if k not in writes:
                self.readers.setdefault(k, []).append(tok)
        for k in writes:
            self.lastw[k] = tok
            self.readers[k] = []

    def _emit_op(self, e, call, reads, writes):
        self._wait(e, self._deps(reads, writes))
        name, a, k = call
        inst = getattr(self.eng[e], name)(*a, **k)
        self.ccnt[e] += 1
        inst.then_inc(self.csem[e], 1)
        tok = ("c_" + e, self.ccnt[e], e)
        self._commit(tok, reads, writes)
        self.nop += 1

    def _emit_dma(self, q, out, in_, reads, writes):
        n = self.dcnt[q]
        s = self.dsem[q][n % self.P]
        sname = "d_%s%d" % (q, n % self.P)
        toks = self._deps(reads, writes)
        if n >= self.P:
            toks.append((sname, 16 * (n // self.P), 'dma'))
        self._wait(q, toks)
        self.eng[q].dma_start(out=out, in_=in_).then_inc(s, 16)
        self.dcnt[q] += 1
        tok = (sname, 16 * (n // self.P + 1), 'dma')
        self._commit(tok, reads, writes)
        self.nop += 1

    def all_tokens(self):
        toks = [("c_" + e, self.ccnt[e], e) for e in self.csem if self.ccnt[e] > 0]
        for q in self.dsem:
            for idx in range(self.P):
                if self.dcnt[q] > idx:
                    uses = (self.dcnt[q] - idx + self.P - 1) // self.P
                    toks.append(("d_%s%d" % (q, idx), 16 * uses, 'dma'))
        return toks

    def barrier(self, engines=None):
        if self.dead:
            return
        self.flush()
        toks = self.all_tokens()
        for e in (engines or list(self.eng.keys())):
            self._wait(e, toks)
        self.lastw = {}
        self.readers = {}


class TL:
    def __init__(self, t, key):
        self.t = t
        self.key = key

    def __getitem__(self, i):
        return self.t[i]


class RT:
    def __init__(self, S, name, tiles):
        self.S = S
        self.name = name
        self.tiles = tiles
        self.i = 0
        for k, t in enumerate(tiles):
            t.key = "%s#%d" % (name, k)
        S.alias[name] = tiles[0].key

    def rotate(self):
        self.i += 1
        self.S.alias[self.name] = self.tiles[self.i % len(self.tiles)].key

    @property
    def key(self):
        return self.tiles[self.i % len(self.tiles)].key

    def __getitem__(self, i):
        return self.tiles[self.i % len(self.tiles)].t[i]


class Rot:
    def __init__(self, tiles):
        self.tiles = tiles
        self.i = 0

    def next(self):
        t = self.tiles[self.i % len(self.tiles)]
        self.i += 1
        return t


def build_program(S_LEN, dbg=None, stop=None):
    NBLK = S_LEN // 256
    NOWN = NBLK // 4
    NT = S_LEN // 128
    NTO = NOWN * 2
    nc = bass.Bass("TRN2", target_bir_lowering=False)
    S = Sched(nc)
    dbg = dbg or []
    dbg_outs = {}

    def din(name, shape, dt=F32):
        return nc.dram_tensor(name, shape, dt, kind="ExternalInput").ap()

    xTf = din("xTf", [D, S_LEN])
    xTo = din("xTo", [D, NOWN * 260])
    cT_d = din("cT", [128, 8])
    posf_d = din("posf", [128, NT], I32)
    poso_d = din("poso", [128, NTO], I32)
    ada_w = din("ada_w", [D, 6 * D])
    ada_b_c = din("ada_b_c", [128, 48])
    n1g_d = din("n1g", [128, 8])
    n2g_d = din("n2g", [128, 8])
    nfg_d = din("nfg", [128, 8])
    w_in_p = din("w_in_p", [D, DIN])
    b_in_row = din("b_in_row", [1, DIN])
    b_in_col = din("b_in_col", [128, 24])
    cw_d = din("cw", [128, 32])
    cb_d = din("cb", [128, 8])
    mng_d = din("mng_bc", [128, 512])
    p_mlstm = din("p_mlstm", [512, D])
    p_moba = din("p_moba", [512, D])
    w_out = din("w_out", [D, D])
    w_gate = din("w_gate", [D, DFF])
    w_up = din("w_up", [D, DFF])
    w_down = din("w_down", [DFF, D])
    ident_d = din("ident", [128, 128])
    tri_d = din("tri", [128, 128])
    ones_d = din("ones", [128, 128])
    invf_d = din("invf", [128, 32])
    pastb_d = din("pastbias", [128, NOWN * 32])
    hv_d = din("hv", [128, NOWN])
    capsel_d = din("capsel", [128, 4])
    etab_d = din("etab", [128, S_LEN])
    outT = nc.dram_tensor("outT", [D, NOWN * 256], F32, kind="ExternalOutput").ap()

    KTs = nc.dram_tensor("KTs", [512, S_LEN], BF16, kind="Internal").ap()
    Vs = nc.dram_tensor("Vs", [8, 128, NT * 65], BF16, kind="Internal").ap()
    QTs = nc.dram_tensor("QTs", [8, 128, NOWN * 256], BF16, kind="Internal").ap()
    KTos = nc.dram_tensor("KTos", [8, 128, NOWN * 256], BF16, kind="Internal").ap()
    XSs = nc.dram_tensor("XSs", [NOWN, 128, 8 * 256], BF16, kind="Internal").ap()
    CCs = nc.dram_tensor("CCs", [NOWN, 128, 4 * 129], F32, kind="Internal").ap()
    Vos = nc.dram_tensor("Vos", [8, 128, NTO * 65], BF16, kind="Internal").ap()
    YMs = nc.dram_tensor("YMs", [NOWN, 128, 4 * 256], BF16, kind="Internal").ap()

    used_names = {}

    def uniq(name):
        k = used_names.get(name, 0)
        used_names[name] = k + 1
        return name if k == 0 else "%s_r%d" % (name, k)

    def sb(st, name, shape, dt=F32):
        return TL(st.enter_context(nc.sbuf_tensor(uniq(name), shape, dt)), name)

    def rt(st, name, shape, dt=F32, k=2):
        import os
        if os.environ.get('KRT', '1') == '0':
            k = 1
        return RT(S, name, [sb(st, "%s_b%d" % (name, i), shape, dt) for i in range(k)])

    def pst(st, name, shape, dt=F32):
        return TL(st.enter_context(nc.psum_tensor(uniq(name), shape, dt)), name)

    def dump(name, ap, shape, keys, is_bf16=False):
        if name not in dbg:
            return
        o = nc.dram_tensor("dbg_" + name, shape, F32, kind="ExternalOutput").ap()
        dbg_outs[name] = shape
        if S.dead:
            return
        with nc.sbuf_tensor("dbgt_" + name, shape, F32) as tmp:
            S.op('dve', lambda e: e.tensor_copy(out=tmp[:], in_=ap), reads=keys, writes=['dbgt_' + name])
            S.dma('sp', o, tmp[:], reads=['dbgt_' + name], writes=['dbgo_' + name])
            S.barrier()

    def chk_stop(tag):
        if stop == tag:
            S.barrier()
            S.dead = True

    G = ExitStack()
    KA = int(_os.environ.get('KA', '4')); KT_ = int(_os.environ.get('KTR', '1'))
    NF32 = 8 - KT_ - 1
    pf = [pst(G, "psm%d" % i, [128, 512]) for i in range(NF32)]
    pstr = Rot([pst(G, "pstr%d" % i, [128, 1024], BF16) for i in range(KT_)])
    pss = [pf[-1], pst(G, "pss1", [128, 512])]
    psm = Rot(pf[0:KA])
    psb = Rot(pf[KA:])
    psmo = Rot(pf[:-1])
    assert len(psb.tiles) >= 2, 'the mLSTM numerator needs two distinct long-lived banks'

    ident_f = sb(G, "ident_f", [128, 128]); tri_f = sb(G, "tri_f", [128, 128]); ones_f = sb(G, "ones_f", [128, 128])
    ident_b = sb(G, "ident_b", [128, 128], BF16); tri_b = sb(G, "tri_b", [128, 128], BF16); ones_b = sb(G, "ones_b", [128, 128], BF16)
    for (d_, f_, b_) in [(ident_d, ident_f, ident_b), (tri_d, tri_f, tri_b), (ones_d, ones_f, ones_b)]:
        S.dma('sp', f_[:], d_, writes=[f_.key])
        S.op('dve', lambda e, f_=f_, b_=b_: e.tensor_copy(out=b_[:], in_=f_[:]), reads=[f_.key], writes=[b_.key])
    eps_t = sb(G, "eps_t", [128, 1]); one_t = sb(G, "one_t", [128, 1])
    S.op('pool', lambda e: e.memset(eps_t[:], EPS), writes=['eps_t'])
    S.op('pool', lambda e: e.memset(one_t[:], 1.0), writes=['one_t'])

    def load_const(name, src, shape, dt=F32):
        t = sb(G, name, shape, dt)
        S.dma('sp', t[:], src, writes=[name])
        return t

    c_t = load_const("c_t", cT_d, [128, 8])
    adab = load_const("adab", ada_b_c, [128, 48])
    n1g = load_const("n1g_t", n1g_d, [128, 8]); n2g = load_const("n2g_t", n2g_d, [128, 8]); nfg = load_const("nfg_t", nfg_d, [128, 8])
    bincol = load_const("bincol", b_in_col, [128, 24])
    biasc = sb(G, "biasc", [128, 24])
    kmaxbc = sb(G, "kmaxbc", [128, 8])
    modc = sb(G, "modc", [128, 48])
    gmod1 = sb(G, "gmod1", [128, 8]); gmod2 = sb(G, "gmod2", [128, 8])
    sc = sb(G, "sc", [128, 8])
    G1 = ExitStack()
    G_save = G
    G = G1
    cw = load_const("cw_t", cw_d, [128, 32]); cb = load_const("cb_t", cb_d, [128, 8])
    mng = load_const("mng_t", mng_d, [128, 512])
    invf = load_const("invf_t", invf_d, [128, 32])
    pastb = load_const("pastb_t", pastb_d, [128, NOWN * 32])
    hv = load_const("hv_t", hv_d, [128, NOWN])
    capsel = load_const("capsel_t", capsel_d, [128, 4])
    sh1bc = sb(G1, "sh1bc", [128, 8, 128])
    bias_bc = sb(G1, "bias_bc", [128, 2568])
    kmT = sb(G1, "kmT", [128, 4, 2, 32], BF16)
    dgw = sb(G1, "dgw", [128, 8, 4, 128], BF16)
    ncb = sb(G1, "ncb", [128, 8]); lnq_t = sb(G1, "lnq_t", [128, 1]); zero_t = sb(G1, "zero_t", [128, 1])
    G = G_save

    for g_ in range(8):
        for j_ in range(4):
            S.op('act', lambda e, g_=g_, j_=j_: e.activation(out=dgw[:, g_, j_, :], in_=ident_f[:], func=AF.Copy, scale=cw[:, g_ * 4 + j_:g_ * 4 + j_ + 1]), reads=['ident_f', 'cw_t'], writes=['dgw'])
    S.op('dve', lambda e: e.tensor_scalar(out=ncb[:], in0=cb[:], scalar1=-1.0, scalar2=None, op0=ALU.mult), reads=['cb_t'], writes=['ncb'])
    S.op('pool', lambda e: e.memset(lnq_t[:], math.log(128.0 ** -0.5)), writes=['lnq_t'])
    S.op('pool', lambda e: e.memset(zero_t[:], 0.0), writes=['zero_t'])
    S.op('act', lambda e: e.activation(out=sc[:], in_=c_t[:], func=AF.Silu), reads=['c_t'], writes=['sc'])
    adaw_v = ada_w.rearrange("(k p) c -> p k c", p=128)

    def mod_groups(awr, g0, g1, key):
        for g in range(g0, g1):
            aw = awr.next()
            ps = psb.next()
            S.dma('sp', aw[:], adaw_v[:, :, g * 128:(g + 1) * 128], writes=[aw.key])
            for kt in range(KT):
                S.op('pe', lambda e, aw=aw, kt=kt, ps=ps: e.matmul(ps[:, 0:1], lhsT=aw[:, kt, :], rhs=sc[:, kt:kt + 1], start=(kt == 0), stop=(kt == KT - 1)),
                     reads=[aw.key, 'sc'], writes=[ps.key])
            S.op('dve', lambda e, g=g, ps=ps: e.tensor_tensor(out=modc[:, g:g + 1], in0=ps[:, 0:1], in1=adab[:, g:g + 1], op=ALU.add), reads=[ps.key, 'adab'], writes=[key])

    with ExitStack() as st:
        awr = Rot([sb(st, "aw%d" % i, [128, 8, 128]) for i in range(3)])
        mod_groups(awr, 0, 16, 'modc')
        S.op('dve', lambda e: e.scalar_tensor_tensor(out=gmod1[:], in0=modc[:, 8:16], scalar=1.0, in1=n1g[:], op0=ALU.add, op1=ALU.mult), reads=['modc', 'n1g_t'], writes=['gmod1'])
        S.barrier()
    C_SH1, C_G1, C_SH2, C_G2 = 0, 16, 24, 40
    dump("modc", modc[:], [128, 48], ['modc'])
    chk_stop('S')

    for kt in range(KT):
        S.op('act', lambda e, kt=kt: e.activation(out=sh1bc[:, kt, :], in_=ones_f[:], func=AF.Copy, scale=modc[:, C_SH1 + kt:C_SH1 + kt + 1]), reads=['ones_f', 'modc'], writes=['sh1bc'])

    def bbc(c):
        return c - 512 if c < NF else 1544 + (c - OFF['mo'])

    def prep_win(st, c0, n, dst, d0, kind, gi0=None):
        stg = Rot([sb(st, "wstg%d" % i, [128, 8, 256]) for i in range(2)])
        brow = Rot([sb(st, "brow%d" % i, [1, 256]) for i in range(2)])
        wv = w_in_p.rearrange("(k p) c -> p k c", p=128)
        for p0 in range(0, n, 256):
            pn = min(256, n - p0)
            sg = stg.next()
            S.dma('sp', sg[:, :, 0:pn], wv[:, :, c0 + p0:c0 + p0 + pn], writes=[sg.key])
            for kt in range(KT):
                S.op('act', lambda e, sg=sg, kt=kt, p0=p0, pn=pn: e.activation(out=dst[:, kt, d0 + p0:d0 + p0 + pn], in_=sg[:, kt, 0:pn], func=AF.Copy, scale=gmod1[:, kt:kt + 1]),
                     reads=[sg.key, 'gmod1'], writes=[dst.key])
            if kind == 'tm':
                ps = psm.next()
                br = brow.next()
                S.dma('sp', br[0:1, 0:pn], b_in_row[0:1, c0 + p0:c0 + p0 + pn], writes=[br.key])
                for kt in range(KT):
                    S.op('pe', lambda e, ps=ps, sg=sg, kt=kt, pn=pn: e.matmul(ps[:, 0:pn], lhsT=sh1bc[:, kt, :], rhs=sg[:, kt, 0:pn], start=(kt == 0), stop=False),
                         reads=['sh1bc', sg.key], writes=[ps.key])
                S.op('pe', lambda e, ps=ps, br=br, pn=pn: e.matmul(ps[:, 0:pn], lhsT=ones_f[0:1, :], rhs=br[0:1, 0:pn], start=False, stop=True),
                     reads=['ones_f', br.key], writes=[ps.key])
                b0 = bbc(c0 + p0)
                S.op('dve', lambda e, ps=ps, b0=b0, pn=pn: e.tensor_copy(out=bias_bc[:, b0:b0 + pn], in_=ps[:, 0:pn]), reads=[ps.key], writes=['bias_bc'])
            else:
                for gg in range(pn // 128):
                    gi = gi0 + (p0 // 128) + gg
                    ps = pss[1]
                    for kt in range(KT):
                        S.op('pe', lambda e, sg=sg, kt=kt, gg=gg, gi=gi: e.matmul(pss[1][:, gi:gi + 1], lhsT=sg[:, kt, gg * 128:(gg + 1) * 128], rhs=modc[:, C_SH1 + kt:C_SH1 + kt + 1], start=(kt == 0), stop=(kt == KT - 1)),
                             reads=[sg.key, 'modc'], writes=[ps.key])
                    S.op('dve', lambda e, gi=gi: e.tensor_tensor(out=biasc[:, gi:gi + 1], in0=pss[1][:, gi:gi + 1], in1=bincol[:, gi:gi + 1], op=ALU.add), reads=[pss[1].key, 'bincol'], writes=['biasc'])

    def front(xt, N, xs, sqr, lnr, rsr):
        sq = sqr.next(); lnv = lnr.next(); rs = rsr.next()
        S.op('act', lambda e: e.activation(out=sq[:, :, 0:N], in_=xt[:, :, 0:N], func=AF.Square), reads=[xt.key], writes=[sq.key])
        ps = psm.next()
        for kt in range(KT):
            S.op('pe', lambda e, kt=kt: e.matmul(ps[:, 0:N], lhsT=ones_b[:], rhs=sq[:, kt, 0:N], start=(kt == 0), stop=(kt == KT - 1)), reads=['ones_b', sq.key], writes=[ps.key])
        S.op('act', lambda e: e.activation(out=lnv[:, 0:N], in_=ps[:, 0:N], func=AF.Ln, scale=1.0 / D, bias=eps_t[:]), reads=[ps.key, 'eps_t'], writes=[lnv.key])
        S.op('act', lambda e: e.activation(out=rs[:, 0:N], in_=lnv[:, 0:N], func=AF.Exp, scale=-0.5), reads=[lnv.key], writes=[rs.key])
        S.op('dve', lambda e: e.tensor_tensor(out=xs[:, 0:4, 0:N], in0=xt[:, 0:4, 0:N], in1=rs[:, 0:N].unsqueeze(1).to_broadcast([128, 4, N]), op=ALU.mult), reads=[xt.key, rs.key], writes=[xs.key + "a"])
        for kt in range(4, 8):
            S.op('pool', lambda e, kt=kt: e.tensor_tensor(out=xs[:, kt, 0:N], in0=xt[:, kt, 0:N], in1=rs[:, 0:N], op=ALU.mult), reads=[xt.key, rs.key], writes=[xs.key + "b"])
        return rs

    def xskeys(xs):
        return [xs.key + "a", xs.key + "b"]

    def wsl(c0, n, WF, WO):
        if c0 < NF:
            return WF, c0
        return WO, c0 - NF

    def proj_fm(ps, W, wc, xs, x0, N):
        for kt in range(KT):
            S.op('pe', lambda e, kt=kt: e.matmul(ps[:, 0:N], lhsT=W[:, kt, wc:wc + 128], rhs=xs[:, kt, x0:x0 + N], start=(kt == 0), stop=(kt == KT - 1)),
                 reads=[W.key] + xskeys(xs), writes=[ps.key])

    def proj_tm(psap, pskey, W, wc, ncol, xs, x0):
        for kt in range(KT):
            S.op('pe', lambda e, kt=kt: e.matmul(psap, lhsT=xs[:, kt, x0:x0 + 128], rhs=W[:, kt, wc:wc + ncol], start=(kt == 0), stop=(kt == KT - 1)),
                 reads=[W.key] + xskeys(xs), writes=[pskey])

    def rope(kx, out3, okey, cos2, sinp, sinn, T, tA, tB):
        c2 = cos2[:, T, :].unsqueeze(1).to_broadcast([128, 8, 64])
        sp_ = sinp[:, T, :].unsqueeze(1).to_broadcast([128, 8, 32])
        sn_ = sinn[:, T, :].unsqueeze(1).to_broadcast([128, 8, 32])
        S.op('dve', lambda e: e.tensor_tensor(out=tA[:], in0=kx[:], in1=c2, op=ALU.mult), reads=[kx.key, cos2.key], writes=[tA.key])
        S.op('dve', lambda e: e.tensor_tensor(out=tB[:, :, 0:32], in0=kx[:, :, 32:64], in1=sn_, op=ALU.mult), reads=[kx.key, sinn.key], writes=[tB.key + "l"])
        S.op('dve', lambda e: e.tensor_tensor(out=tB[:, :, 32:64], in0=kx[:, :, 0:32], in1=sp_, op=ALU.mult), reads=[kx.key, sinp.key], writes=[tB.key + "h"])
        S.op('dve', lambda e: e.tensor_tensor(out=out3, in0=tA[:], in1=tB[:], op=ALU.add), reads=[tA.key, tB.key + "l", tB.key + "h"], writes=[okey])

    gcnt = [0]

    def gates(gif, t, gt):
        l, u, w, dec, tmp = gt['l'], gt['u'], gt['w'], gt['dec'], gt['tmp']
        r_ = gcnt[0] % 4
        gcnt[0] += 1
        gp = TL(pss[1].t[:, 64 * r_:64 * r_ + 64], "pss1")
        S.op('act', lambda e: e.activation(out=tmp[:], in_=gif[:, t, 4:8], func=AF.Exp, scale=-1.0), reads=[gif.key], writes=[tmp.key])
        S.op('act', lambda e: e.activation(out=l[:], in_=tmp[:], func=AF.Ln, bias=one_t[:]), reads=[tmp.key, 'one_t'], writes=[l.key])
        S.op('pe', lambda e: e.matmul(gp[:, 32:36], lhsT=ones_f[:], rhs=l[:], start=True, stop=True), reads=['ones_f', l.key], writes=[gp.key])
        S.op('pe', lambda e: e.matmul(gp[:, 36:40], lhsT=tri_f[:], rhs=l[:], start=True, stop=True), reads=['tri_f', l.key], writes=[gp.key])
        S.op('dve', lambda e: e.tensor_tensor(out=u[:], in0=gp[:, 36:40], in1=gif[:, t, 0:4], op=ALU.add), reads=[gp.key, gif.key], writes=[u.key])
        S.op('dve', lambda e: e.tensor_tensor(out=tmp[:], in0=u[:], in1=gp[:, 32:36], op=ALU.subtract), reads=[gp.key, u.key], writes=[tmp.key])
        S.op('act', lambda e: e.activation(out=w[:], in_=tmp[:], func=AF.Exp), reads=[tmp.key], writes=[w.key])
        S.op('act', lambda e: e.activation(out=dec[:], in_=gp[:, 32:36], func=AF.Exp, scale=-1.0), reads=[gp.key], writes=[dec.key])

    def state_update(vaug, ktm, t, gt, CT, wvr):
        wv = wvr.next()
        for h in range(4):
            S.op('act', lambda e, h=h: e.activation(out=wv[:, h, :], in_=vaug[:, t, h, :], func=AF.Copy, scale=gt['w'][:, h:h + 1]), reads=[vaug.key, gt['w'].key], writes=[wv.key])
        for hh in range(2):
            ps = psb.next()
            for h2 in range(2):
                h = hh * 2 + h2
                S.op('pe', lambda e, h=h, h2=h2, ps=ps: e.matmul(ps[:, h2 * 129:(h2 + 1) * 129], lhsT=ktm[:, t, h, :], rhs=wv[:, h, :], start=True, stop=True), reads=[ktm.key, wv.key], writes=[ps.key])
            for h2 in range(2):
                h = hh * 2 + h2
                S.op('dve', lambda e, h=h, h2=h2, ps=ps: e.scalar_tensor_tensor(out=CT[:, h, :], in0=CT[:, h, :], scalar=gt['dec'][:, h:h + 1], in1=ps[:, h2 * 129:(h2 + 1) * 129], op0=ALU.mult, op1=ALU.add),
                     reads=[ps.key, gt['dec'].key], writes=[CT.key])

    FO = ExitStack()
    WF = sb(FO, "WF", [128, 8, NF], BF16)
    Ccur = sb(FO, "Ccur", [128, 4, 129])
    S.op('pool', lambda e: e.memset(kmT[:], 0.0), writes=['kmT'])
    with ExitStack() as st:
        prep_win(st, OFF['mk'], 512, WF, OFF['mk'], 'fm', gi0=0)
        prep_win(st, OFF['mv'], 1544, WF, OFF['mv'], 'tm')
        S.barrier()

    def rope_tables(tb, pos_ap, NTB):
        S.dma('sp', tb['pi'][:], pos_ap, writes=[tb['pi'].key])
        S.op('dve', lambda e: e.tensor_copy(out=tb['pf'][:], in_=tb['pi'][:]), reads=[tb['pi'].key], writes=[tb['pf'].key])
        v, vi, vf = tb['v'], tb['vi'], tb['vf']
        S.op('dve', lambda e: e.tensor_tensor(out=v[:], in0=invf[:].unsqueeze(1).to_broadcast([128, NTB, 32]), in1=tb['pf'][:].unsqueeze(2).to_broadcast([128, NTB, 32]), op=ALU.mult),
             reads=['invf_t', tb['pf'].key], writes=[v.key])
        for which in ['sin', 'cos']:
            if which == 'cos':
                S.op('dve', lambda e: e.tensor_scalar(out=v[:], in0=v[:], scalar1=0.25, scalar2=None, op0=ALU.add), reads=[], writes=[v.key])
            S.op('dve', lambda e: e.tensor_copy(out=vi[:], in_=v[:]), reads=[v.key], writes=[vi.key])
            S.op('dve', lambda e: e.tensor_copy(out=vf[:], in_=vi[:]), reads=[vi.key], writes=[vf.key])
            S.op('dve', lambda e: e.tensor_tensor(out=vf[:], in0=v[:], in1=vf[:], op=ALU.subtract), reads=[v.key], writes=[vf.key])
            if which == 'sin':
                S.op('act', lambda e: e.activation(out=tb['sinp'][:], in_=vf[:], func=AF.Sin, scale=2 * math.pi), reads=[vf.key], writes=[tb['sinp'].key])
                S.op('dve', lambda e: e.tensor_scalar(out=tb['sinn'][:], in0=tb['sinp'][:], scalar1=-1.0, scalar2=None, op0=ALU.mult), reads=[tb['sinp'].key], writes=[tb['sinn'].key])
            else:
                S.op('act', lambda e: e.activation(out=tb['cos2'][:, :, 0:32], in_=vf[:], func=AF.Sin, scale=2 * math.pi), reads=[vf.key], writes=[tb['cos2'].key])
                S.op('dve', lambda e: e.tensor_copy(out=tb['cos2'][:, :, 32:64], in_=tb['cos2'][:, :, 0:32]), reads=[], writes=[tb['cos2'].key])

    def rope_tiles(st, sfx="", NTB=2, tmp=None):
        tb = {}
        tmp = tmp or {}
        for k_, shp, dt_ in [('pi', [128, NTB], I32), ('pf', [128, NTB], F32), ('v', [128, NTB, 32], F32), ('vi', [128, NTB, 32], I32), ('vf', [128, NTB, 32], F32)]:
            tb[k_] = tmp[k_] if k_ in tmp else sb(st, "rt_" + k_ + sfx, shp, dt_)
        tb['cos2'] = sb(st, "rt_cos2" + sfx, [128, NTB, 64]); tb['sinp'] = sb(st, "rt_sinp" + sfx, [128, NTB, 32]); tb['sinn'] = sb(st, "rt_sinn" + sfx, [128, NTB, 32])
        return tb

    def conv_group(zt, zoff, cg, acc, out_ap, okey, silr, oscale=1.0):
        pc = psb.next()
        for j in range(4):
            S.op('pe', lambda e, j=j: e.matmul(pc[:, 0:256], lhsT=dgw[:, cg, j, :], rhs=zt[:, zoff + j:zoff + j + 256], start=(j == 0), stop=(j == 3)), reads=['dgw', zt.key], writes=[pc.key])
        se = silr.next(); sl = silr.next()
        S.op('act', lambda e: e.activation(out=se[:], in_=pc[:, 0:256], func=AF.Exp, scale=-1.0, bias=ncb[:, cg:cg + 1]), reads=[pc.key, 'ncb'], writes=[se.key])
        S.op('act', lambda e: e.activation(out=sl[:], in_=se[:], func=AF.Ln, bias=one_t[:]), reads=[se.key, 'one_t'], writes=[sl.key])
        qb = lnq_t if oscale != 1.0 else zero_t
        S.op('act', lambda e: e.activation(out=se[:], in_=sl[:], func=AF.Exp, scale=-1.0, bias=qb[:]), reads=[sl.key, qb.key], writes=[se.key])
        S.op('dve', lambda e: e.scalar_tensor_tensor(out=out_ap, in0=pc[:, 0:256], scalar=cb[:, cg:cg + 1], in1=se[:], op0=ALU.add, op1=ALU.mult), reads=[pc.key, 'cb_t', se.key], writes=[okey])

    VSTB = 4 if NBLK >= 4 else NBLK
    chk_stop('W')
    with ExitStack() as st:
        awr = Rot([sb(st, "awb%d" % i, [128, 8, 128]) for i in range(2)])
        mod_groups(awr, 16, 48, 'modc_b')
        S.op('dve', lambda e: e.scalar_tensor_tensor(out=gmod2[:], in0=modc[:, 32:40], scalar=1.0, in1=n2g[:], op0=ALU.add, op1=ALU.mult), reads=['modc_b', 'n2g_t'], writes=['gmod2'])
        RCH = min(8, NBLK)
        tbs = [rope_tiles(st, "a", 2 * RCH)]
        tbs.append(rope_tiles(st, "b", 2 * RCH, tmp=tbs[0]))
        xtr = Rot([sb(st, "xt%d" % i, [128, 8, 260]) for i in range(2)])
        sqr = Rot([sb(st, "sq%d" % i, [128, 8, 260], BF16) for i in range(2)])
        lnr = Rot([sb(st, "ln%d" % i, [128, 260]) for i in range(2)])
        rsr = Rot([sb(st, "rs%d" % i, [128, 260]) for i in range(2)])
        xsr = Rot([sb(st, "xs%d" % i, [128, 8, 260], BF16) for i in range(2)])
        zgr = Rot([sb(st, "zg%d" % i, [128, 260], BF16) for i in range(3)])
        accr = Rot([None])
        silr = Rot([sb(st, "sil%d" % i, [128, 256]) for i in range(4)])
        hal = sb(st, "hal", [128, 4, 3], BF16)
        kT = rt(st, "kT", [128, 4, 256], BF16)
        ktm = rt(st, "ktm", [128, 2, 4, 128], BF16)
        vaugr = Rot([sb(st, "vaug%d" % i, [128, 2, 4, 129], BF16) for i in range(2)])
        gifr = Rot([sb(st, "gif%d" % i, [128, 2, 8]) for i in range(2)])
        kxr = Rot([sb(st, "kx%d" % i, [128, 8, 64]) for i in range(2)])
        tAr = Rot([sb(st, "ropeA%d" % i, [128, 8, 64]) for i in range(2)]); tBr = Rot([sb(st, "ropeB%d" % i, [128, 8, 64]) for i in range(2)])
        Ktm = rt(st, "Ktm", [128, 2, 8, 64], BF16)
        KTblk = Rot([sb(st, "KTblk%d" % i, [128, 4, 256], BF16) for i in range(2)])
        vst = Rot([sb(st, "vst%d" % i, [128, 2 * VSTB, 8, 65], BF16) for i in range(2)])
        CT = sb(st, "CT", [128, 4, 129])
        gts = [{k: sb(st, "g%d_" % r + k, [128, 4]) for k in ['l', 'u', 'w', 'dec', 'tmp']} for r in range(3)]
        gti = [0]
        wvr = Rot([sb(st, "wv%d" % i, [128, 4, 129], BF16) for i in range(2)])
        ksq = rt(st, "ksq", [128, 8, 64]); kn2 = rt(st, "kn2", [128, 8]); kmacc = sb(st, "kmacc", [128, 8])
        S.op('pool', lambda e: e.memset(hal[:], 0.0), writes=['hal'])
        S.op('pool', lambda e: e.memset(CT[:], 0.0), writes=['CT'])
        S.op('pool', lambda e: e.memset(kmacc[:], 0.0), writes=['kmacc'])
        for v_ in vst.tiles:
            S.op('pool', lambda e, v_=v_: e.memset(v_[:], 1.0), writes=[v_.key])
        for v_ in vaugr.tiles:
            S.op('pool', lambda e, v_=v_: e.memset(v_[:], 1.0), writes=[v_.key])
        xTf_v = xTf.rearrange("(k p) t -> p k t", p=128)
        KTs_v = KTs.rearrange("(hp r) t -> r hp t", r=128)
        vcur = None
        for n in range(NBLK):
            i_own = n // 4
            if n % 4 == 0:
                S.op('dve', lambda e, n=n: e.tensor_scalar(out=Ccur[:], in0=CT[:], scalar1=capsel[:, 0:1], scalar2=None, op0=ALU.mult), reads=['CT', 'capsel_t'], writes=['Ccur'])
            else:
                S.op('dve', lambda e, n=n: e.scalar_tensor_tensor(out=Ccur[:], in0=CT[:], scalar=capsel[:, n % 4:n % 4 + 1], in1=Ccur[:], op0=ALU.mult, op1=ALU.add), reads=['CT', 'capsel_t'], writes=['Ccur'])
            if n % 4 == 3:
                S.dma('pool', CCs[i_own].rearrange("p (h e) -> p h e", h=4), Ccur[:], reads=['Ccur'], writes=[('CCs', i_own)])
            xt = xtr.next(); xs = xsr.next()
            tb = tbs[(n // RCH) % 2]
            cos2, sinp, sinn = tb['cos2'], tb['sinp'], tb['sinn']
            kT.rotate(); ktm.rotate(); Ktm.rotate()
            S.dma('sp', xt[:, :, 0:256], xTf_v[:, :, n * 256:(n + 1) * 256], writes=[xt.key])
            if n % RCH == 0:
                rope_tables(tb, posf_d[:, 2 * n:2 * n + 2 * RCH], 2 * RCH)
            front(xt, 256, xs, sqr, lnr, rsr)
            for hd in range(4):
                ps = psm.next(); zg = zgr.next(); acc = accr.next()
                proj_fm(ps, WF, OFF['mk'] + hd * 128, xs, 0, 256)
                S.op('pool', lambda e, hd=hd, zg=zg: e.tensor_copy(out=zg[:, 0:3], in_=hal[:, hd, :]), reads=['hal'], writes=[zg.key])
                S.op('act', lambda e, hd=hd, ps=ps, zg=zg: e.activation(out=zg[:, 3:259], in_=ps[:, 0:256], func=AF.Identity, bias=biasc[:, hd:hd + 1]), reads=[ps.key, 'biasc'], writes=[zg.key])
                S.op('pool', lambda e, hd=hd, zg=zg: e.tensor_copy(out=hal[:, hd, :], in_=zg[:, 256:259]), reads=[zg.key], writes=['hal'])
                conv_group(zg, 0, 4 + hd, acc, kT[:, hd, :], 'kT', silr)
            pt = pstr.next()
            for t in range(2):
                for hd in range(4):
                    S.op('pe', lambda e, t=t, hd=hd, pt=pt: e.transpose(out=pt[:, (t * 4 + hd) * 128:(t * 4 + hd + 1) * 128], in_=kT[:, hd, t * 128:(t + 1) * 128], identity=ident_b[:]), reads=['kT', 'ident_b'], writes=[pt.key])
            S.op('act', lambda e, pt=pt: e.activation(out=ktm[:].rearrange("p t h e -> p (t h e)"), in_=pt[:], func=AF.Copy), reads=[pt.key], writes=['ktm'])
            vaug = vaugr.next(); gif = gifr.next()
            if n % VSTB == 0:
                vcur = vst.next()
            KTb = KTblk.next()
            ptK = pstr.next()
            for t in range(2):
                ps = psm.next()
                proj_tm(ps[:, 0:512], ps.key, WF, OFF['mv'], 512, xs, t * 128)
                S.op('dve', lambda e, t=t, ps=ps, vaug=vaug: e.tensor_tensor(out=vaug[:, t, :, 0:128], in0=ps[:, 0:512].rearrange("p (h d) -> p h d", h=4), in1=bias_bc[:, bbc(OFF['mv']):bbc(OFF['mv']) + 512].rearrange("p (h d) -> p h d", h=4), op=ALU.add),
                     reads=[ps.key, 'bias_bc'], writes=[vaug.key])
                gp = psm.next()
                proj_tm(gp[:, 0:8], gp.key, WF, OFF['mif'], 8, xs, t * 128)
                S.op('dve', lambda e, t=t, gif=gif, gp=gp: e.tensor_tensor(out=gif[:, t, :], in0=gp[:, 0:8], in1=bias_bc[:, bbc(OFF['mif']):bbc(OFF['mif']) + 8], op=ALU.add), reads=[gp.key, 'bias_bc'], writes=[gif.key])
                ps = psm.next(); kx = kxr.next()
                proj_tm(ps[:, 0:512], ps.key, WF, OFF['ak'], 512, xs, t * 128)
                S.op('dve', lambda e, ps=ps, kx=kx: e.tensor_tensor(out=kx[:].rearrange("p h d -> p (h d)"), in0=ps[:, 0:512], in1=bias_bc[:, bbc(OFF['ak']):bbc(OFF['ak']) + 512], op=ALU.add), reads=[ps.key, 'bias_bc'], writes=[kx.key])
                rope(kx, Ktm[:, t, :, :], 'Ktm', cos2, sinp, sinn, (n % RCH) * 2 + t, tAr.next(), tBr.next())
                ps = psm.next()
                proj_tm(ps[:, 0:512], ps.key, WF, OFF['av'], 512, xs, t * 128)
                slot = (n % VSTB) * 2 + t
                S.op('dve', lambda e, ps=ps, slot=slot, vc=vcur: e.tensor_tensor(out=vc[:, slot, :, 0:64], in0=ps[:, 0:512].rearrange("p (h d) -> p h d", h=8), in1=bias_bc[:, bbc(OFF['av']):bbc(OFF['av']) + 512].rearrange("p (h d) -> p h d", h=8), op=ALU.add),
                     reads=[ps.key, 'bias_bc'], writes=[vcur.key])
            for t in range(2):
                for hp in range(4):
                    S.op('pe', lambda e, t=t, hp=hp, ptK=ptK: e.transpose(out=ptK[:, (hp * 2 + t) * 128:(hp * 2 + t + 1) * 128], in_=Ktm[:, t, 2 * hp:2 * hp + 2, :].rearrange("p h d -> p (h d)"), identity=ident_b[:]),
                         reads=['Ktm', 'ident_b'], writes=[ptK.key])
            S.op('act', lambda e, ptK=ptK, KTb=KTb: e.activation(out=KTb[:].rearrange("p h t -> p (h t)"), in_=ptK[:], func=AF.Copy), reads=[ptK.key], writes=[KTb.key])
            S.dma('pool', KTs_v[:, :, n * 256:(n + 1) * 256], KTb[:], reads=[KTb.key], writes=[('KTs', n)])
            kp = psb.next()
            for hp in range(4):
                for t in range(2):
                    S.op('pe', lambda e, t=t, hp=hp, kp=kp: e.matmul(kp[:, 16 + hp:17 + hp], lhsT=Ktm[:, t, 2 * hp:2 * hp + 2, :].rearrange("p h d -> p (h d)"), rhs=ones_b[:, 0:1], start=(t == 0), stop=(t == 1)),
                         reads=['Ktm', 'ones_b'], writes=[kp.key])
            S.op('act', lambda e, n=n: e.activation(out=kmT[0:64, :, 0, n], in_=kp[0:64, 16:20], func=AF.Copy, scale=1.0 / 256), reads=[kp.key], writes=['kmT'])
            S.op('act', lambda e, n=n: e.activation(out=kmT[64:128, :, 1, n], in_=kp[64:128, 16:20], func=AF.Copy, scale=1.0 / 256), reads=[kp.key], writes=['kmT'])
            for t in range(2):
                ksq.rotate(); kn2.rotate()
                S.op('pool', lambda e, t=t: e.tensor_tensor(out=ksq[:], in0=Ktm[:, t, :, :], in1=Ktm[:, t, :, :], op=ALU.mult), reads=['Ktm'], writes=['ksq'])
                S.op('dve', lambda e: e.reduce_sum(out=kn2[:], in_=ksq[:], axis=AX.X), reads=['ksq'], writes=['kn2'])
                S.op('dve', lambda e: e.tensor_tensor(out=kmacc[:], in0=kmacc[:], in1=kn2[:], op=ALU.max), reads=['kn2'], writes=['kmacc'])
            if n % VSTB == VSTB - 1:
                b0 = n - (VSTB - 1)
                for h in range(8):
                    S.dma('pool', Vs[h, :, b0 * 2 * 65:(n + 1) * 2 * 65].rearrange("p (t d) -> p t d", d=65), vcur[:, 0:2 * VSTB, h, :], reads=[vcur.key], writes=[('Vs', h, b0)])
            for t in range(2):
                gt = gts[gti[0] % 3]; gti[0] += 1
                gates(gif, t, gt)
                state_update(vaug, ktm, t, gt, CT, wvr)
        ptf = psm.next()
        S.op('pe', lambda e: e.transpose(out=ptf[0:8, 0:128], in_=kmacc[:], identity=ident_f[:]), reads=['kmacc', 'ident_f'], writes=[ptf.key])
        kmx = sb(st, "kmx", [8, 1]); dg = sb(st, "dg", [8, 8])
        S.op('dve', lambda e: e.reduce_max(out=kmx[:], in_=ptf[0:8, 0:128], axis=AX.X), reads=[ptf.key], writes=['kmx'])
        S.op('dve', lambda e: e.tensor_scalar(out=dg[:], in0=ident_f[0:8, 0:8], scalar1=kmx[:, 0:1], scalar2=None, op0=ALU.mult), reads=['kmx', 'ident_f'], writes=['dg'])
        S.op('pe', lambda e: e.matmul(ptf[:, 256:264], lhsT=ones_f[0:8, :], rhs=dg[:], start=True, stop=True), reads=['ones_f', 'dg', ptf.key], writes=[ptf.key])
        S.op('act', lambda e: e.activation(out=kmaxbc[:], in_=ptf[:, 256:264], func=AF.Ln, bias=eps_t[:]), reads=[ptf.key, 'eps_t'], writes=['kmaxbc'])
        S.op('act', lambda e: e.activation(out=kmaxbc[:], in_=kmaxbc[:], func=AF.Exp, scale=0.5), reads=[], writes=['kmaxbc'])
        dump("kmT", kmT[:].rearrange("p h a n -> p (h a n)"), [128, 256], ['kmT'])
        dump("kmaxbc", kmaxbc[:], [128, 8], ['kmaxbc'])
        dump("CTfin", CT[:].rearrange("p h e -> p (h e)"), [128, 4 * 129], ['CT'])
        S.barrier()

    chk_stop('F')
    with ExitStack() as st:
        WO = sb(st, "WO", [128, 8, 1536], BF16)
        with ExitStack() as st2:
            prep_win(st2, OFF['mq'], 512, WO, OFF['mq'] - NF, 'fm', gi0=4)
            prep_win(st2, OFF['mo'], 1024, WO, OFF['mo'] - NF, 'tm')
            S.barrier()
        tb = {}
        tb['cos2'] = sb(st, "rt_cos2o", [128, NTO, 64]); tb['sinp'] = sb(st, "rt_sinpo", [128, NTO, 32]); tb['sinn'] = sb(st, "rt_sinno", [128, NTO, 32])
        cos2, sinp, sinn = tb['cos2'], tb['sinp'], tb['sinn']
        with ExitStack() as st2:
            for k_, shp, dt_ in [('pi', [128, NTO], I32), ('pf', [128, NTO], F32), ('v', [128, NTO, 32], F32), ('vi', [128, NTO, 32], I32), ('vf', [128, NTO, 32], F32)]:
                tb[k_] = sb(st2, "rt_" + k_ + "o", shp, dt_)
            rope_tables(tb, poso_d, NTO)
            S.barrier()
        xtr = Rot([sb(st, "xt%d" % i, [128, 8, 260]) for i in range(1)])
        sqr = Rot([sb(st, "sq%d" % i, [128, 8, 260], BF16) for i in range(1)])
        lnr = Rot([sb(st, "ln%d" % i, [128, 260]) for i in range(2)])
        rsr = Rot([sb(st, "rs%d" % i, [128, 260]) for i in range(2)])
        xsr = Rot([sb(st, "xs%d" % i, [128, 8, 260], BF16) for i in range(2)])
        zgr = Rot([sb(st, "zg%d" % i, [128, 260], BF16) for i in range(3)])
        accr = Rot([None])
        silr = Rot([sb(st, "sil%d" % i, [128, 256]) for i in range(4)])
        qkT = sb(st, "qkT", [128, 8, 256], BF16)
        ktm = sb(st, "ktm", [128, 2, 4, 128], BF16)
        vaug = rt(st, "vaug", [128, 2, 4, 129], BF16)
        gif = rt(st, "gif", [128, 2, 8])
        kxr = Rot([sb(st, "kx%d" % i, [128, 8, 64]) for i in range(2)])
        tAr = Rot([sb(st, "ropeA%d" % i, [128, 8, 64]) for i in range(2)]); tBr = Rot([sb(st, "ropeB%d" % i, [128, 8, 64]) for i in range(2)])
        Qtm = rt(st, "Qtm", [128, 8, 64], BF16); Ktm1 = rt(st, "Ktm1", [128, 8, 64], BF16)
        Qaug = rt(st, "Qaug", [128, 8, 97], BF16); Kaug = rt(st, "Kaug", [128, 8, 97], BF16)
        QTp = rt(st, "QTp", [128, 4, 128], BF16)
        gm = rt(st, "gm", [128, 8, 32]); t8 = rt(st, "t8", [128, 8, 8]); thr = rt(st, "thr", [128, 8]); msk = rt(st, "msk", [128, 8, 32])
        sqq = sb(st, "sqq", [128, 8, 64]); qn = rt(st, "qn", [128, 8])
        QTab = sb(st, "QTab", [128, 8, 256], BF16); KTab = sb(st, "KTab", [128, 8, 256], BF16)
        Vob = sb(st, "Vob", [128, 8, 2, 65], BF16)
        ymTb = sb(st, "ymTb", [128, 4, 256], BF16)
        mo_f = rt(st, "mo_f", [128, 512], k=1); sigmo = rt(st, "sigmo", [128, 512], BF16)
        CT = sb(st, "CT", [128, 4, 129]); CTb = rt(st, "CTb", [128, 4, 129], BF16)
        gts = [{k: sb(st, "g%d_" % r + k, [128, 4]) for k in ['l', 'u', 'w', 'dec', 'tmp']} for r in range(3)]
        gti = [0]
        wvr = Rot([sb(st, "wv%d" % i, [128, 4, 129], BF16) for i in range(2)])
        LFbc = rt(st, "LFbc", [128, 4, 128], k=1)
        EB = rt(st, "EB", [128, 4, 128], k=1); DT = rt(st, "DT", [128, 4, 128])
        SWT = rt(st, "SWT", [128, 4, 128], BF16); qsT = rt(st, "qsT", [128, 4, 128], BF16)
        absd = rt(st, "absd", [128, 4]); rr = rt(st, "rr", [128, 4])
        hsb = rt(st, "hsb", [128, 4, 128], k=1); hsq = sb(st, "hsq", [128, 4, 128]); ssq = rt(st, "ssq", [128, 4]); rsh = rt(st, "rsh", [128, 4])
        ym = rt(st, "ym", [128, 512], BF16, k=1)
        for _ in range(2):
            S.op('pool', lambda e: e.memset(vaug[:], 1.0), writes=['vaug'])
            S.op('pool', lambda e: e.memset(Kaug[:], 0.0), writes=['Kaug'])
            S.op('pool', lambda e: e.memset(Kaug[:, :, 96:97], 1.0), writes=['Kaug'])
            vaug.rotate(); Kaug.rotate()
        S.op('pool', lambda e: e.memset(Vob[:], 1.0), writes=['Vob'])
        S.op('pool', lambda e: e.memset(QTab[:], 0.0), writes=['QTab'])
        S.op('pool', lambda e: e.memset(KTab[:], 0.0), writes=['KTab'])
        xTo_v = xTo.rearrange("(k p) t -> p k t", p=128)
        for i in range(NOWN):
            xt = xtr.next(); xs = xsr.next()
            vaug.rotate(); gif.rotate()
            S.dma('sp', xt[:], xTo_v[:, :, i * 260:(i + 1) * 260], writes=[xt.key])
            S.dma('sp', CT[:], CCs[i].rearrange("p (h e) -> p h e", h=4), reads=[('CCs', i)], writes=['CT'])
            front(xt, 260, xs, sqr, lnr, rsr)
            S.dma('pool', XSs[i].rearrange("p (k t) -> p k t", k=8), xs[:, :, 4:260], reads=xskeys(xs), writes=[('XSs', i)])
            for g in range(8):
                ps = psm.next(); zg = zgr.next(); acc = accr.next()
                if g < 4:
                    proj_fm(ps, WO, OFF['mq'] - NF + g * 128, xs, 0, 260)
                    bcol = 4 + g
                else:
                    proj_fm(ps, WF, OFF['mk'] + (g - 4) * 128, xs, 0, 260)
                    bcol = g - 4
                S.op('act', lambda e, zg=zg, ps=ps, bcol=bcol: e.activation(out=zg[:], in_=ps[:, 0:260], func=AF.Identity, bias=biasc[:, bcol:bcol + 1]), reads=[ps.key, 'biasc'], writes=[zg.key])
                S.op('dve', lambda e, i=i, zg=zg: e.tensor_scalar(out=zg[:, 0:4], in0=zg[:, 0:4], scalar1=hv[:, i:i + 1], scalar2=None, op0=ALU.mult), reads=['hv_t'], writes=[zg.key])
                conv_group(zg, 1, g, acc, qkT[:, g, :], 'qkT', silr, oscale=(128.0 ** -0.5 if g < 4 else 1.0))
            pt = pstr.next()
            for t in range(2):
                for hd in range(4):
                    S.op('pe', lambda e, t=t, hd=hd, pt=pt: e.transpose(out=pt[:, (t * 4 + hd) * 128:(t * 4 + hd + 1) * 128], in_=qkT[:, 4 + hd, t * 128:(t + 1) * 128], identity=ident_b[:]), reads=['qkT', 'ident_b'], writes=[pt.key])
            S.op('act', lambda e, pt=pt: e.activation(out=ktm[:].rearrange("p t h e -> p (t h e)"), in_=pt[:], func=AF.Copy), reads=[pt.key], writes=['ktm'])
            S.op('act', lambda e: e.activation(out=CTb[:], in_=CT[:], func=AF.Copy), reads=['CT'], writes=['CTb'])
            if i == 0:
                chk_stop('O1a')
            for t in range(2):
                T = i * 2 + t
                x0 = 4 + t * 128
                for r_ in [Qtm, Ktm1, Qaug, Kaug, QTp, gm, t8, thr, msk, qn, mo_f, sigmo, LFbc, EB, DT, SWT, qsT, absd, rr, hsb, ssq, rsh, ym]:
                    r_.rotate()
                tA = tAr.next(); tB = tBr.next()
                gt = gts[gti[0] % 3]; gti[0] += 1
                ps = psm.next()
                proj_tm(ps[:, 0:512], ps.key, WF, OFF['mv'], 512, xs, x0)
                S.op('dve', lambda e, t=t, ps=ps: e.tensor_tensor(out=vaug[:, t, :, 0:128], in0=ps[:, 0:512].rearrange("p (h d) -> p h d", h=4), in1=bias_bc[:, bbc(OFF['mv']):bbc(OFF['mv']) + 512].rearrange("p (h d) -> p h d", h=4), op=ALU.add),
                     reads=[ps.key, 'bias_bc'], writes=['vaug'])
                gp = psm.next()
                proj_tm(gp[:, 0:8], gp.key, WF, OFF['mif'], 8, xs, x0)
                S.op('dve', lambda e, t=t, gp=gp: e.tensor_tensor(out=gif[:, t, :], in0=gp[:, 0:8], in1=bias_bc[:, bbc(OFF['mif']):bbc(OFF['mif']) + 8], op=ALU.add), reads=[gp.key, 'bias_bc'], writes=['gif'])
                ps = psm.next()
                proj_tm(ps[:, 0:512], ps.key, WO, OFF['mo'] - NF, 512, xs, x0)
                S.op('dve', lambda e, ps=ps: e.tensor_tensor(out=mo_f[:], in0=ps[:, 0:512], in1=bias_bc[:, bbc(OFF['mo']):bbc(OFF['mo']) + 512], op=ALU.add), reads=[ps.key, 'bias_bc'], writes=['mo_f'])
                S.op('act', lambda e: e.activation(out=mo_f[:], in_=mo_f[:], func=AF.Exp, scale=-1.0), reads=[], writes=['mo_f'])
                S.op('act', lambda e: e.activation(out=mo_f[:], in_=mo_f[:], func=AF.Ln, bias=one_t[:]), reads=['one_t'], writes=['mo_f'])
                S.op('act', lambda e: e.activation(out=sigmo[:], in_=mo_f[:], func=AF.Exp, scale=-1.0), reads=['mo_f'], writes=['sigmo'])
                ps = psm.next(); kx = kxr.next()
                proj_tm(ps[:, 0:512], ps.key, WO, OFF['aq'] - NF, 512, xs, x0)
                S.op('dve', lambda e, ps=ps, kx=kx: e.tensor_tensor(out=kx[:].rearrange("p h d -> p (h d)"), in0=ps[:, 0:512], in1=bias_bc[:, bbc(OFF['aq']):bbc(OFF['aq']) + 512], op=ALU.add), reads=[ps.key, 'bias_bc'], writes=[kx.key])
                rope(kx, Qtm[:], 'Qtm', cos2, sinp, sinn, T, tA, tB)
                ps = psm.next(); kx = kxr.next()
                proj_tm(ps[:, 0:512], ps.key, WF, OFF['ak'], 512, xs, x0)
                S.op('dve', lambda e, ps=ps, kx=kx: e.tensor_tensor(out=kx[:].rearrange("p h d -> p (h d)"), in0=ps[:, 0:512], in1=bias_bc[:, bbc(OFF['ak']):bbc(OFF['ak']) + 512], op=ALU.add), reads=[ps.key, 'bias_bc'], writes=[kx.key])
                rope(kx, Ktm1[:], 'Ktm1', cos2, sinp, sinn, T, tAr.next(), tBr.next())
                ps = psm.next()
                proj_tm(ps[:, 0:512], ps.key, WF, OFF['av'], 512, xs, x0)
                S.op('dve', lambda e, ps=ps, t=t: e.tensor_tensor(out=Vob[:, :, t, 0:64], in0=ps[:, 0:512].rearrange("p (h d) -> p h d", h=8), in1=bias_bc[:, bbc(OFF['av']):bbc(OFF['av']) + 512].rearrange("p (h d) -> p h d", h=8), op=ALU.add),
                     reads=[ps.key, 'bias_bc'], writes=['Vob'])
                if i == 0 and t == 0:
                    chk_stop('O1b')
                gates(gif, t, gt)
                for h in range(4):
                    S.op('act', lambda e, h=h: e.activation(out=LFbc[:, h, :], in_=ones_f[:], func=AF.Copy, scale=gt['l'][:, h:h + 1]), reads=['ones_f', gt['l'].key], writes=['LFbc'])
                pb = psb.next()
                for h in range(4):
                    S.op('pe', lambda e, h=h, pb=pb: e.matmul(pb[:, h * 128:(h + 1) * 128], lhsT=LFbc[:, h, :], rhs=tri_f[:], start=True, stop=True), reads=['LFbc', 'tri_f'], writes=[pb.key])
                S.op('act', lambda e, pb=pb: e.activation(out=EB[:].rearrange("p h t -> p (h t)"), in_=pb[:, 0:512], func=AF.Exp, scale=-1.0), reads=[pb.key], writes=['EB'])
                for h in range(4):
                    S.op('act', lambda e, h=h, pb=pb: e.activation(out=DT[:, h, :], in_=pb[:, h * 128:(h + 1) * 128], func=AF.Exp, scale=-1.0, bias=gt['u'][:, h:h + 1]), reads=[pb.key, gt['u'].key], writes=['DT'])
                for h in range(4):
                    S.op('pool', lambda e, h=h: e.tensor_tensor(out=DT[:, h, :], in0=DT[:, h, :], in1=tri_f[:], op=ALU.mult), reads=['tri_f'], writes=['DT'])
                pS = psb.next()
                for h in range(4):
                    S.op('pe', lambda e, h=h, t=t, pS=pS: e.matmul(pS[:, h * 128:(h + 1) * 128], lhsT=qkT[:, 4 + h, t * 128:(t + 1) * 128], rhs=qkT[:, h, t * 128:(t + 1) * 128], start=True, stop=True), reads=['qkT'], writes=[pS.key])
                S.op('dve', lambda e, pS=pS: e.tensor_tensor(out=SWT[:].rearrange("p h t -> p (h t)"), in0=DT[:].rearrange("p h t -> p (h t)"), in1=pS[:, 0:512], op=ALU.mult), reads=['DT', pS.key], writes=['SWT'])
                S.op('dve', lambda e, t=t: e.tensor_tensor(out=qsT[:], in0=qkT[:, 0:4, t * 128:(t + 1) * 128], in1=EB[:], op=ALU.mult), reads=['qkT', 'EB'], writes=['qsT'])
                pn = [psb.next(), psb.next()]
                for h in range(4):
                    pp = pn[h // 2]; o0 = (h % 2) * 129
                    S.op('pe', lambda e, h=h, t=t, pp=pp, o0=o0: e.matmul(pp[:, o0:o0 + 129], lhsT=SWT[:, h, :], rhs=vaug[:, t, h, :], start=True, stop=False), reads=['SWT', 'vaug'], writes=[pp.key])
                    S.op('pe', lambda e, h=h, pp=pp, o0=o0: e.matmul(pp[:, o0:o0 + 129], lhsT=qsT[:, h, :], rhs=CTb[:, h, :], start=False, stop=True), reads=['qsT', 'CTb'], writes=[pp.key])
                for h in range(4):
                    pp = pn[h // 2]; o0 = (h % 2) * 129
                    S.op('act', lambda e, h=h, pp=pp, o0=o0: e.activation(out=absd[:, h:h + 1], in_=pp[:, o0 + 128:o0 + 129], func=AF.Abs), reads=[pp.key], writes=['absd'])
                S.op('dve', lambda e: e.tensor_scalar(out=absd[:], in0=absd[:], scalar1=1.0, scalar2=None, op0=ALU.max), reads=[], writes=['absd'])
                S.op('dve', lambda e: e.reciprocal(out=rr[:], in_=absd[:]), reads=['absd'], writes=['rr'])
                for h in range(4):
                    pp = pn[h // 2]; o0 = (h % 2) * 129
                    S.op('act', lambda e, h=h, pp=pp, o0=o0: e.activation(out=hsb[:, h, :], in_=pp[:, o0:o0 + 128], func=AF.Copy, scale=rr[:, h:h + 1]), reads=[pp.key, 'rr'], writes=['hsb'])
                S.op('pool', lambda e: e.tensor_tensor(out=hsq[:], in0=hsb[:], in1=hsb[:], op=ALU.mult), reads=['hsb'], writes=['hsq'])
                S.op('dve', lambda e: e.reduce_sum(out=ssq[:], in_=hsq[:], axis=AX.X), reads=['hsq'], writes=['ssq'])
                S.op('act', lambda e: e.activation(out=ssq[:], in_=ssq[:], func=AF.Ln, scale=1.0 / 128, bias=eps_t[:]), reads=['eps_t'], writes=['ssq'])
                S.op('act', lambda e: e.activation(out=rsh[:], in_=ssq[:], func=AF.Exp, scale=-0.5), reads=['ssq'], writes=['rsh'])
                S.op('pool', lambda e: e.tensor_tensor(out=hsb[:].rearrange("p h d -> p (h d)"), in0=hsb[:].rearrange("p h d -> p (h d)"), in1=mng[:], op=ALU.mult), reads=['hsq', 'mng_t'], writes=['hsb'])
                for h in range(4):
                    S.op('dve', lambda e, h=h: e.scalar_tensor_tensor(out=ym[:, h * 128:(h + 1) * 128], in0=hsb[:, h, :], scalar=rsh[:, h:h + 1], in1=sigmo[:, h * 128:(h + 1) * 128], op0=ALU.mult, op1=ALU.mult),
                         reads=['hsb', 'rsh', 'sigmo'], writes=['ym'])
                pt = pstr.next()
                for c in range(4):
                    S.op('pe', lambda e, c=c, pt=pt: e.transpose(out=pt[:, c * 128:(c + 1) * 128], in_=ym[:, c * 128:(c + 1) * 128], identity=ident_b[:]), reads=['ym', 'ident_b'], writes=[pt.key])
                S.op('act', lambda e, pt=pt, t=t: e.activation(out=ymTb[:, :, t * 128:(t + 1) * 128], in_=pt[:, 0:512].rearrange("p (c t) -> p c t", c=4), func=AF.Copy), reads=[pt.key], writes=['ymTb'])
                state_update(vaug, ktm, t, gt, CT, wvr)
                CTb.rotate()
                S.op('act', lambda e: e.activation(out=CTb[:], in_=CT[:], func=AF.Copy), reads=['CT'], writes=['CTb'])
                if i == 0 and t == 0:
                    chk_stop('O1c')
                pt = pstr.next()
                for hp in range(4):
                    S.op('pe', lambda e, hp=hp, pt=pt: e.transpose(out=pt[:, hp * 128:(hp + 1) * 128], in_=Qtm[:, 2 * hp:2 * hp + 2, :].rearrange("p h d -> p (h d)"), identity=ident_b[:]), reads=['Qtm', 'ident_b'], writes=[pt.key])
                S.op('act', lambda e, pt=pt: e.activation(out=QTp[:].rearrange("p h t -> p (h t)"), in_=pt[:, 0:512], func=AF.Copy), reads=[pt.key], writes=['QTp'])
                pg = psb.next()
                for hp in range(4):
                    S.op('pe', lambda e, hp=hp, pg=pg: e.matmul(pg[:, hp * 64:(hp + 1) * 64], lhsT=QTp[:, hp, :], rhs=kmT[:, hp, :, :].rearrange("p a n -> p (a n)"), start=True, stop=True), reads=['QTp', 'kmT'], writes=[pg.key])
                S.op('dve', lambda e, pg=pg, i=i: e.tensor_tensor(out=gm[:], in0=pg[:, 0:256].rearrange("p (h n) -> p h n", h=8), in1=pastb[:, i * 32:(i + 1) * 32].unsqueeze(1).to_broadcast([128, 8, 32]), op=ALU.add), reads=[pg.key, 'pastb_t'], writes=['gm'])
                for h in range(8):
                    S.op('dve', lambda e, h=h: e.max(out=t8[:, h, :], in_=gm[:, h, :]), reads=['gm'], writes=['t8'])
                S.op('dve', lambda e: e.tensor_scalar(out=thr[:], in0=t8[:, :, 2], scalar1=-1e29, scalar2=None, op0=ALU.max), reads=['t8'], writes=['thr'])
                S.op('dve', lambda e: e.tensor_tensor(out=msk[:], in0=gm[:], in1=thr[:].unsqueeze(2).to_broadcast([128, 8, 32]), op=ALU.is_lt), reads=['gm', 'thr'], writes=['msk'])
                S.op('dve', lambda e: e.tensor_scalar(out=Qaug[:, :, 64:96], in0=msk[:], scalar1=NEG, scalar2=None, op0=ALU.mult), reads=['msk'], writes=['Qaug'])
                S.op('pool', lambda e: e.tensor_copy(out=Qaug[:, :, 0:64], in_=Qtm[:]), reads=['Qtm'], writes=['Qaug'])
                S.op('pool', lambda e: e.tensor_tensor(out=sqq[:], in0=Qtm[:], in1=Qtm[:], op=ALU.mult), reads=['Qtm'], writes=['sqq'])
                S.op('dve', lambda e: e.reduce_sum(out=qn[:], in_=sqq[:], axis=AX.X), reads=['sqq'], writes=['qn'])
                S.op('act', lambda e: e.activation(out=qn[:], in_=qn[:], func=AF.Ln, bias=eps_t[:]), reads=['eps_t'], writes=['qn'])
                S.op('act', lambda e: e.activation(out=qn[:], in_=qn[:], func=AF.Exp, scale=0.5), reads=[], writes=['qn'])
                S.op('dve', lambda e: e.scalar_tensor_tensor(out=Qaug[:, :, 96], in0=qn[:], scalar=-1.0, in1=kmaxbc[:], op0=ALU.mult, op1=ALU.mult), reads=['qn', 'kmaxbc'], writes=['Qaug'])
                S.op('pool', lambda e: e.tensor_copy(out=Kaug[:, :, 0:64], in_=Ktm1[:]), reads=['Ktm1'], writes=['Kaug'])
                for (aug, dstT) in [(Qaug, QTab), (Kaug, KTab)]:
                    pt = pstr.next()
                    for h in range(8):
                        S.op('pe', lambda e, h=h, pt=pt, aug=aug: e.transpose(out=pt[0:97, h * 128:(h + 1) * 128], in_=aug[:, h, :], identity=ident_b[:]), reads=[aug.key, 'ident_b'], writes=[pt.key])
                    S.op('act', lambda e, pt=pt, dstT=dstT, t=t: e.activation(out=dstT[0:97, :, t * 128:(t + 1) * 128], in_=pt[0:97, :].rearrange("p (h t) -> p h t", h=8), func=AF.Copy), reads=[pt.key], writes=[dstT.key])
            if i == 0:
                chk_stop('O1d')
            S.dma('pool', QTs[:, :, i * 256:(i + 1) * 256].rearrange("h r t -> r h t"), QTab[:, :, :], reads=['QTab'], writes=[('QTs', i)])
            S.dma('pool', KTos[:, :, i * 256:(i + 1) * 256].rearrange("h r t -> r h t"), KTab[:, :, :], reads=['KTab'], writes=[('KTos', i)])
            S.dma('pool', Vos[:, :, i * 130:(i + 1) * 130].rearrange("h p x -> p h x"), Vob[:].rearrange("p h t d -> p h (t d)"), reads=['Vob'], writes=[('Vos', i)])
            S.dma('pool', YMs[i].rearrange("p (c t) -> p c t", c=4), ymTb[:], reads=['ymTb'], writes=[('YMs', i)])
        S.barrier()
    FO.close()
    G1.close()

    chk_stop('O1')
    XA = ExitStack()
    xacc = sb(XA, "xacc", [128, 8, NOWN * 256])
    OY = ExitStack()
    yaT = sb(OY, "yaT", [128, 4, NOWN * 256], BF16)
    with ExitStack() as st:
        KTb = [sb(st, "KTbuf%d" % i, [128, S_LEN], BF16) for i in range(2)]
        Vb = [sb(st, "Vbuf%d" % i, [128, NT, 65], BF16) for i in range(2)]
        QTh = [sb(st, "QTh%d" % i, [128, NOWN * 256], BF16) for i in range(2)]
        KTo = [sb(st, "KToh%d" % i, [128, NOWN * 256], BF16) for i in range(2)]
        Voh = [sb(st, "Voh%d" % i, [128, NTO, 65], BF16) for i in range(2)]
        PTr = Rot([sb(st, "PT%d" % i, [128, 512], BF16) for i in range(3)])
        ya = sb(st, "ya", [128, NTO, 512], BF16)
        rden = sb(st, "rden", [128, 2])
        estg = sb(st, "estg", [128, 2048])
        KR = 128 if _os.environ.get('KFWL', '1') == '1' else 97
        for b2 in range(2):
            S.op('pool', lambda e, b2=b2: e.memset(KTb[b2][96:128, :], 0.0), writes=[KTb[b2].key + "aug"])
        for c0 in range(0, S_LEN, 2048):
            cn = min(2048, S_LEN - c0)
            S.dma('sp', estg[64:97, 0:cn], etab_d[64:97, c0:c0 + cn], writes=['estg'])
            for b2 in range(2):
                S.op('dve', lambda e, b2=b2, c0=c0, cn=cn: e.tensor_copy(out=KTb[b2][64:97, c0:c0 + cn], in_=estg[64:97, 0:cn]), reads=['estg'], writes=[KTb[b2].key + "aug"])
        for h in range(8):
            b2 = h % 2
            kt_, vb_, qt_, ko_, vo_ = KTb[b2], Vb[b2], QTh[b2], KTo[b2], Voh[b2]
            for c0 in range(0, S_LEN, 2048):
                cn = min(2048, S_LEN - c0)
                S.dma('sp', kt_[0:64, c0:c0 + cn], KTs[h * 64:(h + 1) * 64, c0:c0 + cn], reads=[('KTs', n) for n in range(c0 // 256, (c0 + cn) // 256)], writes=[kt_.key])
            S.dma('sp', vb_[:].rearrange("p t d -> p (t d)"), Vs[h], reads=[('Vs', h, b0) for b0 in range(0, NBLK, VSTB)], writes=[vb_.key])
            S.dma('sp', qt_[:, :], QTs[h], reads=[('QTs', i) for i in range(NOWN)], writes=[qt_.key])
            S.dma('sp', ko_[:, :], KTos[h], reads=[('KTos', i) for i in range(NOWN)], writes=[ko_.key])
            S.dma('sp', vo_[:].rearrange("p t d -> p (t d)"), Vos[h], reads=[('Vos', i) for i in range(NOWN)], writes=[vo_.key])
            for i in range(NOWN):
                units = [('p', n) for n in range(4 * i + 3)] + [('o', i)]
                acc = [pss[0], pss[1]]
                nmm = {0: 0, 1: 0}
                tot = {0: 2 * (4 * i + 3) + 1, 1: 2 * (4 * i + 3) + 2}

                def emit_S(u):
                    ps = psmo.next()
                    for kt in range(2):
                        if u[0] == 'p':
                            lhs = kt_[0:KR, (2 * u[1] + kt) * 128:(2 * u[1] + kt + 1) * 128]
                            rk = [kt_.key, kt_.key + "aug"]
                        else:
                            lhs = ko_[0:KR, i * 256 + kt * 128:i * 256 + (kt + 1) * 128]
                            rk = [ko_.key]
                        S.op('pe', lambda e, ps=ps, kt=kt, lhs=lhs: e.matmul(ps[:, kt * 256:(kt + 1) * 256], lhsT=lhs, rhs=qt_[0:KR, i * 256:(i + 1) * 256], start=True, stop=True), reads=rk + [qt_.key], writes=[ps.key])
                    return ps

                def emit_PV(u, ps):
                    PT = PTr.next()
                    S.op('act', lambda e: e.activation(out=PT[:], in_=ps[:, 0:512], func=AF.Exp, scale=0.125), reads=[ps.key], writes=[PT.key])
                    if u[0] == 'o':
                        S.op('pool', lambda e: e.tensor_tensor(out=PT[:, 0:128], in0=PT[:, 0:128], in1=tri_b[:], op=ALU.mult), reads=['tri_b'], writes=[PT.key])
                        S.op('pool', lambda e: e.tensor_tensor(out=PT[:, 384:512], in0=PT[:, 384:512], in1=tri_b[:], op=ALU.mult), reads=['tri_b'], writes=[PT.key])
                    for qt in range(2):
                        for kt in range(2):
                            if u[0] == 'o' and kt == 1 and qt == 0:
                                continue
                            if u[0] == 'p':
                                rhs = vb_[:, 2 * u[1] + kt, :]
                                rk = [vb_.key]
                            else:
                                rhs = vo_[:, 2 * i + kt, :]
                                rk = [vo_.key]
                            first = nmm[qt] == 0
                            nmm[qt] += 1
                            last = nmm[qt] == tot[qt]
                            S.op('pe', lambda e, qt=qt, kt=kt, rhs=rhs, first=first, last=last: e.matmul(acc[qt][:, 0:65], lhsT=PT[:, kt * 256 + qt * 128:kt * 256 + (qt + 1) * 128], rhs=rhs, start=first, stop=last),
                                 reads=[PT.key] + rk, writes=[acc[qt].key])

                prev = None
                for u in units:
                    ps = emit_S(u)
                    if prev is not None:
                        emit_PV(*prev)
                    prev = (u, ps)
                emit_PV(*prev)
                assert nmm[0] == tot[0] and nmm[1] == tot[1]
                for qt in range(2):
                    S.op('dve', lambda e, qt=qt: e.reciprocal(out=rden[:, qt:qt + 1], in_=acc[qt][:, 64:65]), reads=[acc[qt].key], writes=['rden'])
                    S.op('dve', lambda e, qt=qt: e.tensor_scalar(out=ya[:, 2 * i + qt, h * 64:(h + 1) * 64], in0=acc[qt][:, 0:64], scalar1=rden[:, qt:qt + 1], scalar2=None, op0=ALU.mult), reads=[acc[qt].key, 'rden'], writes=['ya'])
        for T in range(NTO):
            pt = pstr.next()
            for c in range(4):
                S.op('pe', lambda e, c=c, pt=pt, T=T: e.transpose(out=pt[:, c * 128:(c + 1) * 128], in_=ya[:, T, c * 128:(c + 1) * 128], identity=ident_b[:]), reads=['ya', 'ident_b'], writes=[pt.key])
            S.op('act', lambda e, pt=pt, T=T: e.activation(out=yaT[:, :, T * 128:(T + 1) * 128], in_=pt[:, 0:512].rearrange("p (c t) -> p c t", c=4), func=AF.Copy), reads=[pt.key], writes=['yaT'])
        dump("yaT", yaT[:].rearrange("p c t -> p (c t)"), [128, 4 * NOWN * 256], ['yaT'])
        S.barrier()

    chk_stop('O2')
    with ExitStack() as st:
        WG = sb(st, "WG", [128, 8, 2048], BF16)
        pm = sb(st, "pm", [128, 4, D], BF16); pa = sb(st, "pa", [128, 4, D], BF16); wo = sb(st, "wo", [128, 8, D], BF16)
        with ExitStack() as st2:
            prep_win(st2, OFF['ga'], 2048, WG, 0, 'fm', gi0=8)
            stg = Rot([sb(st2, "pstg%d" % i, [128, 4, 512]) for i in range(2)])
            for (src, dst, nk) in [(p_mlstm, pm, 4), (p_moba, pa, 4), (w_out, wo, 8)]:
                sv = src.rearrange("(k p) c -> p k c", p=128)
                for k0 in range(0, nk, 4):
                    for c0 in range(0, D, 512):
                        sg = stg.next()
                        S.dma('sp', sg[:], sv[:, k0:k0 + 4, c0:c0 + 512], writes=[sg.key])
                        S.op('act', lambda e, sg=sg, dst=dst, k0=k0, c0=c0: e.activation(out=dst[:, k0:k0 + 4, c0:c0 + 512], in_=sg[:], func=AF.Copy), reads=[sg.key], writes=[dst.key])
            S.barrier()
        xsr = Rot([sb(st, "xsb%d" % i, [128, 8, 256], BF16) for i in range(2)])
        xtr = Rot([sb(st, "xtb%d" % i, [128, 8, 260]) for i in range(2)])
        ymr = Rot([sb(st, "ymb%d" % i, [128, 4, 256], BF16) for i in range(2)])
        sgar = Rot([sb(st, "sga%d" % i, [128, 256]) for i in range(2)])
        sgbr = Rot([sb(st, "sgb%d" % i, [128, 256]) for i in range(2)])
        m1r = Rot([sb(st, "m1_%d" % i, [128, 256]) for i in range(2)])
        m2r = Rot([sb(st, "m2_%d" % i, [128, 256]) for i in range(2)])
        mgr = Rot([sb(st, "mg%d" % i, [128, 8, 256], BF16) for i in range(2)])
        xTo_v = xTo.rearrange("(k p) t -> p k t", p=128)
        for i in range(NOWN):
            xs = xsr.next(); xt = xtr.next(); mg = mgr.next(); ymb = ymr.next()
            S.dma('sp', xs[:], XSs[i].rearrange("p (k t) -> p k t", k=8), reads=[('XSs', i)], writes=[xs.key])
            S.dma('sp', ymb[:], YMs[i].rearrange("p (c t) -> p c t", c=4), reads=[('YMs', i)], writes=[ymb.key])
            S.dma('sp', xt[:], xTo_v[:, :, i * 260:(i + 1) * 260], writes=[xt.key])
            tk = slice(i * 256, (i + 1) * 256)
            for cg in range(8):
                sga = sgar.next(); sgb = sgbr.next(); m1 = m1r.next(); m2 = m2r.next()
                for (wc, gi, dst_) in [(cg * 128, 8 + cg, sga), (1024 + cg * 128, 16 + cg, sgb)]:
                    ps = psm.next()
                    for kt in range(KT):
                        S.op('pe', lambda e, kt=kt, ps=ps, wc=wc: e.matmul(ps[:, 0:256], lhsT=WG[:, kt, wc:wc + 128], rhs=xs[:, kt, :], start=(kt == 0), stop=(kt == KT - 1)), reads=['WG', xs.key], writes=[ps.key])
                    S.op('act', lambda e, ps=ps, gi=gi, dst_=dst_: e.activation(out=dst_[:], in_=ps[:, 0:256], func=AF.Sigmoid, bias=biasc[:, gi:gi + 1]), reads=[ps.key, 'biasc'], writes=[dst_.key])
                for (pw, rhs_fn, rkey, sg_, m_) in [(pm, lambda c: ymb[:, c, :], ymb.key, sga, m1), (pa, lambda c: yaT[:, c, tk], 'yaT', sgb, m2)]:
                    ps = psm.next()
                    for c in range(4):
                        S.op('pe', lambda e, c=c, ps=ps, pw=pw, rhs_fn=rhs_fn: e.matmul(ps[:, 0:256], lhsT=pw[:, c, cg * 128:(cg + 1) * 128], rhs=rhs_fn(c), start=(c == 0), stop=(c == 3)), reads=[pw.key, rkey], writes=[ps.key])
                    S.op('dve', lambda e, ps=ps, sg_=sg_, m_=m_: e.tensor_tensor(out=m_[:], in0=sg_[:], in1=ps[:, 0:256], op=ALU.mult), reads=[ps.key, sg_.key], writes=[m_.key])
                S.op('pool', lambda e, cg=cg, m1=m1, m2=m2, mg=mg: e.tensor_tensor(out=mg[:, cg, :], in0=m1[:], in1=m2[:], op=ALU.add), reads=[m1.key, m2.key], writes=[mg.key])
            for og in range(8):
                ps = psm.next()
                for cg in range(8):
                    S.op('pe', lambda e, cg=cg, ps=ps, og=og: e.matmul(ps[:, 0:256], lhsT=wo[:, cg, og * 128:(og + 1) * 128], rhs=mg[:, cg, :], start=(cg == 0), stop=(cg == 7)), reads=['wo', mg.key], writes=[ps.key])
                S.op('dve', lambda e, ps=ps, og=og: e.scalar_tensor_tensor(out=xacc[:, og, tk], in0=ps[:, 0:256], scalar=modc[:, C_G1 + og:C_G1 + og + 1], in1=xt[:, og, 4:260], op0=ALU.mult, op1=ALU.add), reads=[ps.key, 'modc_b', xt.key], writes=['xacc'])
        dump("xmid", xacc[:].rearrange("p k t -> p (k t)"), [128, 8 * NOWN * 256], ['xacc'])
        S.barrier()
    OY.close()

    chk_stop('O3')
    NTOK = NOWN * 256
    TG = 512 if NTOK % 512 == 0 else 256
    with ExitStack() as st:
        h2T = sb(st, "h2T", [128, 8, NTOK], BF16)
        sqr = Rot([sb(st, "sq%d" % i, [128, 8, 260], BF16) for i in range(1)])
        lnr = Rot([sb(st, "ln%d" % i, [128, 260]) for i in range(2)])
        rsr = Rot([sb(st, "rs%d" % i, [128, 260]) for i in range(2)])
        for i in range(NOWN):
            tk = slice(i * 256, (i + 1) * 256)
            sq = sqr.next(); lnv = lnr.next(); rs = rsr.next()
            S.op('act', lambda e, sq=sq, tk=tk: e.activation(out=sq[:, :, 0:256], in_=xacc[:, :, tk], func=AF.Square), reads=['xacc'], writes=[sq.key])
            ps = psm.next()
            for kt in range(KT):
                S.op('pe', lambda e, kt=kt, ps=ps, sq=sq: e.matmul(ps[:, 0:256], lhsT=ones_b[:], rhs=sq[:, kt, 0:256], start=(kt == 0), stop=(kt == KT - 1)), reads=['ones_b', sq.key], writes=[ps.key])
            S.op('act', lambda e, ps=ps, lnv=lnv: e.activation(out=lnv[:, 0:256], in_=ps[:, 0:256], func=AF.Ln, scale=1.0 / D, bias=eps_t[:]), reads=[ps.key, 'eps_t'], writes=[lnv.key])
            S.op('act', lambda e, lnv=lnv, rs=rs: e.activation(out=rs[:, 0:256], in_=lnv[:, 0:256], func=AF.Exp, scale=-0.5), reads=[lnv.key], writes=[rs.key])
            S.op('dve', lambda e, rs=rs, tk=tk: e.tensor_tensor(out=h2T[:, :, tk], in0=xacc[:, :, tk], in1=rs[:, 0:256].unsqueeze(1).to_broadcast([128, 8, 256]), op=ALU.mult), reads=['xacc', rs.key], writes=['h2T'])
        FG = 2
        gstg = Rot([sb(st, "gstg%d" % i, [128, 8, FG * 128]) for i in range(2)])
        ustg = Rot([sb(st, "ustg%d" % i, [128, 8, FG * 128]) for i in range(2)])
        dstg = Rot([sb(st, "dstg%d" % i, [128, FG, D]) for i in range(2)])
        gwr = Rot([sb(st, "gw%d" % i, [128, 8, FG * 128], BF16) for i in range(2)])
        uwr = Rot([sb(st, "uw%d" % i, [128, 8, FG * 128], BF16) for i in range(2)])
        dwr = Rot([sb(st, "dw%d" % i, [128, FG, D], BF16) for i in range(2)])
        bgur = Rot([sb(st, "bgu%d" % i, [128, 2 * FG]) for i in range(2)])
        sgr = Rot([sb(st, "sgl%d" % i, [128, TG]) for i in range(2)])
        fTr = Rot([sb(st, "fT%d" % i, [128, FG, TG], BF16) for i in range(2)])
        wg_v = w_gate.rearrange("(k p) c -> p k c", p=128)
        wu_v = w_up.rearrange("(k p) c -> p k c", p=128)
        wd_v = w_down.rearrange("(f p) c -> p f c", p=128)
        for fp in range(NFC // FG):
            gs = gstg.next(); us = ustg.next(); ds = dstg.next(); gw = gwr.next(); uw = uwr.next(); dw = dwr.next(); bgu = bgur.next()
            c0 = fp * FG * 128
            S.dma('sp', gs[:], wg_v[:, :, c0:c0 + FG * 128], writes=[gs.key])
            S.dma('sp', us[:], wu_v[:, :, c0:c0 + FG * 128], writes=[us.key])
            S.dma('sp', ds[:], wd_v[:, fp * FG:(fp + 1) * FG, :], writes=[ds.key])
            for kt in range(KT):
                S.op('act', lambda e, kt=kt, gs=gs, gw=gw: e.activation(out=gw[:, kt, :], in_=gs[:, kt, :], func=AF.Copy, scale=gmod2[:, kt:kt + 1]), reads=[gs.key, 'gmod2'], writes=[gw.key])
                S.op('act', lambda e, kt=kt, us=us, uw=uw: e.activation(out=uw[:, kt, :], in_=us[:, kt, :], func=AF.Copy, scale=gmod2[:, kt:kt + 1]), reads=[us.key, 'gmod2'], writes=[uw.key])
            S.op('pool', lambda e, ds=ds, dw=dw: e.tensor_copy(out=dw[:], in_=ds[:]), reads=[ds.key], writes=[dw.key])
            bp = pss[1]
            for which, sg in enumerate([gs, us]):
                for f in range(FG):
                    col = 64 + which * FG + f
                    for kt in range(KT):
                        S.op('pe', lambda e, kt=kt, sg=sg, f=f, col=col: e.matmul(bp[:, col:col + 1], lhsT=sg[:, kt, f * 128:(f + 1) * 128], rhs=modc[:, C_SH2 + kt:C_SH2 + kt + 1], start=(kt == 0), stop=(kt == KT - 1)), reads=[sg.key, 'modc_b'], writes=[bp.key])
            S.op('dve', lambda e, bgu=bgu: e.tensor_copy(out=bgu[:], in_=bp[:, 64:64 + 2 * FG]), reads=[bp.key], writes=[bgu.key])
            for tg in range(NTOK // TG):
                tk = slice(tg * TG, (tg + 1) * TG)
                fT = fTr.next()
                for f in range(FG):
                    pg = psm.next(); pu = psm.next(); sgl = sgr.next()
                    for kt in range(KT):
                        S.op('pe', lambda e, kt=kt, pg=pg, f=f: e.matmul(pg[:, 0:TG], lhsT=gw[:, kt, f * 128:(f + 1) * 128], rhs=h2T[:, kt, tk], start=(kt == 0), stop=(kt == KT - 1)), reads=[gw.key, 'h2T'], writes=[pg.key])
                    for kt in range(KT):
                        S.op('pe', lambda e, kt=kt, pu=pu, f=f: e.matmul(pu[:, 0:TG], lhsT=uw[:, kt, f * 128:(f + 1) * 128], rhs=h2T[:, kt, tk], start=(kt == 0), stop=(kt == KT - 1)), reads=[uw.key, 'h2T'], writes=[pu.key])
                    S.op('act', lambda e, pg=pg, f=f, sgl=sgl: e.activation(out=sgl[:], in_=pg[:, 0:TG], func=AF.Silu, bias=bgu[:, f:f + 1]), reads=[pg.key, bgu.key], writes=[sgl.key])
                    S.op('dve', lambda e, pu=pu, f=f, sgl=sgl, fT=fT: e.scalar_tensor_tensor(out=fT[:, f, :], in0=pu[:, 0:TG], scalar=bgu[:, FG + f:FG + f + 1], in1=sgl[:], op0=ALU.add, op1=ALU.mult), reads=[pu.key, bgu.key, sgl.key], writes=[fT.key])
                for og in range(8):
                    ps = psm.next()
                    for f in range(FG):
                        S.op('pe', lambda e, f=f, ps=ps, og=og: e.matmul(ps[:, 0:TG], lhsT=dw[:, f, og * 128:(og + 1) * 128], rhs=fT[:, f, :], start=(f == 0), stop=(f == FG - 1)), reads=[dw.key, fT.key], writes=[ps.key])
                    S.op('dve', lambda e, ps=ps, og=og: e.scalar_tensor_tensor(out=xacc[:, og, tk], in0=ps[:, 0:TG], scalar=modc[:, C_G2 + og:C_G2 + og + 1], in1=xacc[:, og, tk], op0=ALU.mult, op1=ALU.add), reads=[ps.key, 'modc_b'], writes=['xacc'])
        otr = Rot([sb(st, "ot%d" % i, [128, 8, 256]) for i in range(1)])
        outT_v = outT.rearrange("(k p) t -> p k t", p=128)
        out_toks = []
        for i in range(NOWN):
            tk = slice(i * 256, (i + 1) * 256)
            sq = sqr.next(); lnv = lnr.next(); rs = rsr.next(); ot = otr.next()
            S.op('act', lambda e, sq=sq, tk=tk: e.activation(out=sq[:, :, 0:256], in_=xacc[:, :, tk], func=AF.Square), reads=['xacc'], writes=[sq.key])
            ps = psm.next()
            for kt in range(KT):
                S.op('pe', lambda e, kt=kt, ps=ps, sq=sq: e.matmul(ps[:, 0:256], lhsT=ones_b[:], rhs=sq[:, kt, 0:256], start=(kt == 0), stop=(kt == KT - 1)), reads=['ones_b', sq.key], writes=[ps.key])
            S.op('act', lambda e, ps=ps, lnv=lnv: e.activation(out=lnv[:, 0:256], in_=ps[:, 0:256], func=AF.Ln, scale=1.0 / D, bias=eps_t[:]), reads=[ps.key, 'eps_t'], writes=[lnv.key])
            S.op('act', lambda e, lnv=lnv, rs=rs: e.activation(out=rs[:, 0:256], in_=lnv[:, 0:256], func=AF.Exp, scale=-0.5), reads=[lnv.key], writes=[rs.key])
            for kt in range(KT):
                S.op('dve', lambda e, kt=kt, rs=rs, ot=ot, tk=tk: e.scalar_tensor_tensor(out=ot[:, kt, :], in0=xacc[:, kt, tk], scalar=nfg[:, kt:kt + 1], in1=rs[:, 0:256], op0=ALU.mult, op1=ALU.mult), reads=['xacc', 'nfg_t', rs.key], writes=[ot.key])
            out_toks.append(S.dma('sp', outT_v[:, :, tk], ot[:], reads=[ot.key], writes=[('outT', i)]))
        S.barrier()
    XA.close()
    G.close()
    return nc, dbg_outs, S


_CONST_CACHE = {}


def _prep_inputs(S_LEN, inp):
    NBLK = S_LEN // 256
    NOWN = NBLK // 4
    NT = S_LEN // 128
    f32 = np.float32
    x = np.asarray(inp['x'], f32); c = np.asarray(inp['c'], f32); pos = np.asarray(inp['positions'], np.int32)
    perm = np.concatenate([np.arange(512, 1024), np.arange(1024, 1536), np.arange(2568, 3080), np.arange(3080, 3592),
                           np.arange(2048, 2056), np.arange(0, 512), np.arange(1536, 2048), np.arange(2056, 2568),
                           np.arange(3592, 4616), np.arange(4616, 5640)])
    w_in_p = np.ascontiguousarray(np.asarray(inp['w_in'], f32)[0][:, perm])
    b_in_p = np.asarray(inp['b_in'], f32)[0][perm]
    fm_offs = [OFF['mk'] + 128 * h for h in range(4)] + [OFF['mq'] + 128 * h for h in range(4)] + [OFF['ga'] + 128 * g for g in range(8)] + [OFF['gb'] + 128 * g for g in range(8)]
    b_in_col = np.stack([b_in_p[o:o + 128] for o in fm_offs], axis=1).astype(f32)

    def colT(v):
        return np.ascontiguousarray(np.asarray(v, f32).reshape(8, 128).T)

    conv_w = np.asarray(inp['conv_w'], f32)[0]
    cw = np.zeros((128, 32), f32)
    for g in range(8):
        for j in range(4):
            cw[:, g * 4 + j] = conv_w[j, g * 128:(g + 1) * 128]
    cbv = np.ascontiguousarray(np.asarray(inp['conv_b'], f32)[0].reshape(8, 128).T)
    half = 32
    inv_freq = (10000.0 ** (-np.arange(half, dtype=np.float64) / half)) / (2 * np.pi)
    common = dict(
        ada_w=np.ascontiguousarray(np.asarray(inp['ada_w'], f32)[0]),
        ada_b_c=np.ascontiguousarray(np.asarray(inp['ada_b'], f32)[0].reshape(48, 128).T),
        n1g=colT(inp['norm1_g'][0]), n2g=colT(inp['norm2_g'][0]), nfg=colT(inp['normf_g']),
        w_in_p=w_in_p, b_in_row=np.ascontiguousarray(b_in_p[None, :]), b_in_col=np.ascontiguousarray(b_in_col),
        cw=cw, cb=cbv, mng_bc=np.ascontiguousarray(np.broadcast_to(np.asarray(inp['m_norm_g'], f32)[0][None, :], (128, 512))),
        p_mlstm=np.ascontiguousarray(np.asarray(inp['p_mlstm'], f32)[0]), p_moba=np.ascontiguousarray(np.asarray(inp['p_moba'], f32)[0]),
        w_out=np.ascontiguousarray(np.asarray(inp['w_out'], f32)[0]), w_gate=np.ascontiguousarray(np.asarray(inp['w_gate'], f32)[0]),
        w_up=np.ascontiguousarray(np.asarray(inp['w_up'], f32)[0]), w_down=np.ascontiguousarray(np.asarray(inp['w_down'], f32)[0]),
        ident=np.eye(128, dtype=f32), tri=np.triu(np.ones((128, 128), f32)), ones=np.ones((128, 128), f32),
        invf=np.ascontiguousarray(np.broadcast_to(inv_freq.astype(f32)[None, :], (128, 32))),
    )
    etab = np.zeros((128, S_LEN), f32)
    for n in range(NBLK):
        etab[64 + n, n * 256:(n + 1) * 256] = 1.0
    etab[96, :] = 1.0
    in_maps = []
    for core in range(8):
        b, j = core // 4, core % 4
        m = dict(common)
        m['etab'] = etab
        m['xTf'] = np.ascontiguousarray(x[b].T)
        xo = np.zeros((D, NOWN, 260), f32)
        po = np.zeros((128, NOWN * 2), np.int32)
        pastbias = np.zeros((128, NOWN, 32), f32)
        hvv = np.ones((128, NOWN), f32)
        for i in range(NOWN):
            g = 4 * i + j
            s0 = g * 256
            xo[:, i, 4:260] = x[b, s0:s0 + 256].T
            if s0 > 0:
                xo[:, i, 0:4] = x[b, s0 - 4:s0].T
            else:
                hvv[:, i] = 0.0
            for t in range(2):
                po[:, i * 2 + t] = pos[b, s0 + t * 128:s0 + (t + 1) * 128]
            pastbias[:, i, g:] = -1e30
        m['xTo'] = np.ascontiguousarray(xo.reshape(D, NOWN * 260))
        m['cT'] = colT(c[b])
        m['posf'] = np.ascontiguousarray(pos[b].reshape(NT, 128).T)
        m['poso'] = po
        m['pastbias'] = np.ascontiguousarray(pastbias.reshape(128, NOWN * 32))
        m['hv'] = hvv
        cs = np.zeros((128, 4), f32); cs[:, j] = 1.0
        m['capsel'] = cs
        in_maps.append(m)
    return in_maps


def run(inp, S_LEN, dbg=None, stop=None):
    NBLK = S_LEN // 256
    NOWN = NBLK // 4
    nc, dbg_outs, S = build_program(S_LEN, dbg, stop)
    in_maps = _prep_inputs(S_LEN, inp)
    res = run_bass_kernel_spmd(nc, in_maps, core_ids=list(range(8)))
    B = 2
    out = np.zeros((B, S_LEN, D), np.float32)
    for core in range(8):
        b, j = core // 4, core % 4
        oT = np.asarray(res.results[core]["outT"])
        for i in range(NOWN):
            g = 4 * i + j
            out[b, g * 256:(g + 1) * 256, :] = oT[:, i * 256:(i + 1) * 256].T
    dbgres = {}
    for name in dbg_outs:
        dbgres[name] = [np.asarray(res.results[core]["dbg_" + name]) for core in range(8)]
    return out, dbgres


def kernel(**inputs):
    S_LEN = int(np.asarray(inputs['x']).shape[1])
    out, _ = run(inputs, S_LEN)
    return out
```
